# Optimizing a Trainium2 kernel written in Bass

```python
import math
import jax, jax.numpy as jnp
from jax import lax
import numpy as np

D_MODEL = 1024
BATCH = 32
SEQ = 256
DEPTH = 2
DEC_BATCH = 4
DEC_SEQ = 4096
PAST_LEN = 256

GRID_W = 64
DN_HEADS = 4
DN_DK = 128
DN_DV = 128
DN_CONV_W = 3
DN_CHUNK = 64
CF_WIDTH = 512
CF_CONV_W = 31
SC_WIDTH = 512
SC_CONV_W = 3
D_FF = 2816
FFN_CONV_W = 3
N_BRANCH = 3
EPS = 1e-6

DN_QK = DN_HEADS * DN_DK
DN_V = DN_HEADS * DN_DV
SPLITS = (DN_QK, DN_QK, DN_V, DN_V, 2 * DN_HEADS, 2 * DN_HEADS, 2 * CF_WIDTH, 3 * SC_WIDTH, N_BRANCH * D_MODEL)
IN_COLS = sum(SPLITS)

kernel_name = "hybrid_bidir_deltanet_conformer_shortconv_diffusion_step"


def split_cols(x, sizes):
    idx = []
    acc = 0
    for s in sizes[:-1]:
        acc += s
        idx.append(acc)
    return jnp.split(x, idx, axis=-1)


def rmsnorm(x, g):
    x32 = x.astype(jnp.float32)
    y = x32 * lax.rsqrt(jnp.mean(x32 * x32, axis=-1, keepdims=True) + EPS)
    return (y * g.astype(jnp.float32)).astype(x.dtype)


def layernorm(x, g, b):
    x32 = x.astype(jnp.float32)
    mu = jnp.mean(x32, axis=-1, keepdims=True)
    xc = x32 - mu
    y = xc * lax.rsqrt(jnp.mean(xc * xc, axis=-1, keepdims=True) + EPS)
    return (y * g.astype(jnp.float32) + b.astype(jnp.float32)).astype(x.dtype)


def l2norm(x):
    return x * lax.rsqrt(jnp.sum(x * x, axis=-1, keepdims=True) + EPS)


def dwconv_seq(x, w):
    K, C = w.shape
    return lax.conv_general_dilated(x, w[:, None, :].astype(x.dtype), (1,), [(K // 2, K // 2)],
                                    dimension_numbers=("NWC", "WIO", "NWC"), feature_group_count=C)


def dwconv(x, w, axis, on_grid):
    if not on_grid:
        return dwconv_seq(x, w)
    B, L, C = x.shape
    rows = L // GRID_W
    if axis == "h":
        return dwconv_seq(x.reshape(B * rows, GRID_W, C), w).reshape(B, L, C)
    xg = x.reshape(B, rows, GRID_W, C).transpose(0, 2, 1, 3).reshape(B * GRID_W, rows, C)
    y = dwconv_seq(xg, w)
    return y.reshape(B, GRID_W, rows, C).transpose(0, 2, 1, 3).reshape(B, L, C)


def chunk_gated_delta(q, k, v, g, beta, s0):
    B, L, H, DK = q.shape
    DV = v.shape[-1]
    C = DN_CHUNK
    N = L // C

    def blk(t):
        t = t.reshape((B, N, C, H) + t.shape[3:])
        return jnp.moveaxis(t, 3, 1)

    q, k, v, g, beta = blk(q), blk(k), blk(v), blk(g), blk(beta)
    gam = jnp.cumsum(g, axis=3)
    incl = jnp.tril(jnp.ones((C, C), dtype=bool))
    strict = jnp.tril(jnp.ones((C, C), dtype=bool), -1)
    diff = gam[..., :, None] - gam[..., None, :]
    decay = jnp.where(incl, jnp.exp(jnp.where(incl, diff, 0.0)), 0.0)
    a = jnp.where(strict, beta[..., :, None] * jnp.einsum('bhnid,bhnjd->bhnij', k, k) * decay, 0.0)
    rhs = jnp.concatenate([beta[..., None] * v, (beta * jnp.exp(gam))[..., None] * k], axis=-1)
    sol = lax.linalg.triangular_solve(a, rhs, left_side=True, lower=True, unit_diagonal=True)
    u, w = sol[..., :DV], sol[..., DV:]
    qk = jnp.einsum('bhnid,bhnjd->bhnij', q, k) * decay
    q_dec = q * jnp.exp(gam)[..., None]
    k_dec = k * jnp.exp(gam[..., -1:] - gam)[..., None]
    g_tot = jnp.exp(gam[..., -1])

    def step(s, xs):
        u_n, w_n, qk_n, qd_n, kd_n, gt_n = xs
        v_new = u_n - jnp.einsum('bhck,bhkv->bhcv', w_n, s)
        o = jnp.einsum('bhck,bhkv->bhcv', qd_n, s) + jnp.einsum('bhij,bhjv->bhiv', qk_n, v_new)
        s = s * gt_n[..., None, None] + jnp.einsum('bhck,bhcv->bhkv', kd_n, v_new)
        return s, o

    xs = tuple(jnp.moveaxis(t, 2, 0) for t in (u, w, qk, q_dec, k_dec, g_tot))
    s_fin, o = lax.scan(step, s0, xs)
    o = jnp.transpose(o, (1, 0, 3, 2, 4)).reshape(B, L, H, DV)
    return o, s_fin


def trunk_layer(x, mod, p, s0f, s0b, on_grid):
    B, L, D = x.shape
    f32 = jnp.float32
    sh1, sc1, gt1, sh2, sc2, gt2 = jnp.split(mod, 6, axis=-1)

    h = rmsnorm(x, p["g_pre_mix"]) * (1 + sc1) + sh1
    proj = h @ p["w_in"]
    q, k, v, z, a_dir, b_dir, cf_in, sc_in, gate_pre = split_cols(proj, SPLITS)

    qkv = jax.nn.silu(dwconv(jnp.concatenate([q, k, v], axis=-1), p["dn_conv"], "h", on_grid))
    q, k, v = split_cols(qkv, (DN_QK, DN_QK, DN_V))
    q = l2norm(q.reshape(B, L, DN_HEADS, DN_DK).astype(f32)) * (DN_DK ** -0.5)
    k = l2norm(k.reshape(B, L, DN_HEADS, DN_DK).astype(f32))
    v = v.reshape(B, L, DN_HEADS, DN_DV).astype(f32)
    a_dir = a_dir.astype(f32).reshape(B, L, 2, DN_HEADS)
    beta = jax.nn.sigmoid(b_dir.astype(f32).reshape(B, L, 2, DN_HEADS))
    g = -jnp.exp(p["dn_a_log"].astype(f32)) * jax.nn.softplus(a_dir + p["dn_dt_bias"].astype(f32))
    o_f, s_f = chunk_gated_delta(q, k, v, g[:, :, 0], beta[:, :, 0], s0f)
    o_b, s_b = chunk_gated_delta(jnp.flip(q, 1), jnp.flip(k, 1), jnp.flip(v, 1),
                                 jnp.flip(g[:, :, 1], 1), jnp.flip(beta[:, :, 1], 1), s0b)
    o = rmsnorm(o_f + jnp.flip(o_b, 1), p["dn_norm_g"]).astype(x.dtype)
    o = o * jax.nn.silu(z.reshape(B, L, DN_HEADS, DN_DV))
    y_dn = o.reshape(B, L, DN_V) @ p["w_dn_out"]

    ga, gb = jnp.split(cf_in, 2, axis=-1)
    hc = dwconv(ga * jax.nn.sigmoid(gb), p["cf_conv"], "h", on_grid)
    hc = jax.nn.silu(layernorm(hc, p["cf_ln_g"], p["cf_ln_b"]))
    y_cf = hc @ p["w_cf_out"]

    bg, cg, xh = jnp.split(sc_in, 3, axis=-1)
    y_sc = (bg * dwconv(cg * xh, p["sc_conv"], "v", on_grid)) @ p["w_sc_out"]

    g_a, g_b, g_c = jnp.split(jax.nn.sigmoid(gate_pre), 3, axis=-1)
    m = (g_a * y_dn + g_b * y_cf + g_c * y_sc) @ p["w_o"]
    x = x + gt1 * rmsnorm(m, p["g_post_mix"])

    h = rmsnorm(x, p["g_pre_ffn"]) * (1 + sc2) + sh2
    u = dwconv(h @ p["w_ffn_up"], p["ffn_conv"], "v", on_grid)
    ua, ub = jnp.split(u, 2, axis=-1)
    y = (jax.nn.silu(ua) * ub) @ p["w_ffn_down"]
    x = x + gt2 * rmsnorm(y, p["g_post_ffn"])
    return x, s_f, s_b


def setup_inputs(seed: int = 0) -> dict:
    key = jax.random.key(seed)
    ks = jax.random.split(key, 32)
    f32 = jnp.float32
    D = D_MODEL

    def nrm(k, shape, scale):
        return jax.random.normal(k, shape, f32) * scale

    def gain(k, shape):
        return 1.0 + 0.05 * jax.random.normal(k, shape, f32)

    dt = jnp.exp(jax.random.uniform(ks[13], (DEPTH, 2, DN_HEADS), f32, math.log(1e-3), math.log(1e-1)))
    dt_bias = dt + jnp.log(-jnp.expm1(-dt))
    a_log = jnp.log(jax.random.uniform(ks[12], (DEPTH, 2, DN_HEADS), f32, 1.0, 16.0))
    return {
        "x_prompt": nrm(ks[0], (BATCH, SEQ, D), 1.0),
        "x_sample": nrm(ks[1], (DEC_BATCH, DEC_SEQ, D), 1.0),
        "state_dn": nrm(ks[2], (DEC_BATCH, DEPTH, 2, DN_HEADS, DN_DK, DN_DV), DN_DK ** -0.5),
        "c": nrm(ks[3], (DEC_BATCH, D), 1.0),
        "c_ctx": nrm(ks[4], (D,), 1.0),
        "w_mod": nrm(ks[5], (DEPTH, D, 6 * D), 0.5 * D ** -0.5),
        "b_mod": nrm(ks[6], (DEPTH, 6 * D), 0.02),
        "g_pre_mix": gain(ks[7], (DEPTH, D)),
        "g_post_mix": gain(ks[8], (DEPTH, D)),
        "g_pre_ffn": gain(ks[9], (DEPTH, D)),
        "g_post_ffn": gain(ks[10], (DEPTH, D)),
        "w_in": nrm(ks[11], (DEPTH, D, IN_COLS), D ** -0.5),
        "dn_conv": nrm(ks[14], (DEPTH, DN_CONV_W, 2 * DN_QK + DN_V), DN_CONV_W ** -0.5),
        "dn_a_log": a_log,
        "dn_dt_bias": dt_bias,
        "dn_norm_g": gain(ks[15], (DEPTH, DN_DV)),
        "w_dn_out": nrm(ks[16], (DEPTH, DN_V, D), DN_V ** -0.5),
        "cf_conv": nrm(ks[17], (DEPTH, CF_CONV_W, CF_WIDTH), CF_CONV_W ** -0.5),
        "cf_ln_g": gain(ks[18], (DEPTH, CF_WIDTH)),
        "cf_ln_b": nrm(ks[19], (DEPTH, CF_WIDTH), 0.02),
        "w_cf_out": nrm(ks[20], (DEPTH, CF_WIDTH, D), CF_WIDTH ** -0.5),
        "sc_conv": nrm(ks[21], (DEPTH, SC_CONV_W, SC_WIDTH), SC_CONV_W ** -0.5),
        "w_sc_out": nrm(ks[22], (DEPTH, SC_WIDTH, D), SC_WIDTH ** -0.5),
        "w_o": nrm(ks[23], (DEPTH, D, D), D ** -0.5),
        "w_ffn_up": nrm(ks[24], (DEPTH, D, 2 * D_FF), D ** -0.5),
        "ffn_conv": nrm(ks[25], (DEPTH, FFN_CONV_W, 2 * D_FF), FFN_CONV_W ** -0.5),
        "w_ffn_down": nrm(ks[26], (DEPTH, D_FF, D), D_FF ** -0.5),
    }


def reference(x_prompt, x_sample, state_dn, c, c_ctx, w_mod, b_mod, g_pre_mix, g_post_mix, g_pre_ffn, g_post_ffn,
              w_in, dn_conv, dn_a_log, dn_dt_bias, dn_norm_g, w_dn_out, cf_conv, cf_ln_g, cf_ln_b, w_cf_out,
              sc_conv, w_sc_out, w_o, w_ffn_up, ffn_conv, w_ffn_down):
    xp = x_prompt
    xs = x_sample
    zeros = jnp.zeros((x_prompt.shape[0], DN_HEADS, DN_DK, DN_DV), jnp.float32)
    ctx_states = []
    for l in range(DEPTH):
        p = {
            "g_pre_mix": g_pre_mix[l], "g_post_mix": g_post_mix[l],
            "g_pre_ffn": g_pre_ffn[l], "g_post_ffn": g_post_ffn[l],
            "w_in": w_in[l], "dn_conv": dn_conv[l], "dn_a_log": dn_a_log[l], "dn_dt_bias": dn_dt_bias[l],
            "dn_norm_g": dn_norm_g[l], "w_dn_out": w_dn_out[l], "cf_conv": cf_conv[l],
            "cf_ln_g": cf_ln_g[l], "cf_ln_b": cf_ln_b[l], "w_cf_out": w_cf_out[l],
            "sc_conv": sc_conv[l], "w_sc_out": w_sc_out[l], "w_o": w_o[l],
            "w_ffn_up": w_ffn_up[l], "ffn_conv": ffn_conv[l], "w_ffn_down": w_ffn_down[l],
        }
        mod_ctx = (jax.nn.silu(c_ctx) @ w_mod[l] + b_mod[l])[None, None, :]
        mod_lat = (jax.nn.silu(c) @ w_mod[l] + b_mod[l])[:, None, :]
        xp, s_f, s_b = trunk_layer(xp, mod_ctx, p, zeros, zeros, False)
        ctx_states.append(jnp.stack([s_f, s_b], axis=1))
        st = state_dn[:, l].astype(jnp.float32)
        xs, _, _ = trunk_layer(xs, mod_lat, p, st[:, 0], st[:, 1], True)
    new_state_dn = jnp.stack(ctx_states, axis=1).astype(x_prompt.dtype)
    return (xp, xs, new_state_dn)
```

```python
import contextlib
import numpy as np
import concourse.bass as bass
import concourse.mybir as mybir
from concourse.bass_utils import run_bass_kernel_spmd

F32 = mybir.dt.float32
BF16 = mybir.dt.bfloat16
U8 = mybir.dt.uint8
AF = mybir.ActivationFunctionType
ALU = mybir.AluOpType

ENGS = ("pe", "act", "dve", "pool", "sp")
NS_DMA = 12
EPS = 1e-6
NT = 512
D = 1024
NEG = -30000.0


class KB:
    def __init__(self, nc, same_engine_sync=True):
        self.nc = nc
        self.prog = {e: [] for e in ENGS}
        self.cnt = {e: 0 for e in ENGS}
        self.seen = {e: {e2: -1 for e2 in ENGS} for e in ENGS}
        self.seen_dma = {e: set() for e in ENGS}
        self.lastw = {}
        self.readers = {}
        self.dmas = []
        self.dma_cnt = {e: 0 for e in ENGS}
        self.last_slot = {}
        self.dma_mark = {e: 0 for e in ENGS}
        self.pending = {}
        self.needed = {e: set() for e in ENGS}
        self.same_engine_sync = same_engine_sync
        self.n_inst = 0

    def snapshot(self, key):
        out = []
        d = self.lastw.get(key)
        if d is not None:
            out.append(d)
        out.extend(self.readers.get(key, {}).values())
        out.extend(self.pending.get(key, []))
        return out

    def _collect(self, r, w):
        deps = []
        if self.pending:
            for k in list(r) + list(w):
                p = self.pending.pop(k, None)
                if p:
                    deps.extend((d, True) for d in p)
        for k in r:
            d = self.lastw.get(k)
            if d is not None:
                deps.append((d, True))
        for k in w:
            d = self.lastw.get(k)
            if d is not None:
                deps.append((d, True))
            for d in self.readers.get(k, {}).values():
                deps.append((d, False))
        return deps

    def _emit_waits(self, eng, deps):
        for d, is_raw in deps:
            if d[0] == "c":
                _, e2, idx = d
                if e2 == eng:
                    if eng == "pe" or not self.same_engine_sync:
                        continue
                if idx <= self.seen[eng][e2]:
                    continue
                self.seen[eng][e2] = idx
                self.needed[e2].add(idx)
                self.prog[eng].append(("wc", e2, idx))
            else:
                did = d[1]
                if did < self.dma_mark[eng] or did in self.seen_dma[eng]:
                    continue
                self.seen_dma[eng].add(did)
                self.prog[eng].append(("wd", did))

    def _update(self, dep, rk, r, w):
        for k in r:
            self.readers.setdefault(k, {})[rk] = dep
        for k in w:
            self.lastw[k] = dep
            self.readers[k] = {}

    def op(self, eng, fn, r=(), w=()):
        deps = self._collect(r, w)
        self._emit_waits(eng, deps)
        idx = self.cnt[eng]
        self.cnt[eng] += 1
        self.prog[eng].append(("i", fn, idx))
        self._update(("c", eng, idx), eng, r, w)
        self.n_inst += 1
        return idx

    def dma(self, q, out_ap, in_ap, r=(), w=()):
        deps = self._collect(r, w)
        n = self.dma_cnt[q]
        self.dma_cnt[q] += 1
        slot = n % NS_DMA
        value = 16 * (n // NS_DMA + 1)
        did = len(self.dmas)
        prev = self.last_slot.get((q, slot))
        if prev is not None:
            deps.append((("d", prev), True))
        self.last_slot[(q, slot)] = did
        self._emit_waits(q, deps)
        self.dmas.append((q, slot, value))
        self.prog[q].append(("dma", out_ap, in_ap, did))
        self._update(("d", did), ("d", did), r, w)
        self.n_inst += 1
        return did

    def barrier(self):
        deps = []
        for e2 in ENGS:
            if e2 != "sp" and self.cnt[e2] > 0:
                deps.append((("c", e2, self.cnt[e2] - 1), True))
        for did in range(self.dma_mark["sp"], len(self.dmas)):
            deps.append((("d", did), True))
        self._emit_waits("sp", deps)
        idx = self.op("sp", lambda e: e.nop())
        nd = len(self.dmas)
        for e in ENGS:
            if e != "sp":
                self._emit_waits(e, [(("c", "sp", idx), True)])
            for e2 in ENGS:
                self.seen[e][e2] = max(self.seen[e][e2], self.cnt[e2] - 1)
            self.dma_mark[e] = nd
            self.seen_dma[e] = set()

    def wait_all_dmas(self, eng):
        self._emit_waits(eng, [(("d", did), True) for did in range(self.dma_mark[eng], len(self.dmas))])

    def emit(self, st):
        nc = self.nc
        sems = {e: st.enter_context(nc.semaphore("s_" + e)) for e in ENGS}
        dsems = {}
        for q in ENGS:
            if self.dma_cnt[q] > 0:
                dsems[q] = [st.enter_context(nc.semaphore("d_%s_%d" % (q, i)))
                            for i in range(min(NS_DMA, self.dma_cnt[q]))]
        rank = {}
        for e in ENGS:
            for i, idx in enumerate(sorted(self.needed[e])):
                rank[(e, idx)] = i + 1
        block = st.enter_context(nc.Block())
        handles = {"pe": "tensor", "act": "scalar", "dve": "vector", "pool": "gpsimd", "sp": "sync"}
        dmas = self.dmas

        def mk(ename):
            entries = self.prog[ename]

            def body(h):
                for ent in entries:
                    t = ent[0]
                    if t == "wc":
                        h.wait_ge(sems[ent[1]], rank[(ent[1], ent[2])])
                    elif t == "wd":
                        q, slot, value = dmas[ent[1]]
                        h.wait_ge(dsems[q][slot], value)
                    elif t == "i":
                        ins = ent[1](h)
                        if (ename, ent[2]) in rank:
                            ins.then_inc(sems[ename], 1)
                    else:
                        q, slot, value = dmas[ent[3]]
                        h.dma_start(out=ent[1], in_=ent[2]).then_inc(dsems[q][slot], 16)
            return body

        for ename in ENGS:
            if self.prog[ename]:
                getattr(block, handles[ename])(mk(ename))


def _panel(W):
    K, M = W.shape
    return np.ascontiguousarray(W.reshape(K // 128, 128, M // 128, 128).transpose(2, 1, 0, 3))


def _fm(v, nt):
    return np.ascontiguousarray(v.reshape(nt, 128).T)


def _consts():
    c = {}
    c["c_ident"] = np.eye(128, dtype=np.float32)
    t = np.arange(128)
    c["c_tri"] = np.stack([(t[:, None] <= t[None, :]), (t[:, None] >= t[None, :])]).astype(np.float32)
    j = t[:, None]
    i = t[None, :]
    nm = np.stack([i >= j, i > j, i <= j, i < j])
    c["c_nmask"] = np.where(nm, 0.0, NEG).astype(np.float32)
    lm = np.zeros((7, 2, 128, 128), np.uint8)
    for l in range(7):
        s = 1 << l
        same = (t[:, None] // (2 * s)) == (t[None, :] // (2 * s))
        L = same & ((t[:, None] // s) % 2 == 1) & ((t[None, :] // s) % 2 == 0)
        lm[l, 0] = L
        lm[l, 1] = L.T
    c["c_lmask"] = lm
    return c


def _shared(inp):
    sh = {}
    f = lambda a: np.asarray(a, dtype=np.float32)
    w_mod = f(inp["w_mod"])
    sh["wmod"] = np.stack([_panel(w_mod[l]) for l in range(2)])
    sh["bmod"] = np.stack([_fm(f(inp["b_mod"])[l], 48) for l in range(2)])
    g = [f(inp[k]) for k in ("g_pre_mix", "g_post_mix", "g_pre_ffn", "g_post_ffn")]
    sh["gains"] = np.stack([np.stack([_fm(gg[l], 8) for gg in g], axis=1) for l in range(2)])
    w_in = f(inp["w_in"])
    cols = np.concatenate([np.arange(0, 2048), np.arange(2064, 4624)])
    gate0 = 4624
    gcols = np.concatenate([np.arange(gate0 + b * 1024 + j * 128, gate0 + b * 1024 + (j + 1) * 128)
                            for j in range(8) for b in range(3)])
    cols = np.concatenate([cols, gcols])
    sh["win"] = np.stack([_panel(w_in[l][:, cols]) for l in range(2)])
    sh["wgate"] = np.stack([np.ascontiguousarray(w_in[l][:, 2048:2064].reshape(8, 128, 16).transpose(1, 0, 2))
                            for l in range(2)])
    dn_conv = f(inp["dn_conv"])
    sh["dnconv"] = np.stack([np.ascontiguousarray(dn_conv[l].T.reshape(12, 128, 3).transpose(1, 0, 2)) for l in range(2)])
    sh["alog"] = np.ascontiguousarray(np.broadcast_to(f(inp["dn_a_log"]).reshape(2, 1, 8), (2, 128, 8)))
    sh["dtb"] = np.ascontiguousarray(np.broadcast_to(f(inp["dn_dt_bias"]).reshape(2, 1, 8), (2, 128, 8)))
    sh["dnng"] = np.ascontiguousarray(f(inp["dn_norm_g"]).reshape(2, 128, 1))
    wdn, wcf, wsc = f(inp["w_dn_out"]), f(inp["w_cf_out"]), f(inp["w_sc_out"])
    sh["wbr"] = np.stack([np.stack([_panel(wdn[l]), _panel(wcf[l]), _panel(wsc[l])], axis=2) for l in range(2)])
    cf_conv = f(inp["cf_conv"])
    sh["cfconv"] = np.stack([np.ascontiguousarray(cf_conv[l].T.reshape(4, 128, 31).transpose(1, 0, 2)) for l in range(2)])
    sh["cfln"] = np.stack([np.stack([_fm(f(inp["cf_ln_g"])[l], 4), _fm(f(inp["cf_ln_b"])[l], 4)], axis=2) for l in range(2)])
    sc_conv = f(inp["sc_conv"])
    sh["scconv"] = np.stack([np.ascontiguousarray(sc_conv[l].T.reshape(4, 128, 3).transpose(1, 0, 2)) for l in range(2)])
    sh["wo"] = np.stack([_panel(f(inp["w_o"])[l]) for l in range(2)])
    wup = f(inp["w_ffn_up"])
    pu = [_panel(wup[l]) for l in range(2)]
    sh["wup"] = np.stack([np.ascontiguousarray(np.stack([p[0:22], p[22:44]], axis=2)) for p in pu])
    ffn_conv = f(inp["ffn_conv"])
    sh["ffnconv"] = np.stack([np.ascontiguousarray(ffn_conv[l].T.reshape(44, 128, 3).transpose(1, 0, 2)) for l in range(2)])
    sh["wdown"] = np.stack([_panel(f(inp["w_ffn_down"])[l]) for l in range(2)])
    sh.update(_consts())
    return sh


SHARED_SHAPES = {
    "wmod": ([2, 48, 128, 8, 128], F32), "bmod": ([2, 128, 48], F32), "gains": ([2, 128, 4, 8], F32),
    "win": ([2, 60, 128, 8, 128], F32), "wgate": ([2, 128, 8, 16], F32), "dnconv": ([2, 128, 12, 3], F32),
    "alog": ([2, 128, 8], F32), "dtb": ([2, 128, 8], F32), "dnng": ([2, 128, 1], F32),
    "wbr": ([2, 8, 128, 3, 4, 128], F32), "cfconv": ([2, 128, 4, 31], F32), "cfln": ([2, 128, 4, 2], F32),
    "scconv": ([2, 128, 4, 3], F32), "wo": ([2, 8, 128, 8, 128], F32), "wup": ([2, 22, 128, 2, 8, 128], F32),
    "ffnconv": ([2, 128, 44, 3], F32), "wdown": ([2, 8, 128, 22, 128], F32),
    "c_ident": ([128, 128], F32), "c_tri": ([2, 128, 128], F32), "c_nmask": ([4, 128, 128], F32),
    "c_lmask": ([7, 2, 128, 128], U8),
}
TS = 4096
TP = 1024
CORE_SHAPES = {"xs": ([D, TS], F32), "xp": ([D, TP], F32), "st0": ([2, 2, 128, 4, 128], F32), "cvec": ([128, 8, 2], F32)}


class Prog:
    def __init__(self, cfg=None):
        self.cfg = cfg or {}
        self.nc = bass.Bass("TRN2", target_bir_lowering=False)
        self.st = contextlib.ExitStack()
        self.kb = KB(self.nc)
        self.free_banks = list(range(8))
        self.dg_i = 0
        self.regions = []
        self.marks = []
        self.NSLOT = self.cfg.get("nslot", 3)
        self.STAGGER = self.cfg.get("stagger", 5)
        self.dg_cache = {}
        self.dg_gen = {}
        self.NDG = 64
        self.wr_i = 0

    def sb(self, name, shape, dt):
        return self.st.enter_context(self.nc.sbuf_tensor(name, list(shape), dt))

    def balloc(self):
        assert self.free_banks, "out of PSUM banks"
        return self.free_banks.pop(0)

    def bfree(self, b):
        assert b not in self.free_banks
        self.free_banks.append(b)

    def claim(self, key, lo, hi):
        seeds = []
        keep = []
        for (k2, lo2, hi2) in self.regions:
            if lo2 < hi and lo < hi2:
                if k2 != key:
                    seeds.extend(self.kb.snapshot(k2))
                if not (lo <= lo2 and hi2 <= hi):
                    keep.append((k2, lo2, hi2))
            else:
                keep.append((k2, lo2, hi2))
        keep.append((key, lo, hi))
        self.regions = keep
        if seeds:
            self.kb.pending.setdefault(key, []).extend(seeds)

    def aview(self, name, shape, dt, key=None):
        n = 1
        for s in shape[1:]:
            n *= s
        nbytes = n * (4 if dt == F32 else (2 if dt == BF16 else 1))
        nbytes = (nbytes + 31) // 32 * 32
        off = self.aoff
        self.aoff += nbytes
        assert self.aoff <= self.ARENA, ("arena overflow", name, self.aoff)
        if key is not None:
            for k_ in (key if isinstance(key, list) else [key]):
                self.claim(k_, off, off + nbytes)
        a = self.arena[:, off // 4:(off + nbytes) // 4]
        if dt != F32:
            a = a.bitcast(dt)
        a = a[:, 0:n]
        if len(shape) == 3:
            a = a.rearrange("p (a b) -> p a b", a=shape[1])
        elif len(shape) == 4:
            a = a.rearrange("p (a b c) -> p a b c", a=shape[1], b=shape[2])
        elif len(shape) == 5:
            a = a.rearrange("p (a b c d) -> p a b c d", a=shape[1], b=shape[2], c=shape[3])
        return a

    def swp(self, gens):
        active = []
        gens = list(gens)
        k = 0
        while k < len(gens) or active:
            if k < len(gens):
                active.append(gens[k])
                k += 1
            for a in list(reversed(active)):
                try:
                    next(a)
                except StopIteration:
                    active.remove(a)

    def mark(self, name):
        self.marks.append((name, self.kb.cnt["pe"]))

    def phase(self):
        if self.cfg.get("barriers", False):
            self.kb.barrier()
        self.aoff = 0
        self.phase_id = getattr(self, "phase_id", 0) + 1

    def mm(self, out, lhsT, rhs, start, stop, r, w):
        self.kb.op("pe", lambda e: e.matmul(out, lhsT, rhs, start=start, stop=stop), r=r, w=w)

    def tr(self, out, in_, ident, r, w):
        self.kb.op("pe", lambda e: e.transpose(out=out, in_=in_, identity=ident), r=r, w=w)

    def act(self, out, in_, func, r, w, scale=None, bias=None):
        kw = {}
        if scale is not None:
            kw["scale"] = scale
        if bias is not None:
            kw["bias"] = bias
        self.kb.op("act", lambda e: e.activation(out=out, in_=in_, func=func, **kw), r=r, w=w)

    def tt(self, out, in0, in1, op, r, w, eng="dve"):
        self.kb.op(eng, lambda e: e.tensor_tensor(out=out, in0=in0, in1=in1, op=op), r=r, w=w)

    def stt(self, out, in0, scalar, in1, op0, op1, r, w):
        self.kb.op("dve", lambda e: e.scalar_tensor_tensor(out=out, in0=in0, scalar=scalar, in1=in1, op0=op0, op1=op1), r=r, w=w)

    def ts(self, out, in0, s1, op0, r, w, s2=None, op1=None, eng="dve"):
        if op1 is None:
            self.kb.op(eng, lambda e: e.tensor_scalar(out=out, in0=in0, scalar1=s1, scalar2=None, op0=op0), r=r, w=w)
        else:
            self.kb.op(eng, lambda e: e.tensor_scalar(out=out, in0=in0, scalar1=s1, scalar2=s2, op0=op0, op1=op1), r=r, w=w)

    def cp(self, out, in_, r, w, eng="dve"):
        self.kb.op(eng, lambda e: e.tensor_copy(out=out, in_=in_), r=r, w=w)

    def ms(self, ap, val, w, eng="pool"):
        self.kb.op(eng, lambda e: e.memset(ap, val), w=w)

    def cpred(self, out, mask, data, r, w):
        self.kb.op("dve", lambda e: e.copy_predicated(out=out, mask=mask, data=data), r=r, w=w)

    def diag(self, colv, rkey, ck=None):
        if ck is not None and ck in self.dg_cache:
            s, gen = self.dg_cache[ck]
            if self.dg_gen.get(s) == gen:
                return self.DG[:, s, :], ("dg", s)
        s = self.dg_i % self.NDG
        self.dg_i += 1
        self.dg_gen[s] = self.dg_i
        if ck is not None:
            self.dg_cache[ck] = (s, self.dg_i)
        out = self.DG[:, s, :]
        key = ("dg", s)
        self.ts(out, self.IDF[:], colv, ALU.mult, r=["consts", rkey], w=[key], eng="dve")
        return out, key

    def wload(self, src, n):
        s = self.wr_i % self.NWR
        self.wr_i += 1
        key = ("wr", s)
        dst = self.WR[:, s, 0:n]
        self.kb.dma("pool", dst, src, w=[key])
        return dst, key

    def rsqrt_(self, out, in_, scale, r, w):
        self.act(out, in_, AF.Ln, r=r, w=w, scale=scale, bias=EPS)
        self.act(out, out, AF.Exp, r=w, w=w, scale=-0.5)

    def build(self):
        nc, kb = self.nc, self.kb
        cfg = self.cfg
        dbg = cfg.get("dbg", False)
        self.din = {}
        for k, (shp, dt) in list(SHARED_SHAPES.items()) + list(CORE_SHAPES.items()):
            self.din[k] = nc.dram_tensor(k, shp, dt, kind="ExternalInput").ap()
        skind = "ExternalOutput" if dbg else "Internal"
        self.dout = {
            "ys": nc.dram_tensor("ys", [D, TS], F32, kind="ExternalOutput").ap(),
            "yp": nc.dram_tensor("yp", [D, TP], F32, kind="ExternalOutput").ap(),
            "so": nc.dram_tensor("so", [4, 2, 2, 128, 4, 128], F32, kind="ExternalOutput").ap(),
        }
        self.scr = {}
        for g, T in (("s", TS), ("p", TP)):
            self.scr["x1" + g] = nc.dram_tensor("x1" + g, [D, T], F32, kind=skind).ap()
            self.scr["x2" + g] = nc.dram_tensor("x2" + g, [D, T], F32, kind=skind).ap()
            self.scr["of" + g] = nc.dram_tensor("of" + g, [512, T], F32, kind=skind).ap()
            self.scr["qk" + g] = nc.dram_tensor("qk" + g, [1536, T], BF16, kind="Internal").ap()
        st = self.st
        with st:
            self.alloc()
            self.prologue()
            nl = cfg.get("layers", 2)
            groups = cfg.get("groups", ("s", "p"))
            for l in range(nl):
                self.layer_params(l)
                for g in groups:
                    self.run_group(l, g, nl)
            kb.barrier()
            kb.wait_all_dmas("sp")
            kb.emit(st)
        return nc

    def alloc(self):
        self.PB = [self.st.enter_context(self.nc.psum_tensor("pb%d" % i, [128, 512], F32)) for i in range(8)]
        sb = self.sb
        self.IDF = sb("idf", [128, 128], F32)
        self.IDB = sb("idb", [128, 128], BF16)
        self.ID4 = sb("id4", [128, 4, 128], BF16)
        self.ONEF = sb("onef", [128, 128], F32)
        self.ONEB = sb("oneb", [128, 128], BF16)
        self.TRI = sb("tri", [128, 2, 128], F32)
        self.NM4 = sb("nm4", [128, 4, 4, 128], BF16)
        self.LM4 = sb("lm4", [128, 7, 2, 4, 128], U8)
        self.LMr = sb("lmr", [128, 7, 2, 128], U8)
        self.NMr = sb("nmr", [128, 4, 128], F32)
        self.CV = sb("cv", [128, 8, 2], F32)
        self.SC = sb("sc", [128, 8, 2], BF16)
        self.MOD = sb("mod", [128, 2, 48, 2], F32)
        self.BMOD = sb("bmodsb", [128, 2, 48], F32)
        self.MODC = sb("modc", [128, 2, 6, 8], F32)
        self.GAINS = sb("gains_sb", [128, 4, 8], F32)
        self.WG = sb("wg", [128, 8, 16], BF16)
        self.DNC = sb("dnc", [128, 12, 3], F32)
        self.NEXPA = sb("nexpa", [128, 8], F32)
        self.DTB = sb("dtb_sb", [128, 8], F32)
        self.DNNG = sb("dnng_sb", [128, 1], F32)
        self.CFC = sb("cfc", [128, 4, 31], F32)
        self.CFLN = sb("cfln_sb", [128, 4, 2], F32)
        self.SCC = sb("scc", [128, 4, 3], F32)
        self.FFC = sb("ffc", [128, 44, 3], F32)
        self.X = sb("X", [128, 8, 640], F32)
        self.H = sb("H", [128, 8, 640], BF16)
        self.OD = sb("OD", [128, 4, 512], BF16)
        self.NWR = 4
        self.WR = sb("WR", [128, self.NWR, 4096], BF16)
        self.DG = sb("DG", [128, self.NDG, 128], BF16)
        self.S = sb("S", [128, 4, 128], F32)
        self.SBF = sb("SBF", [128, 4, 128], BF16)
        self.ARENA = self.cfg.get("arena_kb", 97) * 1024
        self.arena = sb("arena", [128, self.ARENA // 4], F32)
        self.aoff = 0

    def prologue(self):
        kb, din = self.kb, self.din
        kb.dma("sp", self.IDF[:], din["c_ident"], w=["consts"])
        kb.dma("pool", self.IDB[:], din["c_ident"], w=["consts"])
        kb.dma("sp", self.TRI[:], din["c_tri"].rearrange("d t i -> t d i"), w=["consts"])
        kb.dma("sp", self.NMr[:], din["c_nmask"].rearrange("k j i -> j k i"), w=["nmr"])
        kb.dma("sp", self.LMr[:], din["c_lmask"].rearrange("l k i j -> i l k j"), w=["lmr"])
        kb.dma("sp", self.CV[:], din["cvec"], w=["cv"])
        kb.dma("sp", self.BMOD[:], din["bmod"].rearrange("l p m -> p l m"), w=["bmod"])
        self.ms(self.ONEF[:], 1.0, w=["consts"])
        self.ms(self.ONEB[:], 1.0, w=["consts"])
        for h in range(4):
            self.cp(self.ID4[:, h, :], self.IDF[:], r=["consts"], w=["consts"], eng="pool")
            self.cp(self.NM4[:, :, h, :], self.NMr[:], r=["nmr"], w=["consts"], eng="pool")
            self.cp(self.LM4[:, :, :, h, :], self.LMr[:], r=["lmr"], w=["consts"], eng="pool")
        self.act(self.SC[:], self.CV[:], AF.Silu, r=["cv"], w=["sc"])
        for l in range(2):
            for g in range(12):
                src = din["wmod"][l, 4 * g:4 * g + 4].rearrange("m p k c -> p m (k c)")
                wv, wk = self.wload_3d(src, 4, 1024)
                for m in range(4):
                    mt = 4 * g + m
                    b = self.balloc()
                    for kt in range(8):
                        self.mm(self.PB[b][:, 0:2], wv[:, m, kt * 128:(kt + 1) * 128], self.SC[:, kt, :],
                                start=(kt == 0), stop=(kt == 7), r=[wk, "sc"], w=[("pb", b)])
                    self.ts(self.MOD[:, l, mt, :], self.PB[b][:, 0:2], self.BMOD[:, l, mt:mt + 1], ALU.add,
                            r=[("pb", b), "bmod"], w=["mod"])
                    self.bfree(b)

    def wload_3d(self, src, m, n):
        s = self.wr_i % self.NWR
        self.wr_i += 1
        key = ("wr", s)
        dst = self.WR[:, s, 0:m * n].rearrange("p (m n) -> p m n", m=m)
        self.kb.dma("pool", dst, src, w=[key])
        return dst, key

    def layer_params(self, l):
        kb, din = self.kb, self.din
        kb.barrier()
        kb.dma("sp", self.GAINS[:], din["gains"][l], w=["gains"])
        kb.dma("pool", self.WG[:], din["wgate"][l], w=["wg"])
        kb.dma("sp", self.DNC[:], din["dnconv"][l], w=["dnc"])
        kb.dma("sp", self.NEXPA[:], din["alog"][l], w=["nexpa"])
        kb.dma("sp", self.DTB[:], din["dtb"][l], w=["dtb"])
        kb.dma("sp", self.DNNG[:], din["dnng"][l], w=["dnng"])
        kb.dma("sp", self.CFC[:], din["cfconv"][l], w=["cfc"])
        kb.dma("sp", self.CFLN[:], din["cfln"][l], w=["cfln"])
        kb.dma("sp", self.SCC[:], din["scconv"][l], w=["scc"])
        kb.dma("sp", self.FFC[:], din["ffnconv"][l], w=["ffc"])
        self.act(self.NEXPA[:], self.NEXPA[:], AF.Exp, r=["nexpa"], w=["nexpa"])
        self.ts(self.NEXPA[:], self.NEXPA[:], -1.0, ALU.mult, r=["nexpa"], w=["nexpa"])
        for g in range(2):
            M = self.MOD[:, l, :, g]
            self.stt(self.MODC[:, g, 0, :], M[:, 8:16], 1.0, self.GAINS[:, 0, :], ALU.add, ALU.mult, r=["mod", "gains"], w=["modc"])
            self.cp(self.MODC[:, g, 1, :], M[:, 0:8], r=["mod"], w=["modc"])
            self.tt(self.MODC[:, g, 2, :], M[:, 16:24], self.GAINS[:, 1, :], ALU.mult, r=["mod", "gains"], w=["modc"])
            self.stt(self.MODC[:, g, 3, :], M[:, 32:40], 1.0, self.GAINS[:, 2, :], ALU.add, ALU.mult, r=["mod", "gains"], w=["modc"])
            self.cp(self.MODC[:, g, 4, :], M[:, 24:32], r=["mod"], w=["modc"])
            self.tt(self.MODC[:, g, 5, :], M[:, 40:48], self.GAINS[:, 3, :], ALU.mult, r=["mod", "gains"], w=["modc"])

    def cut(self, n):
        if self.cfg.get("cut") == n:
            self.stop = True
        return getattr(self, "stop", False)

    def geom(self, g):
        if g == "s":
            return dict(T=TS, ntile=TS // NT, seg=64, HL=64, gi=0, nsg=4)
        return dict(T=TP, ntile=TP // NT, seg=256, HL=0, gi=1, nsg=1)

    def run_group(self, l, g, nl):
        ge = self.geom(g)
        self.ge = ge
        self.g = g
        self.l = l
        xin = self.din["x" + g] if l == 0 else self.scr["x2" + g]
        xmid = self.scr["x1" + g]
        xout = self.dout["y" + g] if l == nl - 1 else self.scr["x2" + g]
        stages = self.cfg.get("stages", (1, 2, 3))
        if 1 in stages:
            for i in range(ge["ntile"]):
                if not getattr(self, "stop", False):
                    self.loop_dn(i, 0, xin, None)
        if 2 in stages:
            for i in reversed(range(ge["ntile"])):
                self.loop_dn(i, 1, xin, xmid)
        if 3 in stages:
            for i in range(ge["ntile"]):
                self.loop_ffn(i, xmid, xout)

    def load_x(self, i, xsrc, halo):
        ge = self.ge
        HL = ge["HL"] if halo else 0
        t0 = i * NT
        lo = max(0, t0 - HL)
        hi = min(ge["T"], t0 + NT + HL)
        c0 = lo - (t0 - HL)
        n = hi - lo
        XW = NT + 2 * HL
        src = xsrc.rearrange("(f p) t -> p f t", p=128)[:, :, lo:hi]
        rk = [("dram", id(xsrc), j) for j in range(lo // NT, (hi - 1) // NT + 1)]
        self.kb.dma("sp", self.X[:, :, c0:c0 + n], src, r=rk, w=["X"])
        if c0 > 0:
            self.ms(self.X[:, :, 0:c0], 0.0, w=["X"])
        if c0 + n < XW:
            self.ms(self.X[:, :, c0 + n:XW], 0.0, w=["X"])
        return XW, c0, c0 + n

    def normmod(self, XW, vlo, vhi, ia, ib):
        gi = self.ge["gi"]
        SQ = [self.aview("nm_sq%d" % k, [128, 640], BF16, key=("nm_sq", k)) for k in range(2)]
        RINV = self.aview("nm_rinv", [128, 640], F32, key="nm_rinv")
        TMP = [self.aview("nm_tmp%d" % k, [128, 640], F32, key=("nm_tmp", k)) for k in range(2)]
        ba = self.balloc()
        bb = self.balloc() if XW > 512 else None
        n1 = min(512, XW)
        for ft in range(8):
            sq = SQ[ft % 2]
            k = ("nm_sq", ft % 2)
            self.act(sq[:, 0:XW], self.X[:, ft, 0:XW], AF.Square, r=["X"], w=[k])
            self.mm(self.PB[ba][:, 0:n1], self.ONEB[:], sq[:, 0:n1], start=(ft == 0), stop=(ft == 7), r=["consts", k], w=[("pb", ba)])
            if bb is not None:
                self.mm(self.PB[bb][:, 0:XW - 512], self.ONEB[:], sq[:, 512:XW], start=(ft == 0), stop=(ft == 7), r=["consts", k], w=[("pb", bb)])
        if self.cut(0.1):
            self.bfree(ba)
            return
        self.act(RINV[:, 0:n1], self.PB[ba][:, 0:n1], AF.Ln, r=[("pb", ba)], w=["nm_rinv"], scale=1.0 / D, bias=EPS)
        self.bfree(ba)
        if bb is not None:
            self.act(RINV[:, 512:XW], self.PB[bb][:, 0:XW - 512], AF.Ln, r=[("pb", bb)], w=["nm_rinv"], scale=1.0 / D, bias=EPS)
            self.bfree(bb)
        if self.cut(0.2):
            return
        self.act(RINV[:, 0:XW], RINV[:, 0:XW], AF.Exp, r=["nm_rinv"], w=["nm_rinv"], scale=-0.5)
        if self.cut(0.3):
            return
        for ft in range(8):
            tmp = TMP[ft % 2]
            k = ("nm_tmp", ft % 2)
            self.stt(tmp[:, 0:XW], self.X[:, ft, 0:XW], self.MODC[:, gi, ia, ft:ft + 1], RINV[:, 0:XW], ALU.mult, ALU.mult,
                     r=["X", "modc", "nm_rinv"], w=[k])
            if self.cfg.get("nm_mode") == 1:
                continue
            if self.cfg.get("nm_mode") == 2 or (self.cfg.get("nm_mode") == 3 and ft > 0) or (self.cfg.get("nm_mode") == 4 and ft > 1) or (self.cfg.get("nm_mode") == 5 and ft != 1):
                self.act(self.H[:, ft, 0:XW], tmp[:, 0:XW], AF.Copy, r=[k, "modc"], w=["H"])
                continue
            self.act(self.H[:, ft, 0:XW], tmp[:, 0:XW], AF.Identity, r=[k, "modc"], w=["H"], bias=self.MODC[:, gi, ib, ft:ft + 1])
        if vlo > 0:
            self.ms(self.H[:, :, 0:vlo], 0.0, w=["H"])
        if vhi < XW:
            self.ms(self.H[:, :, vhi:XW], 0.0, w=["H"])

    def proj(self, wv, wk, c0, n, nk=8, rhs=None, rkey="H"):
        b = self.balloc()
        for kt in range(nk):
            r_ = self.H[:, kt, c0:c0 + n] if rhs is None else rhs(kt)
            self.mm(self.PB[b][:, 0:n], wv[:, kt * 128:(kt + 1) * 128], r_, start=(kt == 0), stop=(kt == nk - 1),
                    r=[wk, rkey], w=[("pb", b)])
        return b

    def hconv(self, PC, pkey, gi2, ntap, colv_fn, ckey, cid=None):
        ge = self.ge
        nsg, seg = ge["nsg"], ge["seg"]
        pad = ntap // 2
        SP = seg + 2 * pad
        L = nsg * SP - 2 * pad
        flat = PC[:, gi2, :, :].rearrange("p s t -> p (s t)")
        b = self.balloc()
        for tap in range(ntap):
            dg, dk = self.diag(colv_fn(tap), ckey, ck=(self.phase_id, cid, tap))
            self.mm(self.PB[b][:, 0:L], dg, flat[:, tap:tap + L], start=(tap == 0), stop=(tap == ntap - 1), r=[dk, pkey], w=[("pb", b)])
        valid = self.PB[b][:, 0:nsg * SP].rearrange("p (s t) -> p s t", s=nsg)[:, :, 0:seg]
        return b, valid

    def loop_dn(self, i, d, xin, xmid):
        ge, l, kb, din = self.ge, self.l, self.kb, self.din
        HL, seg, nsg = ge["HL"], ge["seg"], ge["nsg"]
        self.phase()
        QKV = self.aview("QKV", [128, 12, 512], BF16, key=[("qkv", m) for m in range(12)])
        OFT = self.aview("OFT", [128, 4, 512], F32, key="OFT")
        OB = self.aview("OB", [128, 4, 512], F32, key="OB") if d == 1 else None
        self.ZS = self.aview("ZS", [128, 4, 512], BF16, key=[("zs", m) for m in range(4)]) if d == 1 else None
        mark = self.aoff
        t0 = i * NT
        qsc = self.scr["qk" + self.g].rearrange("(m p) t -> p m t", p=128)[:, :, t0:t0 + NT]
        ofs = self.scr["of" + self.g].rearrange("(h p) t -> p h t", p=128)[:, :, t0:t0 + NT]
        XW, vlo, vhi = self.load_x(i, xin, halo=(d == 1))
        self.mark("dn_normmod")
        HLd = HL if d == 1 else 0
        if d == 1:
            kb.dma("sp", QKV[:], qsc, r=[("dram", "qk", self.g, i)], w=[("qkv", m) for m in range(12)])
            kb.dma("sp", OFT[:], ofs, r=[("dram", "of", self.g, i)], w=["OFT"])
        if getattr(self, "stop", False) or self.cut(0):
            return
        self.normmod(XW, vlo, vhi, 0, 1)
        if self.cut(1):
            return
        if d == 0:
            self.qkv_stage(QKV, HLd)
            kb.dma("sp", qsc, QKV[:], r=[("qkv", m) for m in range(12)], w=[("dram", "qk", self.g, i)])
        if self.cut(2):
            return
        if self.cfg.get("barriers", False):
            kb.barrier()
        self.aoff = mark
        self.dn_bufs(d)
        self.mark("chunks")
        self.dn_run(i, d, QKV, OFT, OB, HLd)
        if getattr(self, "stop", False):
            return
        if d == 0:
            kb.dma("sp", ofs, OFT[:], r=["OFT"], w=[("dram", "of", self.g, i)])
            return
        self.onorm(OB)
        self.mixer_rest(i, xmid, XW)

    def qkv_stage(self, QKV, HLd):
        ge, l, din = self.ge, self.l, self.din
        seg, nsg = ge["seg"], ge["nsg"]
        SP3 = seg + 2
        PC = [self.aview("pc%d" % k, [128, 2, nsg, SP3], BF16, key=("pc", k)) for k in range(2)]
        self.mark("qkv")
        S32 = [self.aview("s32_%d" % k, [128, 512], F32, key=("s32", k)) for k in range(8)]
        SQ = [self.aview("sq_%d" % k, [128, 512], BF16, key=("sq", k)) for k in range(4)]
        RN = [self.aview("rn_%d" % k, [128, 512], F32, key=("rn", k)) for k in range(4)]
        for k in range(2):
            self.ms(PC[k][:], 0.0, w=[("pc", k)])
        wst = {}

        def item(mt):
            g3, m = mt // 4, mt % 4
            if m == 0:
                src = din["win"][l, 4 * g3:4 * g3 + 4].rearrange("m p k c -> p m (k c)")
                wst["w"] = self.wload_3d(src, 4, 1024)
            wv, wk = wst["w"]
            for tap in range(3):
                self.diag(self.DNC[:, mt, tap:tap + 1], "dnc", ck=(self.phase_id, ("dn", mt), tap))
            b = self.proj(wv[:, m, :], wk, HLd, NT)
            pc, pk = PC[mt % 2], ("pc", mt % 2)
            for gi2 in range(2):
                self.cp(pc[:, gi2, :, 1:1 + seg], self.PB[b][:, gi2 * 256:(gi2 + 1) * 256].rearrange("p (s t) -> p s t", s=nsg),
                        r=[("pb", b)], w=[pk])
            self.bfree(b)
            yield
            for gi2 in range(2):
                b2, valid = self.hconv(pc, pk, gi2, 3, lambda tap, mt=mt: self.DNC[:, mt, tap:tap + 1], "dnc", cid=("dn", mt))
                if g3 == 2:
                    outv = QKV[:, mt, gi2 * 256:(gi2 + 1) * 256].rearrange("p (s t) -> p s t", s=nsg)
                    self.act(outv, valid, AF.Silu, r=[("pb", b2)], w=[("qkv", mt)])
                else:
                    outv = S32[mt][:, gi2 * 256:(gi2 + 1) * 256].rearrange("p (s t) -> p s t", s=nsg)
                    self.act(outv, valid, AF.Silu, r=[("pb", b2)], w=[("s32", mt)])
                self.bfree(b2)

        self.swp([item(mt) for mt in range(12)])
        def norm_item(mt):
            s32, sk = S32[mt], ("s32", mt)
            sq, qk_ = SQ[mt % 4], ("sq", mt % 4)
            rn, rk = RN[mt % 4], ("rn", mt % 4)
            self.tt(sq[:], s32[:], s32[:], ALU.mult, r=[sk], w=[qk_])
            b3 = self.balloc()
            self.mm(self.PB[b3][:], self.ONEB[:], sq[:], start=True, stop=True, r=["consts", qk_], w=[("pb", b3)])
            self.act(rn[:], self.PB[b3][:], AF.Ln, r=[("pb", b3)], w=[rk], scale=1.0, bias=EPS)
            self.bfree(b3)
            yield
            self.act(rn[:], rn[:], AF.Exp, r=[rk], w=[rk], scale=-0.5)
            yield
            scale = (128.0 ** -0.5) if mt < 4 else 1.0
            self.stt(QKV[:, mt, :], s32[:], scale, rn[:], ALU.mult, ALU.mult, r=[sk, rk], w=[("qkv", mt)])

        self.swp([norm_item(mt) for mt in range(8)])

    def dn_bufs(self, d):
        av = self.aview
        self.Bsh = {}
        for nm in ("R1", "R2", "E"):
            self.Bsh[nm] = av("dnsh_" + nm, [128, 4, 128], F32, key=("dnsh", nm))
        for nm in ("DT", "MD"):
            self.Bsh[nm] = av("dnsh_" + nm, [128, 4, 128], BF16, key=("dnsh", nm))
        for nm in ("TG1", "TG2", "TS1"):
            self.Bsh[nm] = av("dnsh_" + nm, [128, 4], F32, key=("dnsh", nm))
        self.Bs = []
        for sl in range(self.NSLOT):
            B = {}
            B["USB"] = av("dn%d_USB" % sl, [128, 4, 128], F32, key=("dn", sl, "USB"))
            for nm in ("MN", "AN", "T", "U", "YS", "YS2", "QKT", "KBt", "KD", "BV", "WT", "QD", "VN"):
                B[nm] = av("dn%d_%s" % (sl, nm), [128, 4, 128], BF16, key=("dn", sl, nm))
            for nm in ("GG", "BB", "LB", "GAMJ", "NGAMJ", "GTOT", "LGT", "EKB", "EKD"):
                B[nm] = av("dn%d_%s" % (sl, nm), [128, 4], F32, key=("dn", sl, nm))
            self.Bs.append(B)

    def dn_run(self, i, d, QKV, OFT, OB, HLd):
        order = list(range(4)) if d == 0 else list(reversed(range(4)))
        active = []
        nxt = 0
        rounds = 0
        while active or nxt < len(order):
            if nxt < len(order) and len(active) < self.NSLOT and (not active or rounds % self.STAGGER == 0):
                c = order[nxt]
                active.append(self.dn_chunk(i, c, d, QKV, OFT, OB, HLd, nxt % self.NSLOT))
                nxt += 1
            for gen in list(active):
                try:
                    next(gen)
                except StopIteration:
                    active.remove(gen)
            rounds += 1
            if getattr(self, "stop", False):
                return

    def dn_chunk(self, i, c, d, QKV, OFT, OB, HLd, sl):
        kb, B, SH, PB, l, ge, g = self.kb, self.Bs[sl], self.Bsh, self.PB, self.l, self.ge, self.g
        cs = slice(c * 128, (c + 1) * 128)
        hs = slice(HLd + c * 128, HLd + (c + 1) * 128)
        gc = i * 4 + c
        R = lambda *names: [("dn", sl, n) for n in names]
        RS = lambda *names: [("dnsh", n) for n in names]
        flat = lambda ap: ap.rearrange("p h i -> p (h i)")
        bc_t = lambda ap: ap.unsqueeze(1).to_broadcast([128, 4, 128])
        bc_h = lambda ap: ap.unsqueeze(2).to_broadcast([128, 4, 128])
        bg = self.balloc()
        for kt in range(8):
            self.mm(PB[bg][:, 0:16], self.H[:, kt, hs], self.WG[:, kt, :], start=(kt == 0), stop=(kt == 7), r=["H", "wg"], w=[("pb", bg)])
        a_ps = PB[bg][:, 4 * d:4 * d + 4]
        b_ps = PB[bg][:, 8 + 4 * d:12 + 4 * d]
        self.tt(SH["TG1"][:], a_ps, self.DTB[:, 4 * d:4 * d + 4], ALU.add, r=[("pb", bg), "dtb"], w=RS("TG1"))
        self.act(SH["TG2"][:], SH["TG1"][:], AF.Exp, r=RS("TG1"), w=RS("TG2"))
        self.act(SH["TG2"][:], SH["TG2"][:], AF.Ln, r=RS("TG2"), w=RS("TG2"), bias=1.0)
        self.tt(B["GG"][:], SH["TG2"][:], self.NEXPA[:, 4 * d:4 * d + 4], ALU.mult, r=RS("TG2") + ["nexpa"], w=R("GG"))
        self.act(SH["TS1"][:], b_ps, AF.Exp, r=[("pb", bg)], w=RS("TS1"), scale=-1.0)
        self.bfree(bg)
        self.act(SH["TS1"][:], SH["TS1"][:], AF.Ln, r=RS("TS1"), w=RS("TS1"), bias=1.0)
        self.ts(B["LB"][:], SH["TS1"][:], -1.0, ALU.mult, r=RS("TS1"), w=R("LB"))
        self.act(B["BB"][:], SH["TS1"][:], AF.Exp, r=RS("TS1"), w=R("BB"), scale=-1.0)
        yield
        tri = self.TRI[:, d, :]
        bgam = self.balloc()
        self.mm(PB[bgam][:, 0:4], tri, B["GG"][:], start=True, stop=True, r=["consts"] + R("GG"), w=[("pb", bgam)])
        self.mm(PB[bgam][:, 4:8], self.ONEF[:], B["GG"][:], start=True, stop=True, r=["consts"] + R("GG"), w=[("pb", bgam)])
        self.cp(B["GAMJ"][:], PB[bgam][:, 0:4], r=[("pb", bgam)], w=R("GAMJ"))
        self.cp(B["LGT"][:], PB[bgam][:, 4:8], r=[("pb", bgam)], w=R("LGT"))
        self.act(B["GTOT"][:], PB[bgam][:, 4:8], AF.Exp, r=[("pb", bgam)], w=R("GTOT"))
        self.ts(B["NGAMJ"][:], PB[bgam][:, 0:4], -1.0, ALU.mult, r=[("pb", bgam)], w=R("NGAMJ"))
        self.bfree(bgam)
        self.tt(SH["TG1"][:], B["GAMJ"][:], B["LB"][:], ALU.add, r=R("GAMJ", "LB"), w=RS("TG1"))
        self.act(B["EKB"][:], SH["TG1"][:], AF.Exp, r=RS("TG1"), w=R("EKB"))
        self.tt(SH["TG2"][:], B["LGT"][:], B["GAMJ"][:], ALU.subtract, r=R("LGT", "GAMJ"), w=RS("TG2"))
        self.act(B["EKD"][:], SH["TG2"][:], AF.Exp, r=RS("TG2"), w=R("EKD"))
        yield
        self.tt(SH["R1"][:], bc_t(tri), bc_h(B["GG"][:]), ALU.mult, r=["consts"] + R("GG"), w=RS("R1"), eng="pool")
        self.tt(SH["R2"][:], bc_t(self.IDF[:]), bc_h(B["LB"][:]), ALU.mult, r=["consts"] + R("LB"), w=RS("R2"), eng="pool")
        self.tt(SH["R2"][:], SH["R2"][:], SH["R1"][:], ALU.add, r=RS("R1", "R2"), w=RS("R2"), eng="pool")
        p0, p1, p2 = self.balloc(), self.balloc(), self.balloc()
        self.mm(PB[p0][:], self.ONEF[:], flat(SH["R1"][:]), start=True, stop=True, r=["consts"] + RS("R1"), w=[("pb", p0)])
        self.mm(PB[p1][:], self.ONEF[:], flat(SH["R1"][:]), start=True, stop=False, r=["consts"] + RS("R1"), w=[("pb", p1)])
        self.mm(PB[p1][:], self.IDB[:], flat(self.NM4[:, 2 * d, :, :]), start=False, stop=True, r=["consts"], w=[("pb", p1)])
        self.mm(PB[p2][:], self.ONEF[:], flat(SH["R2"][:]), start=True, stop=False, r=["consts"] + RS("R2"), w=[("pb", p2)])
        self.mm(PB[p2][:], self.IDB[:], flat(self.NM4[:, 2 * d + 1, :, :]), start=False, stop=True, r=["consts"], w=[("pb", p2)])
        self.act(flat(SH["E"][:]), PB[p0][:], AF.Exp, r=[("pb", p0)], w=RS("E"))
        self.bfree(p0)
        for h in range(4):
            self.act(SH["DT"][:, h, :], PB[p1][:, h * 128:(h + 1) * 128], AF.Exp, r=[("pb", p1)] + R("NGAMJ"), w=RS("DT"), bias=B["NGAMJ"][:, h:h + 1])
            self.act(SH["MD"][:, h, :], PB[p2][:, h * 128:(h + 1) * 128], AF.Exp, r=[("pb", p2)] + R("NGAMJ"), w=RS("MD"), bias=B["NGAMJ"][:, h:h + 1])
        self.bfree(p1)
        self.bfree(p2)
        self.tt(B["QD"][:], QKV[:, 0:4, cs], SH["E"][:], ALU.mult, r=[("qkv", h) for h in range(4)] + RS("E"), w=R("QD"), eng="pool")
        pkk, pqk = self.balloc(), self.balloc()
        for h in range(4):
            kh = QKV[:, 4 + h, cs]
            qh = QKV[:, h, cs]
            self.mm(PB[pkk][:, h * 128:(h + 1) * 128], kh, kh, start=True, stop=True, r=[("qkv", 4 + h)], w=[("pb", pkk)])
            self.mm(PB[pqk][:, h * 128:(h + 1) * 128], kh, qh, start=True, stop=True, r=[("qkv", 4 + h), ("qkv", h)], w=[("pb", pqk)])
        self.stt(flat(B["MN"][:]), PB[pkk][:], -1.0, flat(SH["MD"][:]), ALU.mult, ALU.mult, r=[("pb", pkk)] + RS("MD"), w=R("MN"))
        self.tt(flat(B["QKT"][:]), PB[pqk][:], flat(SH["DT"][:]), ALU.mult, r=[("pb", pqk)] + RS("DT"), w=R("QKT"))
        self.bfree(pkk)
        self.bfree(pqk)
        yield
        ptr = self.balloc()
        ptv = PB[ptr][:].bitcast(BF16)
        for h in range(4):
            self.tr(ptv[:, h * 128:(h + 1) * 128], B["MN"][:, h, :], self.IDB[:], r=["consts"] + R("MN"), w=[("pb", ptr)])
        self.act(flat(B["AN"][:]), ptv[:, 0:512], AF.Copy, r=[("pb", ptr)], w=R("AN"))
        self.bfree(ptr)
        mT, mU = (0, 1) if d == 0 else (1, 0)
        self.cp(flat(B["T"][:]), flat(self.ID4[:]), r=["consts"], w=R("T"), eng="pool")
        self.cp(flat(B["U"][:]), flat(self.ID4[:]), r=["consts"], w=R("U"), eng="pool")
        self.cpred(flat(B["U"][:]), flat(self.LM4[:, 0, mU, :, :]), flat(B["MN"][:]), r=["consts"] + R("MN", "U"), w=R("U"))
        yield
        self.cpred(flat(B["T"][:]), flat(self.LM4[:, 0, mT, :, :]), flat(B["AN"][:]), r=["consts"] + R("AN", "T"), w=R("T"))
        for lev in range(1, 7):
            lastlev = (lev == 6)
            py = self.balloc()
            py2 = None if lastlev else self.balloc()
            for h in range(4):
                hsl = slice(h * 128, (h + 1) * 128)
                self.mm(PB[py][:, hsl], B["AN"][:, h, :], B["U"][:, h, :], start=True, stop=True, r=R("AN", "U"), w=[("pb", py)])
            if not lastlev:
                for h in range(4):
                    hsl = slice(h * 128, (h + 1) * 128)
                    self.mm(PB[py2][:, hsl], B["MN"][:, h, :], B["T"][:, h, :], start=True, stop=True, r=R("MN", "T"), w=[("pb", py2)])
            self.act(flat(B["YS"][:]), PB[py][:], AF.Copy, r=[("pb", py)], w=R("YS"))
            self.bfree(py)
            if not lastlev:
                self.act(flat(B["YS2"][:]), PB[py2][:], AF.Copy, r=[("pb", py2)], w=R("YS2"))
                self.bfree(py2)
            yield
            pz = self.balloc()
            pz2 = None if lastlev else self.balloc()
            for h in range(4):
                hsl = slice(h * 128, (h + 1) * 128)
                self.mm(PB[pz][:, hsl], B["T"][:, h, :], B["YS"][:, h, :], start=True, stop=True, r=R("T", "YS"), w=[("pb", pz)])
            if not lastlev:
                for h in range(4):
                    hsl = slice(h * 128, (h + 1) * 128)
                    self.mm(PB[pz2][:, hsl], B["U"][:, h, :], B["YS2"][:, h, :], start=True, stop=True, r=R("U", "YS2"), w=[("pb", pz2)])
            self.cpred(flat(B["U"][:]), flat(self.LM4[:, lev, mU, :, :]), PB[pz][:], r=["consts", ("pb", pz)] + R("U"), w=R("U"))
            self.bfree(pz)
            if not lastlev:
                self.cpred(flat(B["T"][:]), flat(self.LM4[:, lev, mT, :, :]), PB[pz2][:], r=["consts", ("pb", pz2)] + R("T"), w=R("T"))
                self.bfree(pz2)
            yield
        pk_, pv_ = self.balloc(), self.balloc()
        pkv, pvv = PB[pk_][:].bitcast(BF16), PB[pv_][:].bitcast(BF16)
        for h in range(4):
            self.tr(pkv[:, h * 128:(h + 1) * 128], QKV[:, 4 + h, cs], self.IDB[:], r=["consts", ("qkv", 4 + h)], w=[("pb", pk_)])
            self.tr(pvv[:, h * 128:(h + 1) * 128], QKV[:, 8 + h, cs], self.IDB[:], r=["consts", ("qkv", 8 + h)], w=[("pb", pv_)])
        pk3 = pkv[:, 0:512].rearrange("p (h x) -> p h x", h=4)
        pv3 = pvv[:, 0:512].rearrange("p (h x) -> p h x", h=4)
        self.tt(B["KBt"][:], pk3, bc_h(B["EKB"][:]), ALU.mult, r=[("pb", pk_)] + R("EKB"), w=R("KBt"))
        self.tt(B["KD"][:], pk3, bc_h(B["EKD"][:]), ALU.mult, r=[("pb", pk_)] + R("EKD"), w=R("KD"))
        self.tt(B["BV"][:], pv3, bc_h(B["BB"][:]), ALU.mult, r=[("pb", pv_)] + R("BB"), w=R("BV"))
        self.bfree(pk_)
        self.bfree(pv_)
        yield
        pu, pw = self.balloc(), self.balloc()
        for h in range(4):
            hsl = slice(h * 128, (h + 1) * 128)
            self.mm(PB[pu][:, hsl], B["U"][:, h, :], B["BV"][:, h, :], start=True, stop=True, r=R("U", "BV"), w=[("pb", pu)])
        for h in range(4):
            hsl = slice(h * 128, (h + 1) * 128)
            self.mm(PB[pw][:, hsl], B["KBt"][:, h, :], B["U"][:, h, :], start=True, stop=True, r=R("KBt", "U"), w=[("pb", pw)])
        self.act(flat(B["USB"][:]), PB[pu][:], AF.Copy, r=[("pb", pu)], w=R("USB"))
        self.cp(flat(B["WT"][:]), PB[pw][:], r=[("pb", pw)], w=R("WT"))
        self.bfree(pu)
        self.bfree(pw)
        yield
        if g == "s":
            first = (gc == 0) if d == 0 else (gc == ge["T"] // 128 - 1)
            if first:
                kb.dma("sp", self.S[:], self.din["st0"][l, d], w=["S"])
                self.act(flat(self.SBF[:]), flat(self.S[:]), AF.Copy, r=["S"], w=["SBF"])
        else:
            first = (gc % 2 == 0) if d == 0 else (gc % 2 == 1)
            if first:
                self.ms(self.S[:], 0.0, w=["S"])
                self.ms(self.SBF[:], 0.0, w=["SBF"])
        pa = self.balloc()
        for h in range(4):
            hsl = slice(h * 128, (h + 1) * 128)
            self.mm(PB[pa][:, hsl], B["WT"][:, h, :], self.SBF[:, h, :], start=True, stop=True, r=R("WT") + ["SBF"], w=[("pb", pa)])
        self.tt(flat(B["VN"][:]), flat(B["USB"][:]), PB[pa][:], ALU.subtract, r=[("pb", pa)] + R("USB"), w=R("VN"))
        self.bfree(pa)
        po, ps_ = self.balloc(), self.balloc()
        for h in range(4):
            hsl = slice(h * 128, (h + 1) * 128)
            self.mm(PB[ps_][:, hsl], B["KD"][:, h, :], B["VN"][:, h, :], start=True, stop=True, r=R("KD", "VN"), w=[("pb", ps_)])
        for h in range(4):
            hsl = slice(h * 128, (h + 1) * 128)
            self.mm(PB[po][:, hsl], self.SBF[:, h, :], B["QD"][:, h, :], start=True, stop=False, r=R("QD") + ["SBF"], w=[("pb", po)])
            self.mm(PB[po][:, hsl], B["VN"][:, h, :], B["QKT"][:, h, :], start=False, stop=True, r=R("VN", "QKT"), w=[("pb", po)])
        for h in range(4):
            hsl = slice(h * 128, (h + 1) * 128)
            self.stt(self.S[:, h, :], self.S[:, h, :], B["GTOT"][:, h:h + 1], PB[ps_][:, hsl], ALU.mult, ALU.add,
                     r=["S", ("pb", ps_)] + R("GTOT"), w=["S"])
        self.bfree(ps_)
        self.act(flat(self.SBF[:]), flat(self.S[:]), AF.Copy, r=["S"], w=["SBF"])
        POv = PB[po][:].rearrange("p (h i) -> p h i", h=4)
        if d == 0:
            self.act(OFT[:, :, cs], POv, AF.Copy, r=[("pb", po)], w=["OFT"])
        else:
            self.tt(OB[:, :, cs], POv, OFT[:, :, cs], ALU.add, r=[("pb", po), "OFT"], w=["OB"])
        self.bfree(po)
        if g == "p":
            lastc = (gc % 2 == 1) if d == 0 else (gc % 2 == 0)
            if lastc:
                kb.dma("sp", self.dout["so"][gc // 2, l, d], self.S[:], r=["S"], w=[("dram", "so", gc // 2, l, d)])

    def onorm(self, OB):
        l, din, PB = self.l, self.din, self.PB
        HL = self.ge["HL"]
        ZS = self.ZS
        self.mark("onorm")
        SQ = [self.aview("on_sq%d" % k, [128, 512], BF16, key=("on_sq", k)) for k in range(2)]
        RN = [self.aview("on_rn%d" % k, [128, 512], F32, key=("on_rn", k)) for k in range(2)]
        TMP = [self.aview("on_tmp%d" % k, [128, 512], F32, key=("on_tmp", k)) for k in range(2)]
        src = din["win"][l, 12:16].rearrange("m p k c -> p m (k c)")
        wv, wk = self.wload_3d(src, 4, 1024)
        for m in range(4):
            b = self.proj(wv[:, m, :], wk, HL, NT)
            self.act(ZS[:, m, :], PB[b][:], AF.Silu, r=[("pb", b)], w=[("zs", m)])
            self.bfree(b)
        for h in range(4):
            k2 = h % 2
            self.tt(SQ[k2][:], OB[:, h, :], OB[:, h, :], ALU.mult, r=["OB"], w=[("on_sq", k2)])
            b = self.balloc()
            self.mm(PB[b][:], self.ONEB[:], SQ[k2][:], start=True, stop=True, r=["consts", ("on_sq", k2)], w=[("pb", b)])
            self.rsqrt_(RN[k2][:], PB[b][:], 1.0 / 128, r=[("pb", b)], w=[("on_rn", k2)])
            self.bfree(b)
            self.tt(TMP[k2][:], OB[:, h, :], RN[k2][:], ALU.mult, r=["OB", ("on_rn", k2)], w=[("on_tmp", k2)])
            self.stt(self.OD[:, h, :], TMP[k2][:], self.DNNG[:, 0:1], ZS[:, h, :], ALU.mult, ALU.mult,
                     r=[("on_tmp", k2), "dnng", ("zs", h)], w=[("od", h)])

    def mixer_rest(self, i, xmid, XW):
        ge, l, kb, din, PB = self.ge, self.l, self.kb, self.din, self.PB
        HL, seg, nsg, gi = ge["HL"], ge["seg"], ge["nsg"], ge["gi"]
        self.phase()
        av = self.aview
        SP31 = seg + 30
        self.mark("mix_cf")
        PC31 = [av("pc31_%d" % k, [128, 2, nsg, SP31], BF16, key=("pc31", k)) for k in range(2)]
        SG = [av("sg%d" % k, [128, 640], F32, key=("sg", k)) for k in range(3)]
        HC = av("HC", [128, 4, 512], F32, key=[("hc", m) for m in range(4)])
        HCB = [av("hcb%d" % k, [128, 512], BF16, key=("hcb", k)) for k in range(2)]
        SQB = [av("sqb%d" % k, [128, 512], BF16, key=("sqb", k)) for k in range(2)]
        MEAN = av("MEAN", [128, 512], F32, key="mean")
        RSTD = av("RSTD", [128, 512], F32, key="rstd")
        HCN = av("HCN", [128, 4, 512], BF16, key=[("hcn", m) for m in range(4)])
        SCV = av("SCV", [128, 4, 512], BF16, key=[("scv", m) for m in range(4)])
        M = av("M", [128, 8, 512], BF16, key=[("m", j) for j in range(8)])
        TM = [av("tm%d" % k, [128, 512], F32, key=("tm", k)) for k in range(3)]
        for k in range(2):
            self.ms(PC31[k][:], 0.0, w=[("pc31", k)])
        wa, wak = self.wload_3d(din["win"][l, 16:20].rearrange("m p k c -> p m (k c)"), 4, 1024)
        wb, wbk = self.wload_3d(din["win"][l, 20:24].rearrange("m p k c -> p m (k c)"), 4, 1024)
        bs1, bs2 = self.balloc(), self.balloc()
        def cf_item(m):
            for tap in range(min(31, self.NDG - 32)):
                self.diag(self.CFC[:, m, tap:tap + 1], "cfc", ck=(self.phase_id, ("cf", m), tap))
            ba = self.proj(wa[:, m, :], wak, HL, NT)
            bb = self.proj(wb[:, m, :], wbk, HL, NT)
            sg, sgk = SG[m % 2], ("sg", m % 2)
            self.act(sg[:, 0:512], PB[bb][:], AF.Sigmoid, r=[("pb", bb)], w=[sgk])
            self.bfree(bb)
            pc, pk = PC31[m % 2], ("pc31", m % 2)
            for g2 in range(2):
                cs2 = slice(g2 * 256, (g2 + 1) * 256)
                self.tt(pc[:, g2, :, 15:15 + seg], PB[ba][:, cs2].rearrange("p (s t) -> p s t", s=nsg),
                        sg[:, cs2].rearrange("p (s t) -> p s t", s=nsg), ALU.mult, r=[("pb", ba), sgk], w=[pk])
            self.bfree(ba)
            yield
            for g2 in range(2):
                cs2 = slice(g2 * 256, (g2 + 1) * 256)
                b2, valid = self.hconv(pc, pk, g2, 31, lambda tap, m=m: self.CFC[:, m, tap:tap + 1], "cfc", cid=("cf", m))
                self.act(HC[:, m, cs2].rearrange("p (s t) -> p s t", s=nsg), valid, AF.Copy, r=[("pb", b2)], w=[("hc", m)])
                self.bfree(b2)
            hb, hbk = HCB[m % 2], ("hcb", m % 2)
            sq, sqk = SQB[m % 2], ("sqb", m % 2)
            self.cp(hb[:], HC[:, m, :], r=[("hc", m)], w=[hbk], eng="pool")
            self.act(sq[:], HC[:, m, :], AF.Square, r=[("hc", m)], w=[sqk])
            yield
            self.mm(PB[bs1][:], self.ONEB[:], hb[:], start=(m == 0), stop=(m == 3), r=["consts", hbk], w=[("pb", bs1)])
            self.mm(PB[bs2][:], self.ONEB[:], sq[:], start=(m == 0), stop=(m == 3), r=["consts", sqk], w=[("pb", bs2)])

        self.swp([cf_item(m) for m in range(4)])
        self.act(MEAN[:], PB[bs1][:], AF.Copy, r=[("pb", bs1)], w=["mean"], scale=1.0 / 512)
        self.bfree(bs1)
        self.tt(RSTD[:], MEAN[:], MEAN[:], ALU.mult, r=["mean"], w=["rstd"])
        self.stt(RSTD[:], PB[bs2][:], 1.0 / 512, RSTD[:], ALU.mult, ALU.subtract, r=[("pb", bs2), "rstd"], w=["rstd"])
        self.bfree(bs2)
        self.ts(RSTD[:], RSTD[:], 0.0, ALU.max, r=["rstd"], w=["rstd"])
        self.act(RSTD[:], RSTD[:], AF.Ln, r=["rstd"], w=["rstd"], bias=EPS)
        self.act(RSTD[:], RSTD[:], AF.Exp, r=["rstd"], w=["rstd"], scale=-0.5)
        for m in range(4):
            tm, tk = TM[m % 2], ("tm", m % 2)
            self.tt(tm[:], HC[:, m, :], MEAN[:], ALU.subtract, r=[("hc", m), "mean"], w=[tk])
            self.tt(tm[:], tm[:], RSTD[:], ALU.mult, r=[tk, "rstd"], w=[tk])
            self.act(HCN[:, m, :], tm[:], AF.Silu, r=[tk, "cfln"], w=[("hcn", m)], scale=self.CFLN[:, m, 0:1], bias=self.CFLN[:, m, 1:2])
        wbg, wbgk = self.wload_3d(din["win"][l, 24:28].rearrange("m p k c -> p m (k c)"), 4, 1024)
        self.mark("mix_sc")
        wcg, wcgk = self.wload_3d(din["win"][l, 28:32].rearrange("m p k c -> p m (k c)"), 4, 1024)
        wxh, wxhk = self.wload_3d(din["win"][l, 32:36].rearrange("m p k c -> p m (k c)"), 4, 1024)
        if HL > 0:
            SCP = [av("scp%d" % k, [128, 640], BF16, key=("scp", k)) for k in range(2)]
        else:
            SCP = [av("scp%d" % k, [128, 2, 1, seg + 2], BF16, key=("scp", k)) for k in range(2)]
            for k in range(2):
                self.ms(SCP[k][:], 0.0, w=[("scp", k)])
        ranges = [(0, 512)] + ([(512, XW)] if XW > 512 else [])
        def sc_item(m):
            scp, sk = SCP[m % 2], ("scp", m % 2)
            cgs, cgk = SG[2], ("sg", 2)
            for (c0, c1) in ranges:
                bc = self.proj(wcg[:, m, :], wcgk, c0, c1 - c0)
                self.act(cgs[:, c0:c1], PB[bc][:, 0:c1 - c0], AF.Copy, r=[("pb", bc)], w=[cgk])
                self.bfree(bc)
                bx = self.proj(wxh[:, m, :], wxhk, c0, c1 - c0)
                if HL > 0:
                    self.tt(scp[:, c0:c1], PB[bx][:, 0:c1 - c0], cgs[:, c0:c1], ALU.mult, r=[("pb", bx), cgk], w=[sk])
                else:
                    for g2 in range(2):
                        cs2 = slice(g2 * 256, (g2 + 1) * 256)
                        self.tt(scp[:, g2, 0, 1:1 + seg], PB[bx][:, cs2], cgs[:, cs2], ALU.mult, r=[("pb", bx), cgk], w=[sk])
                self.bfree(bx)
            bbg = self.proj(wbg[:, m, :], wbgk, HL, NT)
            sg, sgk = SG[m % 2], ("sg", m % 2)
            self.act(sg[:, 0:512], PB[bbg][:], AF.Copy, r=[("pb", bbg)], w=[sgk])
            self.bfree(bbg)
            yield
            bcv = self.balloc()
            if HL > 0:
                for tap in range(3):
                    dg, dk = self.diag(self.SCC[:, m, tap:tap + 1], "scc")
                    self.mm(PB[bcv][:], dg, scp[:, tap * 64:tap * 64 + 512], start=(tap == 0), stop=(tap == 2), r=[dk, sk], w=[("pb", bcv)])
            else:
                for g2 in range(2):
                    for tap in range(3):
                        dg, dk = self.diag(self.SCC[:, m, tap:tap + 1], "scc", ck=(self.phase_id, "sc", m, tap))
                        self.mm(PB[bcv][:, g2 * 256:(g2 + 1) * 256], dg, scp[:, g2, 0, tap:tap + 256], start=(tap == 0), stop=(tap == 2),
                                r=[dk, sk], w=[("pb", bcv)])
            self.tt(SCV[:, m, :], PB[bcv][:], sg[:, 0:512], ALU.mult, r=[("pb", bcv), sgk], w=[("scv", m)])
            self.bfree(bcv)

        self.swp([sc_item(m) for m in range(4)])
        self.mark("mix_merge")
        for j in range(8):
            wg_, wgk = self.wload_3d(din["win"][l, 36 + 3 * j:39 + 3 * j].rearrange("m p k c -> p m (k c)"), 3, 1024)
            wy, wyk = self.wload_3d(din["wbr"][l, j].rearrange("p b k c -> p b (k c)"), 3, 512)
            srcs = [(self.OD, [("od", h) for h in range(4)]), (HCN, [("hcn", h) for h in range(4)]), (SCV, [("scv", h) for h in range(4)])]
            for br in range(3):
                bgt = self.proj(wg_[:, br, :], wgk, HL, NT)
                sg, sgk = SG[br], ("sg", br)
                self.act(sg[:, 0:512], PB[bgt][:], AF.Sigmoid, r=[("pb", bgt)], w=[sgk])
                self.bfree(bgt)
                buf, keys = srcs[br]
                by = self.balloc()
                for kt in range(4):
                    self.mm(PB[by][:], wy[:, br, kt * 128:(kt + 1) * 128], buf[:, kt, :], start=(kt == 0), stop=(kt == 3),
                            r=[wyk, keys[kt]], w=[("pb", by)])
                tm, tk = TM[br], ("tm", br)
                self.tt(tm[:], PB[by][:], sg[:, 0:512], ALU.mult, r=[("pb", by), sgk], w=[tk])
                self.bfree(by)
            self.tt(TM[0][:], TM[0][:], TM[1][:], ALU.add, r=[("tm", 0), ("tm", 1)], w=[("tm", 0)])
            self.tt(M[:, j, :], TM[0][:], TM[2][:], ALU.add, r=[("tm", 0), ("tm", 2)], w=[("m", j)])
        self.mark("mix_wo")
        self.out_proj_res(lambda j: self.wload_3d(din["wo"][l, j:j + 1].rearrange("m p k c -> p m (k c)"), 1, 1024),
                          8, lambda kt: M[:, kt, :], lambda kt: ("m", kt), 2, SQB, HL)
        t0 = i * NT
        dst = xmid.rearrange("(f p) t -> p f t", p=128)[:, :, t0:t0 + NT]
        kb.dma("sp", dst, self.X[:, :, HL:HL + NT], r=["X"], w=[("dram", id(xmid), i)])

    def out_proj_res(self, wfn, nk, rhs, rkeyfn, ic, SQB, HL):
        PB = self.PB
        gi = self.ge["gi"]
        self.MO = self.aview("MO", [128, 8, 512], F32, key=[("mo", j) for j in range(8)])
        RINV = self.aview("opr_rinv", [128, 512], F32, key="opr_rinv")
        TMP = [self.aview("opr_tmp%d" % k, [128, 512], F32, key=("opr_tmp", k)) for k in range(2)]
        bss = self.balloc()

        def item(j):
            wv, wk = wfn(j)
            b = self.balloc()
            for kt in range(nk):
                self.mm(PB[b][:], wv[:, 0, kt * 128:(kt + 1) * 128], rhs(kt), start=(kt == 0), stop=(kt == nk - 1),
                        r=[wk, rkeyfn(kt)], w=[("pb", b)])
            self.act(self.MO[:, j, :], PB[b][:], AF.Copy, r=[("pb", b)], w=[("mo", j)])
            sq, sqk = SQB[j % 2], ("sqb", j % 2)
            self.act(sq[:], PB[b][:], AF.Square, r=[("pb", b)], w=[sqk])
            self.bfree(b)
            yield
            self.mm(PB[bss][:], self.ONEB[:], sq[:], start=(j == 0), stop=(j == 7), r=["consts", sqk], w=[("pb", bss)])

        self.swp([item(j) for j in range(8)])
        self.rsqrt_(RINV[:], PB[bss][:], 1.0 / D, r=[("pb", bss)], w=["opr_rinv"])
        self.bfree(bss)
        for j in range(8):
            tmp, tk = TMP[j % 2], ("opr_tmp", j % 2)
            self.tt(tmp[:], self.MO[:, j, :], RINV[:], ALU.mult, r=[("mo", j), "opr_rinv"], w=[tk])
            xc = self.X[:, j, HL:HL + NT]
            self.stt(xc, tmp[:], self.MODC[:, gi, ic, j:j + 1], xc, ALU.mult, ALU.add, r=[tk, "modc", "X"], w=["X"])

    def loop_ffn(self, i, xmid, xout):
        ge, l, kb, din, PB = self.ge, self.l, self.kb, self.din, self.PB
        HL, seg, nsg, gi = ge["HL"], ge["seg"], ge["nsg"], ge["gi"]
        self.phase()
        XW, vlo, vhi = self.load_x(i, xmid, halo=True)
        self.mark("ffn_normmod")
        self.normmod(XW, vlo, vhi, 3, 4)
        av = self.aview
        HID = av("HID", [128, 22, 512], BF16, key=[("hid", c) for c in range(22)])
        self.mark("ffn_up")
        SA = [av("sa%d" % k, [128, 512], F32, key=("sa", k)) for k in range(2)]
        SQB = [av("fsqb%d" % k, [128, 512], BF16, key=("sqb", k)) for k in range(2)]
        if HL > 0:
            UB = [av("ub%d" % k, [128, 640], BF16, key=("ub", k)) for k in range(4)]
        else:
            UB = [av("ub%d" % k, [128, 2, 1, seg + 2], BF16, key=("ub", k)) for k in range(4)]
            for k in range(4):
                self.ms(UB[k][:], 0.0, w=[("ub", k)])
        ranges = [(0, 512)] + ([(512, XW)] if XW > 512 else [])
        wst = {}

        def up_item(c):
            if c % 2 == 0:
                wst["w"] = self.wload_3d(din["wup"][l, c:c + 2].rearrange("m p a k c -> p m (a k c)"), 2, 2048)
            wv, wk = wst["w"]
            ubs = []
            for ab in range(2):
                ub, uk = UB[(c % 2) * 2 + ab], ("ub", (c % 2) * 2 + ab)
                ubs.append((ub, uk))
                wsl = wv[:, c % 2, ab * 1024:(ab + 1) * 1024]
                for (c0, c1) in ranges:
                    b = self.proj(wsl, wk, c0, c1 - c0)
                    if HL > 0:
                        self.act(ub[:, c0:c1], PB[b][:, 0:c1 - c0], AF.Copy, r=[("pb", b)], w=[uk])
                    else:
                        for g2 in range(2):
                            self.act(ub[:, g2, 0, 1:1 + seg], PB[b][:, g2 * 256:(g2 + 1) * 256], AF.Copy, r=[("pb", b)], w=[uk])
                    self.bfree(b)
            yield
            cb = []
            for ab in range(2):
                ub, uk = ubs[ab]
                ct = ab * 22 + c
                bcv = self.balloc()
                if HL > 0:
                    for tap in range(3):
                        dg, dk = self.diag(self.FFC[:, ct, tap:tap + 1], "ffc")
                        self.mm(PB[bcv][:], dg, ub[:, tap * 64:tap * 64 + 512], start=(tap == 0), stop=(tap == 2), r=[dk, uk], w=[("pb", bcv)])
                else:
                    for g2 in range(2):
                        for tap in range(3):
                            dg, dk = self.diag(self.FFC[:, ct, tap:tap + 1], "ffc", ck=(self.phase_id, "ffn", ct, tap))
                            self.mm(PB[bcv][:, g2 * 256:(g2 + 1) * 256], dg, ub[:, g2, 0, tap:tap + 256], start=(tap == 0), stop=(tap == 2),
                                    r=[dk, uk], w=[("pb", bcv)])
                cb.append(bcv)
            sa, sak = SA[c % 2], ("sa", c % 2)
            self.act(sa[:], PB[cb[0]][:], AF.Silu, r=[("pb", cb[0])], w=[sak])
            self.bfree(cb[0])
            self.tt(HID[:, c, :], PB[cb[1]][:], sa[:], ALU.mult, r=[("pb", cb[1]), sak], w=[("hid", c)])
            self.bfree(cb[1])

        self.swp([up_item(c) for c in range(22)])
        self.mark("ffn_down")
        self.out_proj_res(lambda j: self.wload_3d(din["wdown"][l, j:j + 1].rearrange("m p k c -> p m (k c)"), 1, 2816),
                          22, lambda kt: HID[:, kt, :], lambda kt: ("hid", kt), 5, SQB, HL)
        t0 = i * NT
        dst = xout.rearrange("(f p) t -> p f t", p=128)[:, :, t0:t0 + NT]
        kb.dma("sp", dst, self.X[:, :, HL:HL + NT], r=["X"], w=[("dram", id(xout), i)])


_CACHE = {}


def get_program(cfg=None):
    key = repr(sorted((cfg or {}).items()))
    if key not in _CACHE:
        _CACHE[key] = Prog(cfg).build()
    return _CACHE[key]


def make_in_maps(inputs):
    sh = _shared(inputs)
    xs = np.asarray(inputs["x_sample"], np.float32)
    xp = np.asarray(inputs["x_prompt"], np.float32)
    st = np.asarray(inputs["state_dn"], np.float32)
    c = np.asarray(inputs["c"], np.float32)
    cc = np.asarray(inputs["c_ctx"], np.float32)
    maps = []
    for core in range(8):
        b = core % 4
        m = dict(sh)
        m["xs"] = np.ascontiguousarray(xs[b].T)
        m["xp"] = np.ascontiguousarray(xp[4 * core:4 * core + 4].reshape(TP, D).T)
        m["st0"] = np.ascontiguousarray(st[b].transpose(0, 1, 3, 2, 4))
        m["cvec"] = np.ascontiguousarray(np.stack([_fm(c[b], 8), _fm(cc, 8)], axis=2))
        maps.append(m)
    return maps


def kernel(**inputs):
    nc = get_program()
    maps = make_in_maps(inputs)
    res = run_bass_kernel_spmd(nc, maps, core_ids=list(range(8)))
    R = res.results
    y_sample = np.stack([np.ascontiguousarray(R[b]["ys"].T) for b in range(4)]).astype(np.float32)
    y_prompt = np.concatenate([np.ascontiguousarray(R[c]["yp"].T).reshape(4, 256, D) for c in range(8)]).astype(np.float32)
    so = np.concatenate([R[c]["so"] for c in range(8)])
    new_state = np.ascontiguousarray(so.transpose(0, 1, 2, 4, 3, 5)).astype(np.float32)
    return (y_prompt, y_sample, new_state)
```

```python
import contextlib
import numpy as np
import concourse.bass as bass
import concourse.mybir as mybir
from concourse.bass_utils import run_bass_kernel_spmd

F32 = mybir.dt.float32
BF16 = mybir.dt.bfloat16
U8 = mybir.dt.uint8
AF = mybir.ActivationFunctionType
ALU = mybir.AluOpType

ENGS = ("pe", "act", "dve", "pool", "sp")
NS_DMA = 12
EPS = 1e-6
NT = 512
D = 1024
NEG = -30000.0


class KB:
    def __init__(self, nc, same_engine_sync=True):
        self.nc = nc
        self.prog = {e: [] for e in ENGS}
        self.cnt = {e: 0 for e in ENGS}
        self.seen = {e: {e2: -1 for e2 in ENGS} for e in ENGS}
        self.seen_dma = {e: set() for e in ENGS}
        self.lastw = {}
        self.readers = {}
        self.dmas = []
        self.dma_cnt = {e: 0 for e in ENGS}
        self.last_slot = {}
        self.dma_mark = {e: 0 for e in ENGS}
        self.pending = {}
        self.needed = {e: set() for e in ENGS}
        self.same_engine_sync = same_engine_sync
        self.n_inst = 0

    def snapshot(self, key):
        out = []
        d = self.lastw.get(key)
        if d is not None:
            out.append(d)
        out.extend(self.readers.get(key, {}).values())
        out.extend(self.pending.get(key, []))
        return out

    def _collect(self, r, w):
        deps = []
        if self.pending:
            for k in list(r) + list(w):
                p = self.pending.pop(k, None)
                if p:
                    deps.extend((d, True) for d in p)
        for k in r:
            d = self.lastw.get(k)
            if d is not None:
                deps.append((d, True))
        for k in w:
            d = self.lastw.get(k)
            if d is not None:
                deps.append((d, True))
            for d in self.readers.get(k, {}).values():
                deps.append((d, False))
        return deps

    def _emit_waits(self, eng, deps):
        for d, is_raw in deps:
            if d[0] == "c":
                _, e2, idx = d
                if e2 == eng:
                    if eng == "pe" or not self.same_engine_sync:
                        continue
                if idx <= self.seen[eng][e2]:
                    continue
                self.seen[eng][e2] = idx
                self.needed[e2].add(idx)
                self.prog[eng].append(("wc", e2, idx))
            else:
                did = d[1]
                if did < self.dma_mark[eng] or did in self.seen_dma[eng]:
                    continue
                self.seen_dma[eng].add(did)
                self.prog[eng].append(("wd", did))

    def _update(self, dep, rk, r, w):
        for k in r:
            self.readers.setdefault(k, {})[rk] = dep
        for k in w:
            self.lastw[k] = dep
            self.readers[k] = {}

    def op(self, eng, fn, r=(), w=()):
        deps = self._collect(r, w)
        self._emit_waits(eng, deps)
        idx = self.cnt[eng]
        self.cnt[eng] += 1
        self.prog[eng].append(("i", fn, idx))
        self._update(("c", eng, idx), eng, r, w)
        self.n_inst += 1
        return idx

    def dma(self, q, out_ap, in_ap, r=(), w=()):
        deps = self._collect(r, w)
        n = self.dma_cnt[q]
        self.dma_cnt[q] += 1
        slot = n % NS_DMA
        value = 16 * (n // NS_DMA + 1)
        did = len(self.dmas)
        prev = self.last_slot.get((q, slot))
        if prev is not None:
            deps.append((("d", prev), True))
        self.last_slot[(q, slot)] = did
        self._emit_waits(q, deps)
        self.dmas.append((q, slot, value))
        self.prog[q].append(("dma", out_ap, in_ap, did))
        self._update(("d", did), ("d", did), r, w)
        self.n_inst += 1
        return did

    def barrier(self):
        deps = []
        for e2 in ENGS:
            if e2 != "sp" and self.cnt[e2] > 0:
                deps.append((("c", e2, self.cnt[e2] - 1), True))
        for did in range(self.dma_mark["sp"], len(self.dmas)):
            deps.append((("d", did), True))
        self._emit_waits("sp", deps)
        idx = self.op("sp", lambda e: e.nop())
        nd = len(self.dmas)
        for e in ENGS:
            if e != "sp":
                self._emit_waits(e, [(("c", "sp", idx), True)])
            for e2 in ENGS:
                self.seen[e][e2] = max(self.seen[e][e2], self.cnt[e2] - 1)
            self.dma_mark[e] = nd
            self.seen_dma[e] = set()

    def wait_all_dmas(self, eng):
        self._emit_waits(eng, [(("d", did), True) for did in range(self.dma_mark[eng], len(self.dmas))])

    def emit(self, st):
        nc = self.nc
        sems = {e: st.enter_context(nc.semaphore("s_" + e)) for e in ENGS}
        dsems = {}
        for q in ENGS:
            if self.dma_cnt[q] > 0:
                dsems[q] = [st.enter_context(nc.semaphore("d_%s_%d" % (q, i)))
                            for i in range(min(NS_DMA, self.dma_cnt[q]))]
        rank = {}
        for e in ENGS:
            for i, idx in enumerate(sorted(self.needed[e])):
                rank[(e, idx)] = i + 1
        block = st.enter_context(nc.Block())
        handles = {"pe": "tensor", "act": "scalar", "dve": "vector", "pool": "gpsimd", "sp": "sync"}
        dmas = self.dmas

        def mk(ename):
            entries = self.prog[ename]

            def body(h):
                for ent in entries:
                    t = ent[0]
                    if t == "wc":
                        h.wait_ge(sems[ent[1]], rank[(ent[1], ent[2])])
                    elif t == "wd":
                        q, slot, value = dmas[ent[1]]
                        h.wait_ge(dsems[q][slot], value)
                    elif t == "i":
                        ins = ent[1](h)
                        if (ename, ent[2]) in rank:
                            ins.then_inc(sems[ename], 1)
                    else:
                        q, slot, value = dmas[ent[3]]
                        h.dma_start(out=ent[1], in_=ent[2]).then_inc(dsems[q][slot], 16)
            return body

        for ename in ENGS:
            if self.prog[ename]:
                getattr(block, handles[ename])(mk(ename))


def _panel(W):
    K, M = W.shape
    return np.ascontiguousarray(W.reshape(K // 128, 128, M // 128, 128).transpose(2, 1, 0, 3))


def _fm(v, nt):
    return np.ascontiguousarray(v.reshape(nt, 128).T)


def _consts():
    c = {}
    c["c_ident"] = np.eye(128, dtype=np.float32)
    t = np.arange(128)
    c["c_tri"] = np.stack([(t[:, None] <= t[None, :]), (t[:, None] >= t[None, :])]).astype(np.float32)
    j = t[:, None]
    i = t[None, :]
    nm = np.stack([i >= j, i > j, i <= j, i < j])
    c["c_nmask"] = np.where(nm, 0.0, NEG).astype(np.float32)
    lm = np.zeros((7, 2, 128, 128), np.uint8)
    for l in range(7):
        s = 1 << l
        same = (t[:, None] // (2 * s)) == (t[None, :] // (2 * s))
        L = same & ((t[:, None] // s) % 2 == 1) & ((t[None, :] // s) % 2 == 0)
        lm[l, 0] = L
        lm[l, 1] = L.T
    c["c_lmask"] = lm
    return c


def _shared(inp):
    sh = {}
    f = lambda a: np.asarray(a, dtype=np.float32)
    w_mod = f(inp["w_mod"])
    sh["wmod"] = np.stack([_panel(w_mod[l]) for l in range(2)])
    sh["bmod"] = np.stack([_fm(f(inp["b_mod"])[l], 48) for l in range(2)])
    g = [f(inp[k]) for k in ("g_pre_mix", "g_post_mix", "g_pre_ffn", "g_post_ffn")]
    sh["gains"] = np.stack([np.stack([_fm(gg[l], 8) for gg in g], axis=1) for l in range(2)])
    w_in = f(inp["w_in"])
    cols = np.concatenate([np.arange(0, 2048), np.arange(2064, 4624)])
    gate0 = 4624
    gcols = np.concatenate([np.arange(gate0 + b * 1024 + j * 128, gate0 + b * 1024 + (j + 1) * 128)
                            for j in range(8) for b in range(3)])
    cols = np.concatenate([cols, gcols])
    sh["win"] = np.stack([_panel(w_in[l][:, cols]) for l in range(2)])
    sh["wgate"] = np.stack([np.ascontiguousarray(w_in[l][:, 2048:2064].reshape(8, 128, 16).transpose(1, 0, 2))
                            for l in range(2)])
    dn_conv = f(inp["dn_conv"])
    sh["dnconv"] = np.stack([np.ascontiguousarray(dn_conv[l].T.reshape(12, 128, 3).transpose(1, 0, 2)) for l in range(2)])
    sh["alog"] = np.ascontiguousarray(np.broadcast_to(f(inp["dn_a_log"]).reshape(2, 1, 8), (2, 128, 8)))
    sh["dtb"] = np.ascontiguousarray(np.broadcast_to(f(inp["dn_dt_bias"]).reshape(2, 1, 8), (2, 128, 8)))
    sh["dnng"] = np.ascontiguousarray(f(inp["dn_norm_g"]).reshape(2, 128, 1))
    wdn, wcf, wsc = f(inp["w_dn_out"]), f(inp["w_cf_out"]), f(inp["w_sc_out"])
    sh["wbr"] = np.stack([np.stack([_panel(wdn[l]), _panel(wcf[l]), _panel(wsc[l])], axis=2) for l in range(2)])
    cf_conv = f(inp["cf_conv"])
    sh["cfconv"] = np.stack([np.ascontiguousarray(cf_conv[l].T.reshape(4, 128, 31).transpose(1, 0, 2)) for l in range(2)])
    sh["cfln"] = np.stack([np.stack([_fm(f(inp["cf_ln_g"])[l], 4), _fm(f(inp["cf_ln_b"])[l], 4)], axis=2) for l in range(2)])
    sc_conv = f(inp["sc_conv"])
    sh["scconv"] = np.stack([np.ascontiguousarray(sc_conv[l].T.reshape(4, 128, 3).transpose(1, 0, 2)) for l in range(2)])
    sh["wo"] = np.stack([_panel(f(inp["w_o"])[l]) for l in range(2)])
    wup = f(inp["w_ffn_up"])
    pu = [_panel(wup[l]) for l in range(2)]
    sh["wup"] = np.stack([np.ascontiguousarray(np.stack([p[0:22], p[22:44]], axis=2)) for p in pu])
    ffn_conv = f(inp["ffn_conv"])
    sh["ffnconv"] = np.stack([np.ascontiguousarray(ffn_conv[l].T.reshape(44, 128, 3).transpose(1, 0, 2)) for l in range(2)])
    sh["wdown"] = np.stack([_panel(f(inp["w_ffn_down"])[l]) for l in range(2)])
    sh.update(_consts())
    return sh


SHARED_SHAPES = {
    "wmod": ([2, 48, 128, 8, 128], F32), "bmod": ([2, 128, 48], F32), "gains": ([2, 128, 4, 8], F32),
    "win": ([2, 60, 128, 8, 128], F32), "wgate": ([2, 128, 8, 16], F32), "dnconv": ([2, 128, 12, 3], F32),
    "alog": ([2, 128, 8], F32), "dtb": ([2, 128, 8], F32), "dnng": ([2, 128, 1], F32),
    "wbr": ([2, 8, 128, 3, 4, 128], F32), "cfconv": ([2, 128, 4, 31], F32), "cfln": ([2, 128, 4, 2], F32),
    "scconv": ([2, 128, 4, 3], F32), "wo": ([2, 8, 128, 8, 128], F32), "wup": ([2, 22, 128, 2, 8, 128], F32),
    "ffnconv": ([2, 128, 44, 3], F32), "wdown": ([2, 8, 128, 22, 128], F32),
    "c_ident": ([128, 128], F32), "c_tri": ([2, 128, 128], F32), "c_nmask": ([4, 128, 128], F32),
    "c_lmask": ([7, 2, 128, 128], U8),
}
TS = 4096
TP = 1024
CORE_SHAPES = {"xs": ([D, TS], F32), "xp": ([D, TP], F32), "st0": ([2, 2, 128, 4, 128], F32), "cvec": ([128, 8, 2], F32)}


class Prog:
    def __init__(self, cfg=None):
        self.cfg = cfg or {}
        self.nc = bass.Bass("TRN2", target_bir_lowering=False)
        self.st = contextlib.ExitStack()
        self.kb = KB(self.nc)
        self.free_banks = list(range(8))
        self.dg_i = 0
        self.regions = []
        self.marks = []
        self.NSLOT = self.cfg.get("nslot", 3)
        self.STAGGER = self.cfg.get("stagger", 5)
        self.dg_cache = {}
        self.dg_gen = {}
        self.NDG = 33
        self.wr_i = 0

    def sb(self, name, shape, dt):
        return self.st.enter_context(self.nc.sbuf_tensor(name, list(shape), dt))

    def balloc(self):
        assert self.free_banks, "out of PSUM banks"
        return self.free_banks.pop(0)

    def bfree(self, b):
        assert b not in self.free_banks
        self.free_banks.append(b)

    def claim(self, key, lo, hi):
        seeds = []
        keep = []
        for (k2, lo2, hi2) in self.regions:
            if lo2 < hi and lo < hi2:
                if k2 != key:
                    seeds.extend(self.kb.snapshot(k2))
                if not (lo <= lo2 and hi2 <= hi):
                    keep.append((k2, lo2, hi2))
            else:
                keep.append((k2, lo2, hi2))
        keep.append((key, lo, hi))
        self.regions = keep
        if seeds:
            self.kb.pending.setdefault(key, []).extend(seeds)

    def aview(self, name, shape, dt, key=None):
        n = 1
        for s in shape[1:]:
            n *= s
        nbytes = n * (4 if dt == F32 else (2 if dt == BF16 else 1))
        nbytes = (nbytes + 31) // 32 * 32
        off = self.aoff
        self.aoff += nbytes
        assert self.aoff <= self.ARENA, ("arena overflow", name, self.aoff)
        if key is not None:
            for k_ in (key if isinstance(key, list) else [key]):
                self.claim(k_, off, off + nbytes)
        a = self.arena[:, off // 4:(off + nbytes) // 4]
        if dt != F32:
            a = a.bitcast(dt)
        a = a[:, 0:n]
        if len(shape) == 3:
            a = a.rearrange("p (a b) -> p a b", a=shape[1])
        elif len(shape) == 4:
            a = a.rearrange("p (a b c) -> p a b c", a=shape[1], b=shape[2])
        elif len(shape) == 5:
            a = a.rearrange("p (a b c d) -> p a b c d", a=shape[1], b=shape[2], c=shape[3])
        return a

    def swp(self, gens):
        active = []
        gens = list(gens)
        k = 0
        while k < len(gens) or active:
            if k < len(gens):
                active.append(gens[k])
                k += 1
            for a in list(reversed(active)):
                try:
                    next(a)
                except StopIteration:
                    active.remove(a)

    def mark(self, name):
        self.marks.append((name, self.kb.cnt["pe"]))

    def phase(self):
        if self.cfg.get("barriers", False):
            self.kb.barrier()
        self.aoff = 0
        self.phase_id = getattr(self, "phase_id", 0) + 1

    def mm(self, out, lhsT, rhs, start, stop, r, w):
        self.kb.op("pe", lambda e: e.matmul(out, lhsT, rhs, start=start, stop=stop), r=r, w=w)

    def tr(self, out, in_, ident, r, w):
        self.kb.op("pe", lambda e: e.transpose(out=out, in_=in_, identity=ident), r=r, w=w)

    def act(self, out, in_, func, r, w, scale=None, bias=None):
        kw = {}
        if scale is not None:
            kw["scale"] = scale
        if bias is not None:
            kw["bias"] = bias
        self.kb.op("act", lambda e: e.activation(out=out, in_=in_, func=func, **kw), r=r, w=w)

    def tt(self, out, in0, in1, op, r, w, eng="dve"):
        self.kb.op(eng, lambda e: e.tensor_tensor(out=out, in0=in0, in1=in1, op=op), r=r, w=w)

    def stt(self, out, in0, scalar, in1, op0, op1, r, w):
        self.kb.op("dve", lambda e: e.scalar_tensor_tensor(out=out, in0=in0, scalar=scalar, in1=in1, op0=op0, op1=op1), r=r, w=w)

    def ts(self, out, in0, s1, op0, r, w, s2=None, op1=None, eng="dve"):
        if op1 is None:
            self.kb.op(eng, lambda e: e.tensor_scalar(out=out, in0=in0, scalar1=s1, scalar2=None, op0=op0), r=r, w=w)
        else:
            self.kb.op(eng, lambda e: e.tensor_scalar(out=out, in0=in0, scalar1=s1, scalar2=s2, op0=op0, op1=op1), r=r, w=w)

    def cp(self, out, in_, r, w, eng="dve"):
        self.kb.op(eng, lambda e: e.tensor_copy(out=out, in_=in_), r=r, w=w)

    def ms(self, ap, val, w, eng="pool"):
        self.kb.op(eng, lambda e: e.memset(ap, val), w=w)

    def cpred(self, out, mask, data, r, w):
        self.kb.op("dve", lambda e: e.copy_predicated(out=out, mask=mask, data=data), r=r, w=w)

    def diag(self, colv, rkey, ck=None):
        if ck is not None and ck in self.dg_cache:
            s, gen = self.dg_cache[ck]
            if self.dg_gen.get(s) == gen:
                return self.DG[:, s, :], ("dg", s)
        s = self.dg_i % self.NDG
        self.dg_i += 1
        self.dg_gen[s] = self.dg_i
        if ck is not None:
            self.dg_cache[ck] = (s, self.dg_i)
        out = self.DG[:, s, :]
        key = ("dg", s)
        self.ts(out, self.IDF[:], colv, ALU.mult, r=["consts", rkey], w=[key], eng="dve")
        return out, key

    def wload(self, src, n):
        s = self.wr_i % self.NWR
        self.wr_i += 1
        key = ("wr", s)
        dst = self.WR[:, s, 0:n]
        self.kb.dma("pool", dst, src, w=[key])
        return dst, key

    def rsqrt_(self, out, in_, scale, r, w):
        self.act(out, in_, AF.Ln, r=r, w=w, scale=scale, bias=EPS)
        self.act(out, out, AF.Exp, r=w, w=w, scale=-0.5)

    def build(self):
        nc, kb = self.nc, self.kb
        cfg = self.cfg
        dbg = cfg.get("dbg", False)
        self.din = {}
        for k, (shp, dt) in list(SHARED_SHAPES.items()) + list(CORE_SHAPES.items()):
            self.din[k] = nc.dram_tensor(k, shp, dt, kind="ExternalInput").ap()
        skind = "ExternalOutput" if dbg else "Internal"
        self.dout = {
            "ys": nc.dram_tensor("ys", [D, TS], F32, kind="ExternalOutput").ap(),
            "yp": nc.dram_tensor("yp", [D, TP], F32, kind="ExternalOutput").ap(),
            "so": nc.dram_tensor("so", [4, 2, 2, 128, 4, 128], F32, kind="ExternalOutput").ap(),
        }
        self.scr = {}
        for g, T in (("s", TS), ("p", TP)):
            self.scr["x1" + g] = nc.dram_tensor("x1" + g, [D, T], F32, kind=skind).ap()
            self.scr["x2" + g] = nc.dram_tensor("x2" + g, [D, T], F32, kind=skind).ap()
            self.scr["of" + g] = nc.dram_tensor("of" + g, [512, T], F32, kind=skind).ap()
            self.scr["qk" + g] = nc.dram_tensor("qk" + g, [1536, T], BF16, kind="Internal").ap()
        st = self.st
        with st:
            self.alloc()
            self.prologue()
            nl = cfg.get("layers", 2)
            groups = cfg.get("groups", ("s", "p"))
            for l in range(nl):
                self.layer_params(l)
                for g in groups:
                    self.run_group(l, g, nl)
            kb.barrier()
            kb.wait_all_dmas("sp")
            kb.emit(st)
        return nc

    def alloc(self):
        self.PB = [self.st.enter_context(self.nc.psum_tensor("pb%d" % i, [128, 512], F32)) for i in range(8)]
        sb = self.sb
        self.IDF = sb("idf", [128, 128], F32)
        self.IDB = sb("idb", [128, 128], BF16)
        self.ID4 = sb("id4", [128, 4, 128], BF16)
        self.ONEF = sb("onef", [128, 128], F32)
        self.ONEB = sb("oneb", [128, 128], BF16)
        self.TRI = sb("tri", [128, 2, 128], F32)
        self.NM4 = sb("nm4", [128, 4, 4, 128], BF16)
        self.LM4 = sb("lm4", [128, 7, 2, 4, 128], U8)
        self.LMr = sb("lmr", [128, 7, 2, 128], U8)
        self.NMr = sb("nmr", [128, 4, 128], F32)
        self.CV = sb("cv", [128, 8, 2], F32)
        self.SC = sb("sc", [128, 8, 2], BF16)
        self.MOD = sb("mod", [128, 2, 48, 2], F32)
        self.BMOD = sb("bmodsb", [128, 2, 48], F32)
        self.MODC = sb("modc", [128, 2, 6, 8], F32)
        self.GAINS = sb("gains_sb", [128, 4, 8], F32)
        self.WG = sb("wg", [128, 8, 16], BF16)
        self.DNC = sb("dnc", [128, 12, 3], F32)
        self.NEXPA = sb("nexpa", [128, 8], F32)
        self.DTB = sb("dtb_sb", [128, 8], F32)
        self.DNNG = sb("dnng_sb", [128, 1], F32)
        self.CFC = sb("cfc", [128, 4, 31], F32)
        self.CFLN = sb("cfln_sb", [128, 4, 2], F32)
        self.SCC = sb("scc", [128, 4, 3], F32)
        self.FFC = sb("ffc", [128, 44, 3], F32)
        self.X = sb("X", [128, 8, 640], F32)
        self.H = sb("H", [128, 8, 640], BF16)
        self.OD = sb("OD", [128, 4, 512], BF16)
        self.NWR = 5
        self.WR = sb("WR", [128, self.NWR, 4096], BF16)
        self.DG = sb("DG", [128, self.NDG, 128], BF16)
        self.S = sb("S", [128, 4, 128], F32)
        self.SBF = sb("SBF", [128, 4, 128], BF16)
        self.ARENA = self.cfg.get("arena_kb", 97) * 1024
        self.arena = sb("arena", [128, self.ARENA // 4], F32)
        self.aoff = 0

    def prologue(self):
        kb, din = self.kb, self.din
        kb.dma("sp", self.IDF[:], din["c_ident"], w=["consts"])
        kb.dma("pool", self.IDB[:], din["c_ident"], w=["consts"])
        kb.dma("sp", self.TRI[:], din["c_tri"].rearrange("d t i -> t d i"), w=["consts"])
        kb.dma("sp", self.NMr[:], din["c_nmask"].rearrange("k j i -> j k i"), w=["nmr"])
        kb.dma("sp", self.LMr[:], din["c_lmask"].rearrange("l k i j -> i l k j"), w=["lmr"])
        kb.dma("sp", self.CV[:], din["cvec"], w=["cv"])
        kb.dma("sp", self.BMOD[:], din["bmod"].rearrange("l p m -> p l m"), w=["bmod"])
        self.ms(self.ONEF[:], 1.0, w=["consts"])
        self.ms(self.ONEB[:], 1.0, w=["consts"])
        for h in range(4):
            self.cp(self.ID4[:, h, :], self.IDF[:], r=["consts"], w=["consts"], eng="pool")
            self.cp(self.NM4[:, :, h, :], self.NMr[:], r=["nmr"], w=["consts"], eng="pool")
            self.cp(self.LM4[:, :, :, h, :], self.LMr[:], r=["lmr"], w=["consts"], eng="pool")
        self.act(self.SC[:], self.CV[:], AF.Silu, r=["cv"], w=["sc"])
        for l in range(2):
            for g in range(12):
                src = din["wmod"][l, 4 * g:4 * g + 4].rearrange("m p k c -> p m (k c)")
                wv, wk = self.wload_3d(src, 4, 1024)
                for m in range(4):
                    mt = 4 * g + m
                    b = self.balloc()
                    for kt in range(8):
                        self.mm(self.PB[b][:, 0:2], wv[:, m, kt * 128:(kt + 1) * 128], self.SC[:, kt, :],
                                start=(kt == 0), stop=(kt == 7), r=[wk, "sc"], w=[("pb", b)])
                    self.ts(self.MOD[:, l, mt, :], self.PB[b][:, 0:2], self.BMOD[:, l, mt:mt + 1], ALU.add,
                            r=[("pb", b), "bmod"], w=["mod"])
                    self.bfree(b)

    def wload_3d(self, src, m, n):
        s = self.wr_i % self.NWR
        self.wr_i += 1
        key = ("wr", s)
        dst = self.WR[:, s, 0:m * n].rearrange("p (m n) -> p m n", m=m)
        self.kb.dma("pool", dst, src, w=[key])
        return dst, key

    def layer_params(self, l):
        kb, din = self.kb, self.din
        kb.barrier()
        kb.dma("sp", self.GAINS[:], din["gains"][l], w=["gains"])
        kb.dma("pool", self.WG[:], din["wgate"][l], w=["wg"])
        kb.dma("sp", self.DNC[:], din["dnconv"][l], w=["dnc"])
        kb.dma("sp", self.NEXPA[:], din["alog"][l], w=["nexpa"])
        kb.dma("sp", self.DTB[:], din["dtb"][l], w=["dtb"])
        kb.dma("sp", self.DNNG[:], din["dnng"][l], w=["dnng"])
        kb.dma("sp", self.CFC[:], din["cfconv"][l], w=["cfc"])
        kb.dma("sp", self.CFLN[:], din["cfln"][l], w=["cfln"])
        kb.dma("sp", self.SCC[:], din["scconv"][l], w=["scc"])
        kb.dma("sp", self.FFC[:], din["ffnconv"][l], w=["ffc"])
        self.act(self.NEXPA[:], self.NEXPA[:], AF.Exp, r=["nexpa"], w=["nexpa"])
        self.ts(self.NEXPA[:], self.NEXPA[:], -1.0, ALU.mult, r=["nexpa"], w=["nexpa"])
        for g in range(2):
            M = self.MOD[:, l, :, g]
            self.stt(self.MODC[:, g, 0, :], M[:, 8:16], 1.0, self.GAINS[:, 0, :], ALU.add, ALU.mult, r=["mod", "gains"], w=["modc"])
            self.cp(self.MODC[:, g, 1, :], M[:, 0:8], r=["mod"], w=["modc"])
            self.tt(self.MODC[:, g, 2, :], M[:, 16:24], self.GAINS[:, 1, :], ALU.mult, r=["mod", "gains"], w=["modc"])
            self.stt(self.MODC[:, g, 3, :], M[:, 32:40], 1.0, self.GAINS[:, 2, :], ALU.add, ALU.mult, r=["mod", "gains"], w=["modc"])
            self.cp(self.MODC[:, g, 4, :], M[:, 24:32], r=["mod"], w=["modc"])
            self.tt(self.MODC[:, g, 5, :], M[:, 40:48], self.GAINS[:, 3, :], ALU.mult, r=["mod", "gains"], w=["modc"])

    def cut(self, n):
        if self.cfg.get("cut") == n:
            self.stop = True
        return getattr(self, "stop", False)

    def geom(self, g):
        if g == "s":
            return dict(T=TS, ntile=TS // NT, seg=64, HL=64, gi=0, nsg=4)
        return dict(T=TP, ntile=TP // NT, seg=256, HL=0, gi=1, nsg=1)

    def run_group(self, l, g, nl):
        ge = self.geom(g)
        self.ge = ge
        self.g = g
        self.l = l
        xin = self.din["x" + g] if l == 0 else self.scr["x2" + g]
        xmid = self.scr["x1" + g]
        xout = self.dout["y" + g] if l == nl - 1 else self.scr["x2" + g]
        stages = self.cfg.get("stages", (1, 2, 3))
        if 1 in stages:
            for i in range(ge["ntile"]):
                if not getattr(self, "stop", False):
                    self.loop_dn(i, 0, xin, None)
        if 2 in stages:
            for i in reversed(range(ge["ntile"])):
                self.loop_dn(i, 1, xin, xmid)
        if 3 in stages:
            for i in range(ge["ntile"]):
                self.loop_ffn(i, xmid, xout)

    def load_x(self, i, xsrc, halo):
        ge = self.ge
        HL = ge["HL"] if halo else 0
        t0 = i * NT
        lo = max(0, t0 - HL)
        hi = min(ge["T"], t0 + NT + HL)
        c0 = lo - (t0 - HL)
        n = hi - lo
        XW = NT + 2 * HL
        src = xsrc.rearrange("(f p) t -> p f t", p=128)[:, :, lo:hi]
        rk = [("dram", id(xsrc), j) for j in range(lo // NT, (hi - 1) // NT + 1)]
        self.kb.dma("sp", self.X[:, :, c0:c0 + n], src, r=rk, w=["X"])
        if c0 > 0:
            self.ms(self.X[:, :, 0:c0], 0.0, w=["X"])
        if c0 + n < XW:
            self.ms(self.X[:, :, c0 + n:XW], 0.0, w=["X"])
        return XW, c0, c0 + n

    def normmod(self, XW, vlo, vhi, ia, ib):
        gi = self.ge["gi"]
        SQ = [self.aview("nm_sq%d" % k, [128, 640], BF16, key=("nm_sq", k)) for k in range(2)]
        RINV = self.aview("nm_rinv", [128, 640], F32, key="nm_rinv")
        TMP = [self.aview("nm_tmp%d" % k, [128, 640], F32, key=("nm_tmp", k)) for k in range(2)]
        ba = self.balloc()
        bb = self.balloc() if XW > 512 else None
        n1 = min(512, XW)
        for ft in range(8):
            sq = SQ[ft % 2]
            k = ("nm_sq", ft % 2)
            self.act(sq[:, 0:XW], self.X[:, ft, 0:XW], AF.Square, r=["X"], w=[k])
            self.mm(self.PB[ba][:, 0:n1], self.ONEB[:], sq[:, 0:n1], start=(ft == 0), stop=(ft == 7), r=["consts", k], w=[("pb", ba)])
            if bb is not None:
                self.mm(self.PB[bb][:, 0:XW - 512], self.ONEB[:], sq[:, 512:XW], start=(ft == 0), stop=(ft == 7), r=["consts", k], w=[("pb", bb)])
        if self.cut(0.1):
            self.bfree(ba)
            return
        self.act(RINV[:, 0:n1], self.PB[ba][:, 0:n1], AF.Ln, r=[("pb", ba)], w=["nm_rinv"], scale=1.0 / D, bias=EPS)
        self.bfree(ba)
        if bb is not None:
            self.act(RINV[:, 512:XW], self.PB[bb][:, 0:XW - 512], AF.Ln, r=[("pb", bb)], w=["nm_rinv"], scale=1.0 / D, bias=EPS)
            self.bfree(bb)
        if self.cut(0.2):
            return
        self.act(RINV[:, 0:XW], RINV[:, 0:XW], AF.Exp, r=["nm_rinv"], w=["nm_rinv"], scale=-0.5)
        if self.cut(0.3):
            return
        for ft in range(8):
            tmp = TMP[ft % 2]
            k = ("nm_tmp", ft % 2)
            self.stt(tmp[:, 0:XW], self.X[:, ft, 0:XW], self.MODC[:, gi, ia, ft:ft + 1], RINV[:, 0:XW], ALU.mult, ALU.mult,
                     r=["X", "modc", "nm_rinv"], w=[k])
            if self.cfg.get("nm_mode") == 1:
                continue
            if self.cfg.get("nm_mode") == 2 or (self.cfg.get("nm_mode") == 3 and ft > 0) or (self.cfg.get("nm_mode") == 4 and ft > 1) or (self.cfg.get("nm_mode") == 5 and ft != 1):
                self.act(self.H[:, ft, 0:XW], tmp[:, 0:XW], AF.Copy, r=[k, "modc"], w=["H"])
                continue
            self.act(self.H[:, ft, 0:XW], tmp[:, 0:XW], AF.Identity, r=[k, "modc"], w=["H"], bias=self.MODC[:, gi, ib, ft:ft + 1])
        if vlo > 0:
            self.ms(self.H[:, :, 0:vlo], 0.0, w=["H"])
        if vhi < XW:
            self.ms(self.H[:, :, vhi:XW], 0.0, w=["H"])

    def proj(self, wv, wk, c0, n, nk=8, rhs=None, rkey="H"):
        b = self.balloc()
        for kt in range(nk):
            r_ = self.H[:, kt, c0:c0 + n] if rhs is None else rhs(kt)
            self.mm(self.PB[b][:, 0:n], wv[:, kt * 128:(kt + 1) * 128], r_, start=(kt == 0), stop=(kt == nk - 1),
                    r=[wk, rkey], w=[("pb", b)])
        return b

    def hconv(self, PC, pkey, gi2, ntap, colv_fn, ckey, cid=None):
        ge = self.ge
        nsg, seg = ge["nsg"], ge["seg"]
        pad = ntap // 2
        SP = seg + 2 * pad
        L = nsg * SP - 2 * pad
        flat = PC[:, gi2, :, :].rearrange("p s t -> p (s t)")
        b = self.balloc()
        for tap in range(ntap):
            dg, dk = self.diag(colv_fn(tap), ckey, ck=(self.phase_id, cid, tap))
            self.mm(self.PB[b][:, 0:L], dg, flat[:, tap:tap + L], start=(tap == 0), stop=(tap == ntap - 1), r=[dk, pkey], w=[("pb", b)])
        valid = self.PB[b][:, 0:nsg * SP].rearrange("p (s t) -> p s t", s=nsg)[:, :, 0:seg]
        return b, valid

    def loop_dn(self, i, d, xin, xmid):
        ge, l, kb, din = self.ge, self.l, self.kb, self.din
        HL, seg, nsg = ge["HL"], ge["seg"], ge["nsg"]
        self.phase()
        QKV = self.aview("QKV", [128, 12, 512], BF16, key=[("qkv", m) for m in range(12)])
        OFT = self.aview("OFT", [128, 4, 512], F32, key="OFT")
        OB = self.aview("OB", [128, 4, 512], F32, key="OB") if d == 1 else None
        self.ZS = self.aview("ZS", [128, 4, 512], BF16, key=[("zs", m) for m in range(4)]) if d == 1 else None
        mark = self.aoff
        t0 = i * NT
        qsc = self.scr["qk" + self.g].rearrange("(m p) t -> p m t", p=128)[:, :, t0:t0 + NT]
        ofs = self.scr["of" + self.g].rearrange("(h p) t -> p h t", p=128)[:, :, t0:t0 + NT]
        XW, vlo, vhi = self.load_x(i, xin, halo=(d == 1))
        self.mark("dn_normmod")
        HLd = HL if d == 1 else 0
        if d == 1:
            kb.dma("sp", QKV[:], qsc, r=[("dram", "qk", self.g, i)], w=[("qkv", m) for m in range(12)])
            kb.dma("sp", OFT[:], ofs, r=[("dram", "of", self.g, i)], w=["OFT"])
        if getattr(self, "stop", False) or self.cut(0):
            return
        self.normmod(XW, vlo, vhi, 0, 1)
        if self.cut(1):
            return
        if d == 0:
            self.qkv_stage(QKV, HLd)
            kb.dma("sp", qsc, QKV[:], r=[("qkv", m) for m in range(12)], w=[("dram", "qk", self.g, i)])
        if self.cut(2):
            return
        if self.cfg.get("barriers", False):
            kb.barrier()
        self.aoff = mark
        self.dn_bufs(d)
        self.mark("chunks")
        self.dn_run(i, d, QKV, OFT, OB, HLd)
        if getattr(self, "stop", False):
            return
        if d == 0:
            kb.dma("sp", ofs, OFT[:], r=["OFT"], w=[("dram", "of", self.g, i)])
            return
        self.onorm(OB)
        self.mixer_rest(i, xmid, XW)

    def qkv_stage(self, QKV, HLd):
        ge, l, din = self.ge, self.l, self.din
        seg, nsg = ge["seg"], ge["nsg"]
        SP3 = seg + 2
        PC = [self.aview("pc%d" % k, [128, 2, nsg, SP3], BF16, key=("pc", k)) for k in range(2)]
        self.mark("qkv")
        S32 = [self.aview("s32_%d" % k, [128, 512], F32, key=("s32", k)) for k in range(8)]
        SQ = [self.aview("sq_%d" % k, [128, 512], BF16, key=("sq", k)) for k in range(4)]
        RN = [self.aview("rn_%d" % k, [128, 512], F32, key=("rn", k)) for k in range(4)]
        for k in range(2):
            self.ms(PC[k][:], 0.0, w=[("pc", k)])
        wst = {}

        def item(mt):
            g3, m = mt // 4, mt % 4
            if m == 0:
                src = din["win"][l, 4 * g3:4 * g3 + 4].rearrange("m p k c -> p m (k c)")
                wst["w"] = self.wload_3d(src, 4, 1024)
            wv, wk = wst["w"]
            for tap in range(3):
                self.diag(self.DNC[:, mt, tap:tap + 1], "dnc", ck=(self.phase_id, ("dn", mt), tap))
            b = self.proj(wv[:, m, :], wk, HLd, NT)
            pc, pk = PC[mt % 2], ("pc", mt % 2)
            for gi2 in range(2):
                self.cp(pc[:, gi2, :, 1:1 + seg], self.PB[b][:, gi2 * 256:(gi2 + 1) * 256].rearrange("p (s t) -> p s t", s=nsg),
                        r=[("pb", b)], w=[pk])
            self.bfree(b)
            yield
            for gi2 in range(2):
                b2, valid = self.hconv(pc, pk, gi2, 3, lambda tap, mt=mt: self.DNC[:, mt, tap:tap + 1], "dnc", cid=("dn", mt))
                if g3 == 2:
                    outv = QKV[:, mt, gi2 * 256:(gi2 + 1) * 256].rearrange("p (s t) -> p s t", s=nsg)
                    self.act(outv, valid, AF.Silu, r=[("pb", b2)], w=[("qkv", mt)])
                else:
                    outv = S32[mt][:, gi2 * 256:(gi2 + 1) * 256].rearrange("p (s t) -> p s t", s=nsg)
                    self.act(outv, valid, AF.Silu, r=[("pb", b2)], w=[("s32", mt)])
                self.bfree(b2)

        self.swp([item(mt) for mt in range(12)])
        def norm_item(mt):
            s32, sk = S32[mt], ("s32", mt)
            sq, qk_ = SQ[mt % 4], ("sq", mt % 4)
            rn, rk = RN[mt % 4], ("rn", mt % 4)
            self.tt(sq[:], s32[:], s32[:], ALU.mult, r=[sk], w=[qk_])
            b3 = self.balloc()
            self.mm(self.PB[b3][:], self.ONEB[:], sq[:], start=True, stop=True, r=["consts", qk_], w=[("pb", b3)])
            self.act(rn[:], self.PB[b3][:], AF.Ln, r=[("pb", b3)], w=[rk], scale=1.0, bias=EPS)
            self.bfree(b3)
            yield
            self.act(rn[:], rn[:], AF.Exp, r=[rk], w=[rk], scale=-0.5)
            yield
            scale = (128.0 ** -0.5) if mt < 4 else 1.0
            self.stt(QKV[:, mt, :], s32[:], scale, rn[:], ALU.mult, ALU.mult, r=[sk, rk], w=[("qkv", mt)])

        self.swp([norm_item(mt) for mt in range(8)])

    def dn_bufs(self, d):
        av = self.aview
        self.Bsh = {}
        for nm in ("R1", "R2", "E"):
            self.Bsh[nm] = av("dnsh_" + nm, [128, 4, 128], F32, key=("dnsh", nm))
        for nm in ("DT", "MD"):
            self.Bsh[nm] = av("dnsh_" + nm, [128, 4, 128], BF16, key=("dnsh", nm))
        for nm in ("TG1", "TG2", "TS1"):
            self.Bsh[nm] = av("dnsh_" + nm, [128, 4], F32, key=("dnsh", nm))
        self.Bs = []
        for sl in range(self.NSLOT):
            B = {}
            B["USB"] = av("dn%d_USB" % sl, [128, 4, 128], F32, key=("dn", sl, "USB"))
            for nm in ("MN", "AN", "T", "U", "YS", "YS2", "QKT", "KBt", "KD", "BV", "WT", "QD", "VN"):
                B[nm] = av("dn%d_%s" % (sl, nm), [128, 4, 128], BF16, key=("dn", sl, nm))
            for nm in ("GG", "BB", "LB", "GAMJ", "NGAMJ", "GTOT", "LGT", "EKB", "EKD"):
                B[nm] = av("dn%d_%s" % (sl, nm), [128, 4], F32, key=("dn", sl, nm))
            self.Bs.append(B)

    def dn_run(self, i, d, QKV, OFT, OB, HLd):
        order = list(range(4)) if d == 0 else list(reversed(range(4)))
        active = []
        nxt = 0
        rounds = 0
        while active or nxt < len(order):
            if nxt < len(order) and len(active) < self.NSLOT and (not active or rounds % self.STAGGER == 0):
                c = order[nxt]
                active.append(self.dn_chunk(i, c, d, QKV, OFT, OB, HLd, nxt % self.NSLOT))
                nxt += 1
            for gen in list(active):
                try:
                    next(gen)
                except StopIteration:
                    active.remove(gen)
            rounds += 1
            if getattr(self, "stop", False):
                return

    def dn_chunk(self, i, c, d, QKV, OFT, OB, HLd, sl):
        kb, B, SH, PB, l, ge, g = self.kb, self.Bs[sl], self.Bsh, self.PB, self.l, self.ge, self.g
        cs = slice(c * 128, (c + 1) * 128)
        hs = slice(HLd + c * 128, HLd + (c + 1) * 128)
        gc = i * 4 + c
        R = lambda *names: [("dn", sl, n) for n in names]
        RS = lambda *names: [("dnsh", n) for n in names]
        flat = lambda ap: ap.rearrange("p h i -> p (h i)")
        bc_t = lambda ap: ap.unsqueeze(1).to_broadcast([128, 4, 128])
        bc_h = lambda ap: ap.unsqueeze(2).to_broadcast([128, 4, 128])
        bg = self.balloc()
        for kt in range(8):
            self.mm(PB[bg][:, 0:16], self.H[:, kt, hs], self.WG[:, kt, :], start=(kt == 0), stop=(kt == 7), r=["H", "wg"], w=[("pb", bg)])
        a_ps = PB[bg][:, 4 * d:4 * d + 4]
        b_ps = PB[bg][:, 8 + 4 * d:12 + 4 * d]
        self.tt(SH["TG1"][:], a_ps, self.DTB[:, 4 * d:4 * d + 4], ALU.add, r=[("pb", bg), "dtb"], w=RS("TG1"))
        self.act(SH["TG2"][:], SH["TG1"][:], AF.Exp, r=RS("TG1"), w=RS("TG2"))
        self.act(SH["TG2"][:], SH["TG2"][:], AF.Ln, r=RS("TG2"), w=RS("TG2"), bias=1.0)
        self.tt(B["GG"][:], SH["TG2"][:], self.NEXPA[:, 4 * d:4 * d + 4], ALU.mult, r=RS("TG2") + ["nexpa"], w=R("GG"))
        self.act(SH["TS1"][:], b_ps, AF.Exp, r=[("pb", bg)], w=RS("TS1"), scale=-1.0)
        self.bfree(bg)
        self.act(SH["TS1"][:], SH["TS1"][:], AF.Ln, r=RS("TS1"), w=RS("TS1"), bias=1.0)
        self.ts(B["LB"][:], SH["TS1"][:], -1.0, ALU.mult, r=RS("TS1"), w=R("LB"))
        self.act(B["BB"][:], SH["TS1"][:], AF.Exp, r=RS("TS1"), w=R("BB"), scale=-1.0)
        yield
        tri = self.TRI[:, d, :]
        bgam = self.balloc()
        self.mm(PB[bgam][:, 0:4], tri, B["GG"][:], start=True, stop=True, r=["consts"] + R("GG"), w=[("pb", bgam)])
        self.mm(PB[bgam][:, 4:8], self.ONEF[:], B["GG"][:], start=True, stop=True, r=["consts"] + R("GG"), w=[("pb", bgam)])
        self.cp(B["GAMJ"][:], PB[bgam][:, 0:4], r=[("pb", bgam)], w=R("GAMJ"))
        self.cp(B["LGT"][:], PB[bgam][:, 4:8], r=[("pb", bgam)], w=R("LGT"))
        self.act(B["GTOT"][:], PB[bgam][:, 4:8], AF.Exp, r=[("pb", bgam)], w=R("GTOT"))
        self.ts(B["NGAMJ"][:], PB[bgam][:, 0:4], -1.0, ALU.mult, r=[("pb", bgam)], w=R("NGAMJ"))
        self.bfree(bgam)
        self.tt(SH["TG1"][:], B["GAMJ"][:], B["LB"][:], ALU.add, r=R("GAMJ", "LB"), w=RS("TG1"))
        self.act(B["EKB"][:], SH["TG1"][:], AF.Exp, r=RS("TG1"), w=R("EKB"))
        self.tt(SH["TG2"][:], B["LGT"][:], B["GAMJ"][:], ALU.subtract, r=R("LGT", "GAMJ"), w=RS("TG2"))
        self.act(B["EKD"][:], SH["TG2"][:], AF.Exp, r=RS("TG2"), w=R("EKD"))
        yield
        self.tt(SH["R1"][:], bc_t(tri), bc_h(B["GG"][:]), ALU.mult, r=["consts"] + R("GG"), w=RS("R1"), eng="pool")
        self.tt(SH["R2"][:], bc_t(self.IDF[:]), bc_h(B["LB"][:]), ALU.mult, r=["consts"] + R("LB"), w=RS("R2"), eng="pool")
        self.tt(SH["R2"][:], SH["R2"][:], SH["R1"][:], ALU.add, r=RS("R1", "R2"), w=RS("R2"), eng="pool")
        p0, p1, p2 = self.balloc(), self.balloc(), self.balloc()
        self.mm(PB[p0][:], self.ONEF[:], flat(SH["R1"][:]), start=True, stop=True, r=["consts"] + RS("R1"), w=[("pb", p0)])
        self.mm(PB[p1][:], self.ONEF[:], flat(SH["R1"][:]), start=True, stop=False, r=["consts"] + RS("R1"), w=[("pb", p1)])
        self.mm(PB[p1][:], self.IDB[:], flat(self.NM4[:, 2 * d, :, :]), start=False, stop=True, r=["consts"], w=[("pb", p1)])
        self.mm(PB[p2][:], self.ONEF[:], flat(SH["R2"][:]), start=True, stop=False, r=["consts"] + RS("R2"), w=[("pb", p2)])
        self.mm(PB[p2][:], self.IDB[:], flat(self.NM4[:, 2 * d + 1, :, :]), start=False, stop=True, r=["consts"], w=[("pb", p2)])
        self.act(flat(SH["E"][:]), PB[p0][:], AF.Exp, r=[("pb", p0)], w=RS("E"))
        self.bfree(p0)
        for h in range(4):
            self.act(SH["DT"][:, h, :], PB[p1][:, h * 128:(h + 1) * 128], AF.Exp, r=[("pb", p1)] + R("NGAMJ"), w=RS("DT"), bias=B["NGAMJ"][:, h:h + 1])
            self.act(SH["MD"][:, h, :], PB[p2][:, h * 128:(h + 1) * 128], AF.Exp, r=[("pb", p2)] + R("NGAMJ"), w=RS("MD"), bias=B["NGAMJ"][:, h:h + 1])
        self.bfree(p1)
        self.bfree(p2)
        self.tt(B["QD"][:], QKV[:, 0:4, cs], SH["E"][:], ALU.mult, r=[("qkv", h) for h in range(4)] + RS("E"), w=R("QD"), eng="pool")
        pkk, pqk = self.balloc(), self.balloc()
        for h in range(4):
            kh = QKV[:, 4 + h, cs]
            qh = QKV[:, h, cs]
            self.mm(PB[pkk][:, h * 128:(h + 1) * 128], kh, kh, start=True, stop=True, r=[("qkv", 4 + h)], w=[("pb", pkk)])
            self.mm(PB[pqk][:, h * 128:(h + 1) * 128], kh, qh, start=True, stop=True, r=[("qkv", 4 + h), ("qkv", h)], w=[("pb", pqk)])
        self.stt(flat(B["MN"][:]), PB[pkk][:], -1.0, flat(SH["MD"][:]), ALU.mult, ALU.mult, r=[("pb", pkk)] + RS("MD"), w=R("MN"))
        self.tt(flat(B["QKT"][:]), PB[pqk][:], flat(SH["DT"][:]), ALU.mult, r=[("pb", pqk)] + RS("DT"), w=R("QKT"))
        self.bfree(pkk)
        self.bfree(pqk)
        yield
        ptr = self.balloc()
        ptv = PB[ptr][:].bitcast(BF16)
        for h in range(4):
            self.tr(ptv[:, h * 128:(h + 1) * 128], B["MN"][:, h, :], self.IDB[:], r=["consts"] + R("MN"), w=[("pb", ptr)])
        self.act(flat(B["AN"][:]), ptv[:, 0:512], AF.Copy, r=[("pb", ptr)], w=R("AN"))
        self.bfree(ptr)
        mT, mU = (0, 1) if d == 0 else (1, 0)
        self.cp(flat(B["T"][:]), flat(self.ID4[:]), r=["consts"], w=R("T"), eng="pool")
        self.cp(flat(B["U"][:]), flat(self.ID4[:]), r=["consts"], w=R("U"), eng="pool")
        self.cpred(flat(B["U"][:]), flat(self.LM4[:, 0, mU, :, :]), flat(B["MN"][:]), r=["consts"] + R("MN", "U"), w=R("U"))
        yield
        self.cpred(flat(B["T"][:]), flat(self.LM4[:, 0, mT, :, :]), flat(B["AN"][:]), r=["consts"] + R("AN", "T"), w=R("T"))
        for lev in range(1, 7):
            lastlev = (lev == 6)
            py = self.balloc()
            py2 = None if lastlev else self.balloc()
            for h in range(4):
                hsl = slice(h * 128, (h + 1) * 128)
                self.mm(PB[py][:, hsl], B["AN"][:, h, :], B["U"][:, h, :], start=True, stop=True, r=R("AN", "U"), w=[("pb", py)])
            if not lastlev:
                for h in range(4):
                    hsl = slice(h * 128, (h + 1) * 128)
                    self.mm(PB[py2][:, hsl], B["MN"][:, h, :], B["T"][:, h, :], start=True, stop=True, r=R("MN", "T"), w=[("pb", py2)])
            self.act(flat(B["YS"][:]), PB[py][:], AF.Copy, r=[("pb", py)], w=R("YS"))
            self.bfree(py)
            if not lastlev:
                self.act(flat(B["YS2"][:]), PB[py2][:], AF.Copy, r=[("pb", py2)], w=R("YS2"))
                self.bfree(py2)
            yield
            pz = self.balloc()
            pz2 = None if lastlev else self.balloc()
            for h in range(4):
                hsl = slice(h * 128, (h + 1) * 128)
                self.mm(PB[pz][:, hsl], B["T"][:, h, :], B["YS"][:, h, :], start=True, stop=True, r=R("T", "YS"), w=[("pb", pz)])
            if not lastlev:
                for h in range(4):
                    hsl = slice(h * 128, (h + 1) * 128)
                    self.mm(PB[pz2][:, hsl], B["U"][:, h, :], B["YS2"][:, h, :], start=True, stop=True, r=R("U", "YS2"), w=[("pb", pz2)])
            self.cpred(flat(B["U"][:]), flat(self.LM4[:, lev, mU, :, :]), PB[pz][:], r=["consts", ("pb", pz)] + R("U"), w=R("U"))
            self.bfree(pz)
            if not lastlev:
                self.cpred(flat(B["T"][:]), flat(self.LM4[:, lev, mT, :, :]), PB[pz2][:], r=["consts", ("pb", pz2)] + R("T"), w=R("T"))
                self.bfree(pz2)
            yield
        pk_, pv_ = self.balloc(), self.balloc()
        pkv, pvv = PB[pk_][:].bitcast(BF16), PB[pv_][:].bitcast(BF16)
        for h in range(4):
            self.tr(pkv[:, h * 128:(h + 1) * 128], QKV[:, 4 + h, cs], self.IDB[:], r=["consts", ("qkv", 4 + h)], w=[("pb", pk_)])
            self.tr(pvv[:, h * 128:(h + 1) * 128], QKV[:, 8 + h, cs], self.IDB[:], r=["consts", ("qkv", 8 + h)], w=[("pb", pv_)])
        pk3 = pkv[:, 0:512].rearrange("p (h x) -> p h x", h=4)
        pv3 = pvv[:, 0:512].rearrange("p (h x) -> p h x", h=4)
        self.tt(B["KBt"][:], pk3, bc_h(B["EKB"][:]), ALU.mult, r=[("pb", pk_)] + R("EKB"), w=R("KBt"))
        self.tt(B["KD"][:], pk3, bc_h(B["EKD"][:]), ALU.mult, r=[("pb", pk_)] + R("EKD"), w=R("KD"))
        self.tt(B["BV"][:], pv3, bc_h(B["BB"][:]), ALU.mult, r=[("pb", pv_)] + R("BB"), w=R("BV"))
        self.bfree(pk_)
        self.bfree(pv_)
        yield
        pu, pw = self.balloc(), self.balloc()
        for h in range(4):
            hsl = slice(h * 128, (h + 1) * 128)
            self.mm(PB[pu][:, hsl], B["U"][:, h, :], B["BV"][:, h, :], start=True, stop=True, r=R("U", "BV"), w=[("pb", pu)])
        for h in range(4):
            hsl = slice(h * 128, (h + 1) * 128)
            self.mm(PB[pw][:, hsl], B["KBt"][:, h, :], B["U"][:, h, :], start=True, stop=True, r=R("KBt", "U"), w=[("pb", pw)])
        self.act(flat(B["USB"][:]), PB[pu][:], AF.Copy, r=[("pb", pu)], w=R("USB"))
        self.cp(flat(B["WT"][:]), PB[pw][:], r=[("pb", pw)], w=R("WT"))
        self.bfree(pu)
        self.bfree(pw)
        yield
        if g == "s":
            first = (gc == 0) if d == 0 else (gc == ge["T"] // 128 - 1)
            if first:
                kb.dma("sp", self.S[:], self.din["st0"][l, d], w=["S"])
                self.act(flat(self.SBF[:]), flat(self.S[:]), AF.Copy, r=["S"], w=["SBF"])
        else:
            first = (gc % 2 == 0) if d == 0 else (gc % 2 == 1)
            if first:
                self.ms(self.S[:], 0.0, w=["S"])
                self.ms(self.SBF[:], 0.0, w=["SBF"])
        pa = self.balloc()
        for h in range(4):
            hsl = slice(h * 128, (h + 1) * 128)
            self.mm(PB[pa][:, hsl], B["WT"][:, h, :], self.SBF[:, h, :], start=True, stop=True, r=R("WT") + ["SBF"], w=[("pb", pa)])
        self.tt(flat(B["VN"][:]), flat(B["USB"][:]), PB[pa][:], ALU.subtract, r=[("pb", pa)] + R("USB"), w=R("VN"))
        self.bfree(pa)
        po, ps_ = self.balloc(), self.balloc()
        for h in range(4):
            hsl = slice(h * 128, (h + 1) * 128)
            self.mm(PB[ps_][:, hsl], B["KD"][:, h, :], B["VN"][:, h, :], start=True, stop=True, r=R("KD", "VN"), w=[("pb", ps_)])
        for h in range(4):
            hsl = slice(h * 128, (h + 1) * 128)
            self.mm(PB[po][:, hsl], self.SBF[:, h, :], B["QD"][:, h, :], start=True, stop=False, r=R("QD") + ["SBF"], w=[("pb", po)])
            self.mm(PB[po][:, hsl], B["VN"][:, h, :], B["QKT"][:, h, :], start=False, stop=True, r=R("VN", "QKT"), w=[("pb", po)])
        for h in range(4):
            hsl = slice(h * 128, (h + 1) * 128)
            self.stt(self.S[:, h, :], self.S[:, h, :], B["GTOT"][:, h:h + 1], PB[ps_][:, hsl], ALU.mult, ALU.add,
                     r=["S", ("pb", ps_)] + R("GTOT"), w=["S"])
        self.bfree(ps_)
        self.act(flat(self.SBF[:]), flat(self.S[:]), AF.Copy, r=["S"], w=["SBF"])
        POv = PB[po][:].rearrange("p (h i) -> p h i", h=4)
        if d == 0:
            self.act(OFT[:, :, cs], POv, AF.Copy, r=[("pb", po)], w=["OFT"])
        else:
            self.tt(OB[:, :, cs], POv, OFT[:, :, cs], ALU.add, r=[("pb", po), "OFT"], w=["OB"])
        self.bfree(po)
        if g == "p":
            lastc = (gc % 2 == 1) if d == 0 else (gc % 2 == 0)
            if lastc:
                kb.dma("sp", self.dout["so"][gc // 2, l, d], self.S[:], r=["S"], w=[("dram", "so", gc // 2, l, d)])

    def onorm(self, OB):
        l, din, PB = self.l, self.din, self.PB
        HL = self.ge["HL"]
        ZS = self.ZS
        self.mark("onorm")
        SQ = [self.aview("on_sq%d" % k, [128, 512], BF16, key=("on_sq", k)) for k in range(2)]
        RN = [self.aview("on_rn%d" % k, [128, 512], F32, key=("on_rn", k)) for k in range(2)]
        TMP = [self.aview("on_tmp%d" % k, [128, 512], F32, key=("on_tmp", k)) for k in range(2)]
        src = din["win"][l, 12:16].rearrange("m p k c -> p m (k c)")
        wv, wk = self.wload_3d(src, 4, 1024)
        for m in range(4):
            b = self.proj(wv[:, m, :], wk, HL, NT)
            self.act(ZS[:, m, :], PB[b][:], AF.Silu, r=[("pb", b)], w=[("zs", m)])
            self.bfree(b)
        for h in range(4):
            k2 = h % 2
            self.tt(SQ[k2][:], OB[:, h, :], OB[:, h, :], ALU.mult, r=["OB"], w=[("on_sq", k2)])
            b = self.balloc()
            self.mm(PB[b][:], self.ONEB[:], SQ[k2][:], start=True, stop=True, r=["consts", ("on_sq", k2)], w=[("pb", b)])
            self.rsqrt_(RN[k2][:], PB[b][:], 1.0 / 128, r=[("pb", b)], w=[("on_rn", k2)])
            self.bfree(b)
            self.tt(TMP[k2][:], OB[:, h, :], RN[k2][:], ALU.mult, r=["OB", ("on_rn", k2)], w=[("on_tmp", k2)])
            self.stt(self.OD[:, h, :], TMP[k2][:], self.DNNG[:, 0:1], ZS[:, h, :], ALU.mult, ALU.mult,
                     r=[("on_tmp", k2), "dnng", ("zs", h)], w=[("od", h)])

    def mixer_rest(self, i, xmid, XW):
        ge, l, kb, din, PB = self.ge, self.l, self.kb, self.din, self.PB
        HL, seg, nsg, gi = ge["HL"], ge["seg"], ge["nsg"], ge["gi"]
        self.phase()
        av = self.aview
        SP31 = seg + 30
        self.mark("mix_cf")
        PC31 = [av("pc31_%d" % k, [128, 2, nsg, SP31], BF16, key=("pc31", k)) for k in range(2)]
        SG = [av("sg%d" % k, [128, 640], F32, key=("sg", k)) for k in range(3)]
        HC = av("HC", [128, 4, 512], F32, key=[("hc", m) for m in range(4)])
        HCB = [av("hcb%d" % k, [128, 512], BF16, key=("hcb", k)) for k in range(2)]
        SQB = [av("sqb%d" % k, [128, 512], BF16, key=("sqb", k)) for k in range(2)]
        MEAN = av("MEAN", [128, 512], F32, key="mean")
        RSTD = av("RSTD", [128, 512], F32, key="rstd")
        HCN = av("HCN", [128, 4, 512], BF16, key=[("hcn", m) for m in range(4)])
        SCV = av("SCV", [128, 4, 512], BF16, key=[("scv", m) for m in range(4)])
        M = av("M", [128, 8, 512], BF16, key=[("m", j) for j in range(8)])
        TM = [av("tm%d" % k, [128, 512], F32, key=("tm", k)) for k in range(3)]
        for k in range(2):
            self.ms(PC31[k][:], 0.0, w=[("pc31", k)])
        wa, wak = self.wload_3d(din["win"][l, 16:20].rearrange("m p k c -> p m (k c)"), 4, 1024)
        wb, wbk = self.wload_3d(din["win"][l, 20:24].rearrange("m p k c -> p m (k c)"), 4, 1024)
        bs1, bs2 = self.balloc(), self.balloc()
        def cf_item(m):
            for tap in range(min(31, self.NDG - 32)):
                self.diag(self.CFC[:, m, tap:tap + 1], "cfc", ck=(self.phase_id, ("cf", m), tap))
            ba = self.proj(wa[:, m, :], wak, HL, NT)
            bb = self.proj(wb[:, m, :], wbk, HL, NT)
            sg, sgk = SG[m % 2], ("sg", m % 2)
            self.act(sg[:, 0:512], PB[bb][:], AF.Sigmoid, r=[("pb", bb)], w=[sgk])
            self.bfree(bb)
            pc, pk = PC31[m % 2], ("pc31", m % 2)
            for g2 in range(2):
                cs2 = slice(g2 * 256, (g2 + 1) * 256)
                self.tt(pc[:, g2, :, 15:15 + seg], PB[ba][:, cs2].rearrange("p (s t) -> p s t", s=nsg),
                        sg[:, cs2].rearrange("p (s t) -> p s t", s=nsg), ALU.mult, r=[("pb", ba), sgk], w=[pk])
            self.bfree(ba)
            yield
            for g2 in range(2):
                cs2 = slice(g2 * 256, (g2 + 1) * 256)
                b2, valid = self.hconv(pc, pk, g2, 31, lambda tap, m=m: self.CFC[:, m, tap:tap + 1], "cfc", cid=("cf", m))
                self.act(HC[:, m, cs2].rearrange("p (s t) -> p s t", s=nsg), valid, AF.Copy, r=[("pb", b2)], w=[("hc", m)])
                self.bfree(b2)
            hb, hbk = HCB[m % 2], ("hcb", m % 2)
            sq, sqk = SQB[m % 2], ("sqb", m % 2)
            self.cp(hb[:], HC[:, m, :], r=[("hc", m)], w=[hbk], eng="pool")
            self.act(sq[:], HC[:, m, :], AF.Square, r=[("hc", m)], w=[sqk])
            yield
            self.mm(PB[bs1][:], self.ONEB[:], hb[:], start=(m == 0), stop=(m == 3), r=["consts", hbk], w=[("pb", bs1)])
            self.mm(PB[bs2][:], self.ONEB[:], sq[:], start=(m == 0), stop=(m == 3), r=["consts", sqk], w=[("pb", bs2)])

        self.swp([cf_item(m) for m in range(4)])
        self.act(MEAN[:], PB[bs1][:], AF.Copy, r=[("pb", bs1)], w=["mean"], scale=1.0 / 512)
        self.bfree(bs1)
        self.tt(RSTD[:], MEAN[:], MEAN[:], ALU.mult, r=["mean"], w=["rstd"])
        self.stt(RSTD[:], PB[bs2][:], 1.0 / 512, RSTD[:], ALU.mult, ALU.subtract, r=[("pb", bs2), "rstd"], w=["rstd"])
        self.bfree(bs2)
        self.ts(RSTD[:], RSTD[:], 0.0, ALU.max, r=["rstd"], w=["rstd"])
        self.act(RSTD[:], RSTD[:], AF.Ln, r=["rstd"], w=["rstd"], bias=EPS)
        self.act(RSTD[:], RSTD[:], AF.Exp, r=["rstd"], w=["rstd"], scale=-0.5)
        for m in range(4):
            tm, tk = TM[m % 2], ("tm", m % 2)
            self.tt(tm[:], HC[:, m, :], MEAN[:], ALU.subtract, r=[("hc", m), "mean"], w=[tk])
            self.tt(tm[:], tm[:], RSTD[:], ALU.mult, r=[tk, "rstd"], w=[tk])
            self.act(HCN[:, m, :], tm[:], AF.Silu, r=[tk, "cfln"], w=[("hcn", m)], scale=self.CFLN[:, m, 0:1], bias=self.CFLN[:, m, 1:2])
        wbg, wbgk = self.wload_3d(din["win"][l, 24:28].rearrange("m p k c -> p m (k c)"), 4, 1024)
        self.mark("mix_sc")
        wcg, wcgk = self.wload_3d(din["win"][l, 28:32].rearrange("m p k c -> p m (k c)"), 4, 1024)
        wxh, wxhk = self.wload_3d(din["win"][l, 32:36].rearrange("m p k c -> p m (k c)"), 4, 1024)
        if HL > 0:
            SCP = [av("scp%d" % k, [128, 640], BF16, key=("scp", k)) for k in range(2)]
        else:
            SCP = [av("scp%d" % k, [128, 2, 1, seg + 2], BF16, key=("scp", k)) for k in range(2)]
            for k in range(2):
                self.ms(SCP[k][:], 0.0, w=[("scp", k)])
        ranges = [(0, 512)] + ([(512, XW)] if XW > 512 else [])
        def sc_item(m):
            scp, sk = SCP[m % 2], ("scp", m % 2)
            cgs, cgk = SG[2], ("sg", 2)
            for (c0, c1) in ranges:
                bc = self.proj(wcg[:, m, :], wcgk, c0, c1 - c0)
                self.act(cgs[:, c0:c1], PB[bc][:, 0:c1 - c0], AF.Copy, r=[("pb", bc)], w=[cgk])
                self.bfree(bc)
                bx = self.proj(wxh[:, m, :], wxhk, c0, c1 - c0)
                if HL > 0:
                    self.tt(scp[:, c0:c1], PB[bx][:, 0:c1 - c0], cgs[:, c0:c1], ALU.mult, r=[("pb", bx), cgk], w=[sk])
                else:
                    for g2 in range(2):
                        cs2 = slice(g2 * 256, (g2 + 1) * 256)
                        self.tt(scp[:, g2, 0, 1:1 + seg], PB[bx][:, cs2], cgs[:, cs2], ALU.mult, r=[("pb", bx), cgk], w=[sk])
                self.bfree(bx)
            bbg = self.proj(wbg[:, m, :], wbgk, HL, NT)
            sg, sgk = SG[m % 2], ("sg", m % 2)
            self.act(sg[:, 0:512], PB[bbg][:], AF.Copy, r=[("pb", bbg)], w=[sgk])
            self.bfree(bbg)
            yield
            bcv = self.balloc()
            if HL > 0:
                for tap in range(3):
                    dg, dk = self.diag(self.SCC[:, m, tap:tap + 1], "scc")
                    self.mm(PB[bcv][:], dg, scp[:, tap * 64:tap * 64 + 512], start=(tap == 0), stop=(tap == 2), r=[dk, sk], w=[("pb", bcv)])
            else:
                for g2 in range(2):
                    for tap in range(3):
                        dg, dk = self.diag(self.SCC[:, m, tap:tap + 1], "scc", ck=(self.phase_id, "sc", m, tap))
                        self.mm(PB[bcv][:, g2 * 256:(g2 + 1) * 256], dg, scp[:, g2, 0, tap:tap + 256], start=(tap == 0), stop=(tap == 2),
                                r=[dk, sk], w=[("pb", bcv)])
            self.tt(SCV[:, m, :], PB[bcv][:], sg[:, 0:512], ALU.mult, r=[("pb", bcv), sgk], w=[("scv", m)])
            self.bfree(bcv)

        self.swp([sc_item(m) for m in range(4)])
        self.mark("mix_merge")
        for j in range(8):
            wg_, wgk = self.wload_3d(din["win"][l, 36 + 3 * j:39 + 3 * j].rearrange("m p k c -> p m (k c)"), 3, 1024)
            wy, wyk = self.wload_3d(din["wbr"][l, j].rearrange("p b k c -> p b (k c)"), 3, 512)
            srcs = [(self.OD, [("od", h) for h in range(4)]), (HCN, [("hcn", h) for h in range(4)]), (SCV, [("scv", h) for h in range(4)])]
            for br in range(3):
                bgt = self.proj(wg_[:, br, :], wgk, HL, NT)
                sg, sgk = SG[br], ("sg", br)
                self.act(sg[:, 0:512], PB[bgt][:], AF.Sigmoid, r=[("pb", bgt)], w=[sgk])
                self.bfree(bgt)
                buf, keys = srcs[br]
                by = self.balloc()
                for kt in range(4):
                    self.mm(PB[by][:], wy[:, br, kt * 128:(kt + 1) * 128], buf[:, kt, :], start=(kt == 0), stop=(kt == 3),
                            r=[wyk, keys[kt]], w=[("pb", by)])
                tm, tk = TM[br], ("tm", br)
                self.tt(tm[:], PB[by][:], sg[:, 0:512], ALU.mult, r=[("pb", by), sgk], w=[tk])
                self.bfree(by)
            self.tt(TM[0][:], TM[0][:], TM[1][:], ALU.add, r=[("tm", 0), ("tm", 1)], w=[("tm", 0)])
            self.tt(M[:, j, :], TM[0][:], TM[2][:], ALU.add, r=[("tm", 0), ("tm", 2)], w=[("m", j)])
        self.mark("mix_wo")
        self.out_proj_res(lambda j: self.wload_3d(din["wo"][l, j:j + 1].rearrange("m p k c -> p m (k c)"), 1, 1024),
                          8, lambda kt: M[:, kt, :], lambda kt: ("m", kt), 2, SQB, HL)
        t0 = i * NT
        dst = xmid.rearrange("(f p) t -> p f t", p=128)[:, :, t0:t0 + NT]
        kb.dma("sp", dst, self.X[:, :, HL:HL + NT], r=["X"], w=[("dram", id(xmid), i)])

    def out_proj_res(self, wfn, nk, rhs, rkeyfn, ic, SQB, HL):
        PB = self.PB
        gi = self.ge["gi"]
        self.MO = self.aview("MO", [128, 8, 512], F32, key=[("mo", j) for j in range(8)])
        RINV = self.aview("opr_rinv", [128, 512], F32, key="opr_rinv")
        TMP = [self.aview("opr_tmp%d" % k, [128, 512], F32, key=("opr_tmp", k)) for k in range(2)]
        bss = self.balloc()

        def item(j):
            wv, wk = wfn(j)
            b = self.balloc()
            for kt in range(nk):
                self.mm(PB[b][:], wv[:, 0, kt * 128:(kt + 1) * 128], rhs(kt), start=(kt == 0), stop=(kt == nk - 1),
                        r=[wk, rkeyfn(kt)], w=[("pb", b)])
            self.act(self.MO[:, j, :], PB[b][:], AF.Copy, r=[("pb", b)], w=[("mo", j)])
            sq, sqk = SQB[j % 2], ("sqb", j % 2)
            self.act(sq[:], PB[b][:], AF.Square, r=[("pb", b)], w=[sqk])
            self.bfree(b)
            yield
            self.mm(PB[bss][:], self.ONEB[:], sq[:], start=(j == 0), stop=(j == 7), r=["consts", sqk], w=[("pb", bss)])

        self.swp([item(j) for j in range(8)])
        self.rsqrt_(RINV[:], PB[bss][:], 1.0 / D, r=[("pb", bss)], w=["opr_rinv"])
        self.bfree(bss)
        for j in range(8):
            tmp, tk = TMP[j % 2], ("opr_tmp", j % 2)
            self.tt(tmp[:], self.MO[:, j, :], RINV[:], ALU.mult, r=[("mo", j), "opr_rinv"], w=[tk])
            xc = self.X[:, j, HL:HL + NT]
            self.stt(xc, tmp[:], self.MODC[:, gi, ic, j:j + 1], xc, ALU.mult, ALU.add, r=[tk, "modc", "X"], w=["X"])

    def loop_ffn(self, i, xmid, xout):
        ge, l, kb, din, PB = self.ge, self.l, self.kb, self.din, self.PB
        HL, seg, nsg, gi = ge["HL"], ge["seg"], ge["nsg"], ge["gi"]
        self.phase()
        XW, vlo, vhi = self.load_x(i, xmid, halo=True)
        self.mark("ffn_normmod")
        self.normmod(XW, vlo, vhi, 3, 4)
        av = self.aview
        HID = av("HID", [128, 22, 512], BF16, key=[("hid", c) for c in range(22)])
        self.mark("ffn_up")
        SA = [av("sa%d" % k, [128, 512], F32, key=("sa", k)) for k in range(2)]
        SQB = [av("fsqb%d" % k, [128, 512], BF16, key=("sqb", k)) for k in range(2)]
        if HL > 0:
            UB = [av("ub%d" % k, [128, 640], BF16, key=("ub", k)) for k in range(4)]
        else:
            UB = [av("ub%d" % k, [128, 2, 1, seg + 2], BF16, key=("ub", k)) for k in range(4)]
            for k in range(4):
                self.ms(UB[k][:], 0.0, w=[("ub", k)])
        ranges = [(0, 512)] + ([(512, XW)] if XW > 512 else [])
        wst = {}

        def up_item(c):
            if c % 2 == 0:
                wst["w"] = self.wload_3d(din["wup"][l, c:c + 2].rearrange("m p a k c -> p m (a k c)"), 2, 2048)
            wv, wk = wst["w"]
            ubs = []
            for ab in range(2):
                ub, uk = UB[(c % 2) * 2 + ab], ("ub", (c % 2) * 2 + ab)
                ubs.append((ub, uk))
                wsl = wv[:, c % 2, ab * 1024:(ab + 1) * 1024]
                for (c0, c1) in ranges:
                    b = self.proj(wsl, wk, c0, c1 - c0)
                    if HL > 0:
                        self.act(ub[:, c0:c1], PB[b][:, 0:c1 - c0], AF.Copy, r=[("pb", b)], w=[uk])
                    else:
                        for g2 in range(2):
                            self.act(ub[:, g2, 0, 1:1 + seg], PB[b][:, g2 * 256:(g2 + 1) * 256], AF.Copy, r=[("pb", b)], w=[uk])
                    self.bfree(b)
            yield
            cb = []
            for ab in range(2):
                ub, uk = ubs[ab]
                ct = ab * 22 + c
                bcv = self.balloc()
                if HL > 0:
                    for tap in range(3):
                        dg, dk = self.diag(self.FFC[:, ct, tap:tap + 1], "ffc")
                        self.mm(PB[bcv][:], dg, ub[:, tap * 64:tap * 64 + 512], start=(tap == 0), stop=(tap == 2), r=[dk, uk], w=[("pb", bcv)])
                else:
                    for g2 in range(2):
                        for tap in range(3):
                            dg, dk = self.diag(self.FFC[:, ct, tap:tap + 1], "ffc", ck=(self.phase_id, "ffn", ct, tap))
                            self.mm(PB[bcv][:, g2 * 256:(g2 + 1) * 256], dg, ub[:, g2, 0, tap:tap + 256], start=(tap == 0), stop=(tap == 2),
                                    r=[dk, uk], w=[("pb", bcv)])
                cb.append(bcv)
            sa, sak = SA[c % 2], ("sa", c % 2)
            self.act(sa[:], PB[cb[0]][:], AF.Silu, r=[("pb", cb[0])], w=[sak])
            self.bfree(cb[0])
            self.tt(HID[:, c, :], PB[cb[1]][:], sa[:], ALU.mult, r=[("pb", cb[1]), sak], w=[("hid", c)])
            self.bfree(cb[1])

        self.swp([up_item(c) for c in range(22)])
        self.mark("ffn_down")
        self.out_proj_res(lambda j: self.wload_3d(din["wdown"][l, j:j + 1].rearrange("m p k c -> p m (k c)"), 1, 2816),
                          22, lambda kt: HID[:, kt, :], lambda kt: ("hid", kt), 5, SQB, HL)
        t0 = i * NT
        dst = xout.rearrange("(f p) t -> p f t", p=128)[:, :, t0:t0 + NT]
        kb.dma("sp", dst, self.X[:, :, HL:HL + NT], r=["X"], w=[("dram", id(xout), i)])


_CACHE = {}


def get_program(cfg=None):
    key = repr(sorted((cfg or {}).items()))
    if key not in _CACHE:
        _CACHE[key] = Prog(cfg).build()
    return _CACHE[key]


def make_in_maps(inputs):
    sh = _shared(inputs)
    xs = np.asarray(inputs["x_sample"], np.float32)
    xp = np.asarray(inputs["x_prompt"], np.float32)
    st = np.asarray(inputs["state_dn"], np.float32)
    c = np.asarray(inputs["c"], np.float32)
    cc = np.asarray(inputs["c_ctx"], np.float32)
    maps = []
    for core in range(8):
        b = core % 4
        m = dict(sh)
        m["xs"] = np.ascontiguousarray(xs[b].T)
        m["xp"] = np.ascontiguousarray(xp[4 * core:4 * core + 4].reshape(TP, D).T)
        m["st0"] = np.ascontiguousarray(st[b].transpose(0, 1, 3, 2, 4))
        m["cvec"] = np.ascontiguousarray(np.stack([_fm(c[b], 8), _fm(cc, 8)], axis=2))
        maps.append(m)
    return maps


def kernel(**inputs):
    nc = get_program()
    maps = make_in_maps(inputs)
    res = run_bass_kernel_spmd(nc, maps, core_ids=list(range(8)))
    R = res.results
    y_sample = np.stack([np.ascontiguousarray(R[b]["ys"].T) for b in range(4)]).astype(np.float32)
    y_prompt = np.concatenate([np.ascontiguousarray(R[c]["yp"].T).reshape(4, 256, D) for c in range(8)]).astype(np.float32)
    so = np.concatenate([R[c]["so"] for c in range(8)])
    new_state = np.ascontiguousarray(so.transpose(0, 1, 2, 4, 3, 5)).astype(np.float32)
    return (y_prompt, y_sample, new_state)
```

```python
import contextlib
import numpy as np
import concourse.bass as bass
import concourse.mybir as mybir
from concourse.bass_utils import run_bass_kernel_spmd

F32 = mybir.dt.float32
BF16 = mybir.dt.bfloat16
U8 = mybir.dt.uint8
AF = mybir.ActivationFunctionType
ALU = mybir.AluOpType

ENGS = ("pe", "act", "dve", "pool", "sp")
NS_DMA = 12
EPS = 1e-6
NT = 512
D = 1024
NEG = -30000.0


class KB:
    def __init__(self, nc, same_engine_sync=True):
        self.nc = nc
        self.prog = {e: [] for e in ENGS}
        self.cnt = {e: 0 for e in ENGS}
        self.seen = {e: {e2: -1 for e2 in ENGS} for e in ENGS}
        self.seen_dma = {e: set() for e in ENGS}
        self.lastw = {}
        self.readers = {}
        self.dmas = []
        self.dma_cnt = {e: 0 for e in ENGS}
        self.last_slot = {}
        self.dma_mark = {e: 0 for e in ENGS}
        self.pending = {}
        self.needed = {e: set() for e in ENGS}
        self.same_engine_sync = same_engine_sync
        self.n_inst = 0

    def snapshot(self, key):
        out = []
        d = self.lastw.get(key)
        if d is not None:
            out.append(d)
        out.extend(self.readers.get(key, {}).values())
        out.extend(self.pending.get(key, []))
        return out

    def _collect(self, r, w):
        deps = []
        if self.pending:
            for k in list(r) + list(w):
                p = self.pending.pop(k, None)
                if p:
                    deps.extend((d, True) for d in p)
        for k in r:
            d = self.lastw.get(k)
            if d is not None:
                deps.append((d, True))
        for k in w:
            d = self.lastw.get(k)
            if d is not None:
                deps.append((d, True))
            for d in self.readers.get(k, {}).values():
                deps.append((d, False))
        return deps

    def _emit_waits(self, eng, deps):
        for d, is_raw in deps:
            if d[0] == "c":
                _, e2, idx = d
                if e2 == eng:
                    if eng == "pe" or not self.same_engine_sync:
                        continue
                if idx <= self.seen[eng][e2]:
                    continue
                self.seen[eng][e2] = idx
                self.needed[e2].add(idx)
                self.prog[eng].append(("wc", e2, idx))
            else:
                did = d[1]
                if did < self.dma_mark[eng] or did in self.seen_dma[eng]:
                    continue
                self.seen_dma[eng].add(did)
                self.prog[eng].append(("wd", did))

    def _update(self, dep, rk, r, w):
        for k in r:
            self.readers.setdefault(k, {})[rk] = dep
        for k in w:
            self.lastw[k] = dep
            self.readers[k] = {}

    def op(self, eng, fn, r=(), w=()):
        deps = self._collect(r, w)
        self._emit_waits(eng, deps)
        idx = self.cnt[eng]
        self.cnt[eng] += 1
        self.prog[eng].append(("i", fn, idx))
        self._update(("c", eng, idx), eng, r, w)
        self.n_inst += 1
        return idx

    def dma(self, q, out_ap, in_ap, r=(), w=()):
        deps = self._collect(r, w)
        n = self.dma_cnt[q]
        self.dma_cnt[q] += 1
        slot = n % NS_DMA
        value = 16 * (n // NS_DMA + 1)
        did = len(self.dmas)
        prev = self.last_slot.get((q, slot))
        if prev is not None:
            deps.append((("d", prev), True))
        self.last_slot[(q, slot)] = did
        self._emit_waits(q, deps)
        self.dmas.append((q, slot, value))
        self.prog[q].append(("dma", out_ap, in_ap, did))
        self._update(("d", did), ("d", did), r, w)
        self.n_inst += 1
        return did

    def barrier(self):
        deps = []
        for e2 in ENGS:
            if e2 != "sp" and self.cnt[e2] > 0:
                deps.append((("c", e2, self.cnt[e2] - 1), True))
        for did in range(self.dma_mark["sp"], len(self.dmas)):
            deps.append((("d", did), True))
        self._emit_waits("sp", deps)
        idx = self.op("sp", lambda e: e.nop())
        nd = len(self.dmas)
        for e in ENGS:
            if e != "sp":
                self._emit_waits(e, [(("c", "sp", idx), True)])
            for e2 in ENGS:
                self.seen[e][e2] = max(self.seen[e][e2], self.cnt[e2] - 1)
            self.dma_mark[e] = nd
            self.seen_dma[e] = set()

    def wait_all_dmas(self, eng):
        self._emit_waits(eng, [(("d", did), True) for did in range(self.dma_mark[eng], len(self.dmas))])

    def emit(self, st):
        nc = self.nc
        sems = {e: st.enter_context(nc.semaphore("s_" + e)) for e in ENGS}
        dsems = {}
        for q in ENGS:
            if self.dma_cnt[q] > 0:
                dsems[q] = [st.enter_context(nc.semaphore("d_%s_%d" % (q, i)))
                            for i in range(min(NS_DMA, self.dma_cnt[q]))]
        rank = {}
        for e in ENGS:
            for i, idx in enumerate(sorted(self.needed[e])):
                rank[(e, idx)] = i + 1
        block = st.enter_context(nc.Block())
        handles = {"pe": "tensor", "act": "scalar", "dve": "vector", "pool": "gpsimd", "sp": "sync"}
        dmas = self.dmas

        def mk(ename):
            entries = self.prog[ename]

            def body(h):
                for ent in entries:
                    t = ent[0]
                    if t == "wc":
                        h.wait_ge(sems[ent[1]], rank[(ent[1], ent[2])])
                    elif t == "wd":
                        q, slot, value = dmas[ent[1]]
                        h.wait_ge(dsems[q][slot], value)
                    elif t == "i":
                        ins = ent[1](h)
                        if (ename, ent[2]) in rank:
                            ins.then_inc(sems[ename], 1)
                    else:
                        q, slot, value = dmas[ent[3]]
                        h.dma_start(out=ent[1], in_=ent[2]).then_inc(dsems[q][slot], 16)
            return body

        for ename in ENGS:
            if self.prog[ename]:
                getattr(block, handles[ename])(mk(ename))


def _panel(W):
    K, M = W.shape
    return np.ascontiguousarray(W.reshape(K // 128, 128, M // 128, 128).transpose(2, 1, 0, 3))


def _fm(v, nt):
    return np.ascontiguousarray(v.reshape(nt, 128).T)


def _consts():
    c = {}
    c["c_ident"] = np.eye(128, dtype=np.float32)
    t = np.arange(128)
    c["c_tri"] = np.stack([(t[:, None] <= t[None, :]), (t[:, None] >= t[None, :])]).astype(np.float32)
    j = t[:, None]
    i = t[None, :]
    nm = np.stack([i >= j, i > j, i <= j, i < j])
    c["c_nmask"] = np.where(nm, 0.0, NEG).astype(np.float32)
    lm = np.zeros((7, 2, 128, 128), np.uint8)
    for l in range(7):
        s = 1 << l
        same = (t[:, None] // (2 * s)) == (t[None, :] // (2 * s))
        L = same & ((t[:, None] // s) % 2 == 1) & ((t[None, :] // s) % 2 == 0)
        lm[l, 0] = L
        lm[l, 1] = L.T
    c["c_lmask"] = lm
    return c


def _shared(inp):
    sh = {}
    f = lambda a: np.asarray(a, dtype=np.float32)
    w_mod = f(inp["w_mod"])
    sh["wmod"] = np.stack([_panel(w_mod[l]) for l in range(2)])
    sh["bmod"] = np.stack([_fm(f(inp["b_mod"])[l], 48) for l in range(2)])
    g = [f(inp[k]) for k in ("g_pre_mix", "g_post_mix", "g_pre_ffn", "g_post_ffn")]
    sh["gains"] = np.stack([np.stack([_fm(gg[l], 8) for gg in g], axis=1) for l in range(2)])
    w_in = f(inp["w_in"])
    cols = np.concatenate([np.arange(0, 2048), np.arange(2064, 4624)])
    gate0 = 4624
    gcols = np.concatenate([np.arange(gate0 + b * 1024 + j * 128, gate0 + b * 1024 + (j + 1) * 128)
                            for j in range(8) for b in range(3)])
    cols = np.concatenate([cols, gcols])
    sh["win"] = np.stack([_panel(w_in[l][:, cols]) for l in range(2)])
    sh["wgate"] = np.stack([np.ascontiguousarray(w_in[l][:, 2048:2064].reshape(8, 128, 16).transpose(1, 0, 2))
                            for l in range(2)])
    dn_conv = f(inp["dn_conv"])
    sh["dnconv"] = np.stack([np.ascontiguousarray(dn_conv[l].T.reshape(12, 128, 3).transpose(1, 0, 2)) for l in range(2)])
    sh["alog"] = np.ascontiguousarray(np.broadcast_to(f(inp["dn_a_log"]).reshape(2, 1, 8), (2, 128, 8)))
    sh["dtb"] = np.ascontiguousarray(np.broadcast_to(f(inp["dn_dt_bias"]).reshape(2, 1, 8), (2, 128, 8)))
    sh["dnng"] = np.ascontiguousarray(f(inp["dn_norm_g"]).reshape(2, 128, 1))
    wdn, wcf, wsc = f(inp["w_dn_out"]), f(inp["w_cf_out"]), f(inp["w_sc_out"])
    sh["wbr"] = np.stack([np.stack([_panel(wdn[l]), _panel(wcf[l]), _panel(wsc[l])], axis=2) for l in range(2)])
    cf_conv = f(inp["cf_conv"])
    sh["cfconv"] = np.stack([np.ascontiguousarray(cf_conv[l].T.reshape(4, 128, 31).transpose(1, 0, 2)) for l in range(2)])
    sh["cfln"] = np.stack([np.stack([_fm(f(inp["cf_ln_g"])[l], 4), _fm(f(inp["cf_ln_b"])[l], 4)], axis=2) for l in range(2)])
    sc_conv = f(inp["sc_conv"])
    sh["scconv"] = np.stack([np.ascontiguousarray(sc_conv[l].T.reshape(4, 128, 3).transpose(1, 0, 2)) for l in range(2)])
    sh["wo"] = np.stack([_panel(f(inp["w_o"])[l]) for l in range(2)])
    wup = f(inp["w_ffn_up"])
    pu = [_panel(wup[l]) for l in range(2)]
    sh["wup"] = np.stack([np.ascontiguousarray(np.stack([p[0:22], p[22:44]], axis=2)) for p in pu])
    ffn_conv = f(inp["ffn_conv"])
    sh["ffnconv"] = np.stack([np.ascontiguousarray(ffn_conv[l].T.reshape(44, 128, 3).transpose(1, 0, 2)) for l in range(2)])
    sh["wdown"] = np.stack([_panel(f(inp["w_ffn_down"])[l]) for l in range(2)])
    sh.update(_consts())
    return sh


SHARED_SHAPES = {
    "wmod": ([2, 48, 128, 8, 128], F32), "bmod": ([2, 128, 48], F32), "gains": ([2, 128, 4, 8], F32),
    "win": ([2, 60, 128, 8, 128], F32), "wgate": ([2, 128, 8, 16], F32), "dnconv": ([2, 128, 12, 3], F32),
    "alog": ([2, 128, 8], F32), "dtb": ([2, 128, 8], F32), "dnng": ([2, 128, 1], F32),
    "wbr": ([2, 8, 128, 3, 4, 128], F32), "cfconv": ([2, 128, 4, 31], F32), "cfln": ([2, 128, 4, 2], F32),
    "scconv": ([2, 128, 4, 3], F32), "wo": ([2, 8, 128, 8, 128], F32), "wup": ([2, 22, 128, 2, 8, 128], F32),
    "ffnconv": ([2, 128, 44, 3], F32), "wdown": ([2, 8, 128, 22, 128], F32),
    "c_ident": ([128, 128], F32), "c_tri": ([2, 128, 128], F32), "c_nmask": ([4, 128, 128], F32),
    "c_lmask": ([7, 2, 128, 128], U8),
}
TS = 4096
TP = 1024
CORE_SHAPES = {"xs": ([D, TS], F32), "xp": ([D, TP], F32), "st0": ([2, 2, 128, 4, 128], F32), "cvec": ([128, 8, 2], F32)}


class Prog:
    def __init__(self, cfg=None):
        self.cfg = cfg or {}
        self.nc = bass.Bass("TRN2", target_bir_lowering=False)
        self.st = contextlib.ExitStack()
        self.kb = KB(self.nc)
        self.free_banks = list(range(8))
        self.dg_i = 0
        self.regions = []
        self.wcache = {}
        self.marks = []
        self.NSLOT = self.cfg.get("nslot", 3)
        self.STAGGER = self.cfg.get("stagger", 5)
        self.dg_cache = {}
        self.dg_gen = {}
        self.NDG = 64
        self.wr_i = 0

    def sb(self, name, shape, dt):
        return self.st.enter_context(self.nc.sbuf_tensor(name, list(shape), dt))

    def balloc(self):
        assert self.free_banks, "out of PSUM banks"
        return self.free_banks.pop(0)

    def bfree(self, b):
        assert b not in self.free_banks
        self.free_banks.append(b)

    def claim(self, key, lo, hi):
        seeds = []
        keep = []
        for (k2, lo2, hi2) in self.regions:
            if lo2 < hi and lo < hi2:
                if k2 != key:
                    seeds.extend(self.kb.snapshot(k2))
                if not (lo <= lo2 and hi2 <= hi):
                    keep.append((k2, lo2, hi2))
            else:
                keep.append((k2, lo2, hi2))
        keep.append((key, lo, hi))
        self.regions = keep
        if seeds:
            self.kb.pending.setdefault(key, []).extend(seeds)

    def aview(self, name, shape, dt, key=None):
        n = 1
        for s in shape[1:]:
            n *= s
        nbytes = n * (4 if dt == F32 else (2 if dt == BF16 else 1))
        nbytes = (nbytes + 31) // 32 * 32
        off = self.aoff
        self.aoff += nbytes
        assert self.aoff <= self.ARENA, ("arena overflow", name, self.aoff)
        if key is not None:
            for k_ in (key if isinstance(key, list) else [key]):
                self.claim(k_, off, off + nbytes)
        a = self.arena[:, off // 4:(off + nbytes) // 4]
        if dt != F32:
            a = a.bitcast(dt)
        a = a[:, 0:n]
        if len(shape) == 3:
            a = a.rearrange("p (a b) -> p a b", a=shape[1])
        elif len(shape) == 4:
            a = a.rearrange("p (a b c) -> p a b c", a=shape[1], b=shape[2])
        elif len(shape) == 5:
            a = a.rearrange("p (a b c d) -> p a b c d", a=shape[1], b=shape[2], c=shape[3])
        return a

    def swp(self, gens):
        active = []
        gens = list(gens)
        k = 0
        while k < len(gens) or active:
            if k < len(gens):
                active.append(gens[k])
                k += 1
            for a in list(reversed(active)):
                try:
                    next(a)
                except StopIteration:
                    active.remove(a)

    def mark(self, name):
        self.marks.append((name, self.kb.cnt["pe"]))

    def phase(self):
        if self.cfg.get("barriers", False):
            self.kb.barrier()
        self.aoff = 0
        self.phase_id = getattr(self, "phase_id", 0) + 1

    def mm(self, out, lhsT, rhs, start, stop, r, w):
        self.kb.op("pe", lambda e: e.matmul(out, lhsT, rhs, start=start, stop=stop), r=r, w=w)

    def tr(self, out, in_, ident, r, w):
        self.kb.op("pe", lambda e: e.transpose(out=out, in_=in_, identity=ident), r=r, w=w)

    def act(self, out, in_, func, r, w, scale=None, bias=None):
        kw = {}
        if scale is not None:
            kw["scale"] = scale
        if bias is not None:
            kw["bias"] = bias
        self.kb.op("act", lambda e: e.activation(out=out, in_=in_, func=func, **kw), r=r, w=w)

    def tt(self, out, in0, in1, op, r, w, eng="dve"):
        self.kb.op(eng, lambda e: e.tensor_tensor(out=out, in0=in0, in1=in1, op=op), r=r, w=w)

    def stt(self, out, in0, scalar, in1, op0, op1, r, w):
        self.kb.op("dve", lambda e: e.scalar_tensor_tensor(out=out, in0=in0, scalar=scalar, in1=in1, op0=op0, op1=op1), r=r, w=w)

    def ts(self, out, in0, s1, op0, r, w, s2=None, op1=None, eng="dve"):
        if op1 is None:
            self.kb.op(eng, lambda e: e.tensor_scalar(out=out, in0=in0, scalar1=s1, scalar2=None, op0=op0), r=r, w=w)
        else:
            self.kb.op(eng, lambda e: e.tensor_scalar(out=out, in0=in0, scalar1=s1, scalar2=s2, op0=op0, op1=op1), r=r, w=w)

    def cp(self, out, in_, r, w, eng="dve"):
        self.kb.op(eng, lambda e: e.tensor_copy(out=out, in_=in_), r=r, w=w)

    def ms(self, ap, val, w, eng="pool"):
        self.kb.op(eng, lambda e: e.memset(ap, val), w=w)

    def cpred(self, out, mask, data, r, w):
        self.kb.op("dve", lambda e: e.copy_predicated(out=out, mask=mask, data=data), r=r, w=w)

    def diag(self, colv, rkey, ck=None):
        if ck is not None and ck in self.dg_cache:
            s, gen = self.dg_cache[ck]
            if self.dg_gen.get(s) == gen:
                return self.DG[:, s, :], ("dg", s)
        s = self.dg_i % self.NDG
        self.dg_i += 1
        self.dg_gen[s] = self.dg_i
        if ck is not None:
            self.dg_cache[ck] = (s, self.dg_i)
        out = self.DG[:, s, :]
        key = ("dg", s)
        self.ts(out, self.IDF[:], colv, ALU.mult, r=["consts", rkey], w=[key], eng="dve")
        return out, key

    def wload(self, src, n):
        s = self.wr_i % self.NWR
        self.wr_i += 1
        key = ("wr", s)
        dst = self.WR[:, s, 0:n]
        self.kb.dma("pool", dst, src, w=[key])
        return dst, key

    def rsqrt_(self, out, in_, scale, r, w):
        self.act(out, in_, AF.Ln, r=r, w=w, scale=scale, bias=EPS)
        self.act(out, out, AF.Exp, r=w, w=w, scale=-0.5)

    def build(self):
        nc, kb = self.nc, self.kb
        cfg = self.cfg
        dbg = cfg.get("dbg", False)
        self.din = {}
        for k, (shp, dt) in list(SHARED_SHAPES.items()) + list(CORE_SHAPES.items()):
            self.din[k] = nc.dram_tensor(k, shp, dt, kind="ExternalInput").ap()
        skind = "ExternalOutput" if dbg else "Internal"
        self.dout = {
            "ys": nc.dram_tensor("ys", [D, TS], F32, kind="ExternalOutput").ap(),
            "yp": nc.dram_tensor("yp", [D, TP], F32, kind="ExternalOutput").ap(),
            "so": nc.dram_tensor("so", [4, 2, 2, 128, 4, 128], F32, kind="ExternalOutput").ap(),
        }
        self.scr = {}
        for g, T in (("s", TS), ("p", TP)):
            self.scr["x1" + g] = nc.dram_tensor("x1" + g, [D, T], F32, kind=skind).ap()
            self.scr["x2" + g] = nc.dram_tensor("x2" + g, [D, T], F32, kind=skind).ap()
            self.scr["of" + g] = nc.dram_tensor("of" + g, [512, T], F32, kind=skind).ap()
            self.scr["qk" + g] = nc.dram_tensor("qk" + g, [1536, T], BF16, kind="Internal").ap()
        self.wsc = nc.dram_tensor("wsc", [104, 128, 4096], BF16, kind="Internal").ap()
        st = self.st
        with st:
            self.alloc()
            self.prologue()
            nl = cfg.get("layers", 2)
            groups = cfg.get("groups", ("s", "p"))
            for l in range(nl):
                self.layer_params(l)
                for g in groups:
                    self.run_group(l, g, nl)
            kb.barrier()
            kb.wait_all_dmas("sp")
            kb.emit(st)
        return nc

    def alloc(self):
        self.PB = [self.st.enter_context(self.nc.psum_tensor("pb%d" % i, [128, 512], F32)) for i in range(8)]
        sb = self.sb
        self.IDF = sb("idf", [128, 128], F32)
        self.IDB = sb("idb", [128, 128], BF16)
        self.ID4 = sb("id4", [128, 4, 128], BF16)
        self.ONEF = sb("onef", [128, 128], F32)
        self.ONEB = sb("oneb", [128, 128], BF16)
        self.TRI = sb("tri", [128, 2, 128], F32)
        self.NM4 = sb("nm4", [128, 4, 4, 128], BF16)
        self.LM4 = sb("lm4", [128, 7, 2, 4, 128], U8)
        self.LMr = sb("lmr", [128, 7, 2, 128], U8)
        self.NMr = sb("nmr", [128, 4, 128], F32)
        self.CV = sb("cv", [128, 8, 2], F32)
        self.SC = sb("sc", [128, 8, 2], BF16)
        self.MOD = sb("mod", [128, 2, 48, 2], F32)
        self.BMOD = sb("bmodsb", [128, 2, 48], F32)
        self.MODC = sb("modc", [128, 2, 6, 8], F32)
        self.GAINS = sb("gains_sb", [128, 4, 8], F32)
        self.WG = sb("wg", [128, 8, 16], BF16)
        self.DNC = sb("dnc", [128, 12, 3], F32)
        self.NEXPA = sb("nexpa", [128, 8], F32)
        self.DTB = sb("dtb_sb", [128, 8], F32)
        self.DNNG = sb("dnng_sb", [128, 1], F32)
        self.CFC = sb("cfc", [128, 4, 31], F32)
        self.CFLN = sb("cfln_sb", [128, 4, 2], F32)
        self.SCC = sb("scc", [128, 4, 3], F32)
        self.FFC = sb("ffc", [128, 44, 3], F32)
        self.X = sb("X", [128, 8, 640], F32)
        self.H = sb("H", [128, 8, 640], BF16)
        self.OD = sb("OD", [128, 4, 512], BF16)
        self.NWR = 4
        self.WR = sb("WR", [128, self.NWR, 4096], BF16)
        self.DG = sb("DG", [128, self.NDG, 128], BF16)
        self.S = sb("S", [128, 4, 128], F32)
        self.SBF = sb("SBF", [128, 4, 128], BF16)
        self.ARENA = self.cfg.get("arena_kb", 97) * 1024
        self.arena = sb("arena", [128, self.ARENA // 4], F32)
        self.aoff = 0

    def prologue(self):
        kb, din = self.kb, self.din
        kb.dma("sp", self.IDF[:], din["c_ident"], w=["consts"])
        kb.dma("pool", self.IDB[:], din["c_ident"], w=["consts"])
        kb.dma("sp", self.TRI[:], din["c_tri"].rearrange("d t i -> t d i"), w=["consts"])
        kb.dma("sp", self.NMr[:], din["c_nmask"].rearrange("k j i -> j k i"), w=["nmr"])
        kb.dma("sp", self.LMr[:], din["c_lmask"].rearrange("l k i j -> i l k j"), w=["lmr"])
        kb.dma("sp", self.CV[:], din["cvec"], w=["cv"])
        kb.dma("sp", self.BMOD[:], din["bmod"].rearrange("l p m -> p l m"), w=["bmod"])
        self.ms(self.ONEF[:], 1.0, w=["consts"])
        self.ms(self.ONEB[:], 1.0, w=["consts"])
        for h in range(4):
            self.cp(self.ID4[:, h, :], self.IDF[:], r=["consts"], w=["consts"], eng="pool")
            self.cp(self.NM4[:, :, h, :], self.NMr[:], r=["nmr"], w=["consts"], eng="pool")
            self.cp(self.LM4[:, :, :, h, :], self.LMr[:], r=["lmr"], w=["consts"], eng="pool")
        self.act(self.SC[:], self.CV[:], AF.Silu, r=["cv"], w=["sc"])
        for l in range(2):
            for g in range(12):
                src = din["wmod"][l, 4 * g:4 * g + 4].rearrange("m p k c -> p m (k c)")
                wv, wk = self.wload_3d(src, 4, 1024)
                for m in range(4):
                    mt = 4 * g + m
                    b = self.balloc()
                    for kt in range(8):
                        self.mm(self.PB[b][:, 0:2], wv[:, m, kt * 128:(kt + 1) * 128], self.SC[:, kt, :],
                                start=(kt == 0), stop=(kt == 7), r=[wk, "sc"], w=[("pb", b)])
                    self.ts(self.MOD[:, l, mt, :], self.PB[b][:, 0:2], self.BMOD[:, l, mt:mt + 1], ALU.add,
                            r=[("pb", b), "bmod"], w=["mod"])
                    self.bfree(b)

    def wload_3d(self, src, m, n, wid=None):
        s = self.wr_i % self.NWR
        self.wr_i += 1
        key = ("wr", s)
        dst = self.WR[:, s, 0:m * n].rearrange("p (m n) -> p m n", m=m)
        if wid is None or not self.cfg.get("wcache", True):
            self.kb.dma("pool", dst, src, w=[key])
            return dst, key
        if wid not in self.wcache:
            idx = len(self.wcache)
            self.wcache[wid] = idx
            self.kb.dma("pool", dst, src, w=[key])
            sc = self.wsc[idx][:, 0:m * n].rearrange("p (m n) -> p m n", m=m)
            self.kb.dma("pool", sc, dst, r=[key], w=[("dram", "w", idx)])
        else:
            idx = self.wcache[wid]
            sc = self.wsc[idx][:, 0:m * n].rearrange("p (m n) -> p m n", m=m)
            self.kb.dma("sp", dst, sc, r=[("dram", "w", idx)], w=[key])
        return dst, key

    def layer_params(self, l):
        kb, din = self.kb, self.din
        kb.barrier()
        kb.dma("sp", self.GAINS[:], din["gains"][l], w=["gains"])
        kb.dma("pool", self.WG[:], din["wgate"][l], w=["wg"])
        kb.dma("sp", self.DNC[:], din["dnconv"][l], w=["dnc"])
        kb.dma("sp", self.NEXPA[:], din["alog"][l], w=["nexpa"])
        kb.dma("sp", self.DTB[:], din["dtb"][l], w=["dtb"])
        kb.dma("sp", self.DNNG[:], din["dnng"][l], w=["dnng"])
        kb.dma("sp", self.CFC[:], din["cfconv"][l], w=["cfc"])
        kb.dma("sp", self.CFLN[:], din["cfln"][l], w=["cfln"])
        kb.dma("sp", self.SCC[:], din["scconv"][l], w=["scc"])
        kb.dma("sp", self.FFC[:], din["ffnconv"][l], w=["ffc"])
        self.act(self.NEXPA[:], self.NEXPA[:], AF.Exp, r=["nexpa"], w=["nexpa"])
        self.ts(self.NEXPA[:], self.NEXPA[:], -1.0, ALU.mult, r=["nexpa"], w=["nexpa"])
        for g in range(2):
            M = self.MOD[:, l, :, g]
            self.stt(self.MODC[:, g, 0, :], M[:, 8:16], 1.0, self.GAINS[:, 0, :], ALU.add, ALU.mult, r=["mod", "gains"], w=["modc"])
            self.cp(self.MODC[:, g, 1, :], M[:, 0:8], r=["mod"], w=["modc"])
            self.tt(self.MODC[:, g, 2, :], M[:, 16:24], self.GAINS[:, 1, :], ALU.mult, r=["mod", "gains"], w=["modc"])
            self.stt(self.MODC[:, g, 3, :], M[:, 32:40], 1.0, self.GAINS[:, 2, :], ALU.add, ALU.mult, r=["mod", "gains"], w=["modc"])
            self.cp(self.MODC[:, g, 4, :], M[:, 24:32], r=["mod"], w=["modc"])
            self.tt(self.MODC[:, g, 5, :], M[:, 40:48], self.GAINS[:, 3, :], ALU.mult, r=["mod", "gains"], w=["modc"])

    def cut(self, n):
        if self.cfg.get("cut") == n:
            self.stop = True
        return getattr(self, "stop", False)

    def geom(self, g):
        if g == "s":
            return dict(T=TS, ntile=TS // NT, seg=64, HL=64, gi=0, nsg=4)
        return dict(T=TP, ntile=TP // NT, seg=256, HL=0, gi=1, nsg=1)

    def run_group(self, l, g, nl):
        ge = self.geom(g)
        self.ge = ge
        self.g = g
        self.l = l
        xin = self.din["x" + g] if l == 0 else self.scr["x2" + g]
        xmid = self.scr["x1" + g]
        xout = self.dout["y" + g] if l == nl - 1 else self.scr["x2" + g]
        stages = self.cfg.get("stages", (1, 2, 3))
        if 1 in stages:
            for i in range(ge["ntile"]):
                if not getattr(self, "stop", False):
                    self.loop_dn(i, 0, xin, None)
        if 2 in stages:
            for i in reversed(range(ge["ntile"])):
                self.loop_dn(i, 1, xin, xmid)
        if 3 in stages:
            for i in range(ge["ntile"]):
                self.loop_ffn(i, xmid, xout)

    def load_x(self, i, xsrc, halo):
        ge = self.ge
        HL = ge["HL"] if halo else 0
        t0 = i * NT
        lo = max(0, t0 - HL)
        hi = min(ge["T"], t0 + NT + HL)
        c0 = lo - (t0 - HL)
        n = hi - lo
        XW = NT + 2 * HL
        src = xsrc.rearrange("(f p) t -> p f t", p=128)[:, :, lo:hi]
        rk = [("dram", id(xsrc), j) for j in range(lo // NT, (hi - 1) // NT + 1)]
        self.kb.dma("pool", self.X[:, :, c0:c0 + n], src, r=rk, w=["X"])
        if c0 > 0:
            self.ms(self.X[:, :, 0:c0], 0.0, w=["X"])
        if c0 + n < XW:
            self.ms(self.X[:, :, c0 + n:XW], 0.0, w=["X"])
        return XW, c0, c0 + n

    def normmod(self, XW, vlo, vhi, ia, ib):
        gi = self.ge["gi"]
        SQ = [self.aview("nm_sq%d" % k, [128, 640], BF16, key=("nm_sq", k)) for k in range(2)]
        RINV = self.aview("nm_rinv", [128, 640], F32, key="nm_rinv")
        TMP = [self.aview("nm_tmp%d" % k, [128, 640], F32, key=("nm_tmp", k)) for k in range(2)]
        ba = self.balloc()
        bb = self.balloc() if XW > 512 else None
        n1 = min(512, XW)
        for ft in range(8):
            sq = SQ[ft % 2]
            k = ("nm_sq", ft % 2)
            self.act(sq[:, 0:XW], self.X[:, ft, 0:XW], AF.Square, r=["X"], w=[k])
            self.mm(self.PB[ba][:, 0:n1], self.ONEB[:], sq[:, 0:n1], start=(ft == 0), stop=(ft == 7), r=["consts", k], w=[("pb", ba)])
            if bb is not None:
                self.mm(self.PB[bb][:, 0:XW - 512], self.ONEB[:], sq[:, 512:XW], start=(ft == 0), stop=(ft == 7), r=["consts", k], w=[("pb", bb)])
        if self.cut(0.1):
            self.bfree(ba)
            return
        self.act(RINV[:, 0:n1], self.PB[ba][:, 0:n1], AF.Ln, r=[("pb", ba)], w=["nm_rinv"], scale=1.0 / D, bias=EPS)
        self.bfree(ba)
        if bb is not None:
            self.act(RINV[:, 512:XW], self.PB[bb][:, 0:XW - 512], AF.Ln, r=[("pb", bb)], w=["nm_rinv"], scale=1.0 / D, bias=EPS)
            self.bfree(bb)
        if self.cut(0.2):
            return
        self.act(RINV[:, 0:XW], RINV[:, 0:XW], AF.Exp, r=["nm_rinv"], w=["nm_rinv"], scale=-0.5)
        if self.cut(0.3):
            return
        for ft in range(8):
            tmp = TMP[ft % 2]
            k = ("nm_tmp", ft % 2)
            self.stt(tmp[:, 0:XW], self.X[:, ft, 0:XW], self.MODC[:, gi, ia, ft:ft + 1], RINV[:, 0:XW], ALU.mult, ALU.mult,
                     r=["X", "modc", "nm_rinv"], w=[k])
            if self.cfg.get("nm_mode") == 1:
                continue
            if self.cfg.get("nm_mode") == 2 or (self.cfg.get("nm_mode") == 3 and ft > 0) or (self.cfg.get("nm_mode") == 4 and ft > 1) or (self.cfg.get("nm_mode") == 5 and ft != 1):
                self.act(self.H[:, ft, 0:XW], tmp[:, 0:XW], AF.Copy, r=[k, "modc"], w=["H"])
                continue
            self.act(self.H[:, ft, 0:XW], tmp[:, 0:XW], AF.Identity, r=[k, "modc"], w=["H"], bias=self.MODC[:, gi, ib, ft:ft + 1])
        if vlo > 0:
            self.ms(self.H[:, :, 0:vlo], 0.0, w=["H"])
        if vhi < XW:
            self.ms(self.H[:, :, vhi:XW], 0.0, w=["H"])

    def proj(self, wv, wk, c0, n, nk=8, rhs=None, rkey="H"):
        b = self.balloc()
        for kt in range(nk):
            r_ = self.H[:, kt, c0:c0 + n] if rhs is None else rhs(kt)
            self.mm(self.PB[b][:, 0:n], wv[:, kt * 128:(kt + 1) * 128], r_, start=(kt == 0), stop=(kt == nk - 1),
                    r=[wk, rkey], w=[("pb", b)])
        return b

    def hconv(self, PC, pkey, gi2, ntap, colv_fn, ckey, cid=None):
        ge = self.ge
        nsg, seg = ge["nsg"], ge["seg"]
        pad = ntap // 2
        SP = seg + 2 * pad
        L = nsg * SP - 2 * pad
        flat = PC[:, gi2, :, :].rearrange("p s t -> p (s t)")
        b = self.balloc()
        for tap in range(ntap):
            dg, dk = self.diag(colv_fn(tap), ckey, ck=(self.phase_id, cid, tap))
            self.mm(self.PB[b][:, 0:L], dg, flat[:, tap:tap + L], start=(tap == 0), stop=(tap == ntap - 1), r=[dk, pkey], w=[("pb", b)])
        valid = self.PB[b][:, 0:nsg * SP].rearrange("p (s t) -> p s t", s=nsg)[:, :, 0:seg]
        return b, valid

    def loop_dn(self, i, d, xin, xmid):
        ge, l, kb, din = self.ge, self.l, self.kb, self.din
        HL, seg, nsg = ge["HL"], ge["seg"], ge["nsg"]
        self.phase()
        QKV = self.aview("QKV", [128, 12, 512], BF16, key=[("qkv", m) for m in range(12)])
        OFT = self.aview("OFT", [128, 4, 512], F32, key="OFT")
        OB = self.aview("OB", [128, 4, 512], F32, key="OB") if d == 1 else None
        self.ZS = self.aview("ZS", [128, 4, 512], BF16, key=[("zs", m) for m in range(4)]) if d == 1 else None
        mark = self.aoff
        t0 = i * NT
        qsc = self.scr["qk" + self.g].rearrange("(m p) t -> p m t", p=128)[:, :, t0:t0 + NT]
        ofs = self.scr["of" + self.g].rearrange("(h p) t -> p h t", p=128)[:, :, t0:t0 + NT]
        XW, vlo, vhi = self.load_x(i, xin, halo=(d == 1))
        self.mark("dn_normmod")
        HLd = HL if d == 1 else 0
        if d == 1:
            kb.dma("pool", QKV[:], qsc, r=[("dram", "qk", self.g, i)], w=[("qkv", m) for m in range(12)])
            kb.dma("pool", OFT[:], ofs, r=[("dram", "of", self.g, i)], w=["OFT"])
        if getattr(self, "stop", False) or self.cut(0):
            return
        self.normmod(XW, vlo, vhi, 0, 1)
        if self.cut(1):
            return
        if d == 0:
            self.qkv_stage(QKV, HLd)
            kb.dma("pool", qsc, QKV[:], r=[("qkv", m) for m in range(12)], w=[("dram", "qk", self.g, i)])
        if self.cut(2):
            return
        if self.cfg.get("barriers", False):
            kb.barrier()
        self.aoff = mark
        self.dn_bufs(d)
        self.mark("chunks")
        self.dn_run(i, d, QKV, OFT, OB, HLd)
        if getattr(self, "stop", False):
            return
        if d == 0:
            kb.dma("pool", ofs, OFT[:], r=["OFT"], w=[("dram", "of", self.g, i)])
            return
        self.onorm(OB)
        self.mixer_rest(i, xmid, XW)

    def qkv_stage(self, QKV, HLd):
        ge, l, din = self.ge, self.l, self.din
        seg, nsg = ge["seg"], ge["nsg"]
        SP3 = seg + 2
        PC = [self.aview("pc%d" % k, [128, 2, nsg, SP3], BF16, key=("pc", k)) for k in range(2)]
        self.mark("qkv")
        S32 = [self.aview("s32_%d" % k, [128, 512], F32, key=("s32", k)) for k in range(8)]
        SQ = [self.aview("sq_%d" % k, [128, 512], BF16, key=("sq", k)) for k in range(4)]
        RN = [self.aview("rn_%d" % k, [128, 512], F32, key=("rn", k)) for k in range(4)]
        for k in range(2):
            self.ms(PC[k][:], 0.0, w=[("pc", k)])
        wst = {}

        def item(mt):
            g3, m = mt // 4, mt % 4
            if m == 0:
                src = din["win"][l, 4 * g3:4 * g3 + 4].rearrange("m p k c -> p m (k c)")
                wst["w"] = self.wload_3d(src, 4, 1024, wid=("win", l, 4 * g3))
            wv, wk = wst["w"]
            for tap in range(3):
                self.diag(self.DNC[:, mt, tap:tap + 1], "dnc", ck=(self.phase_id, ("dn", mt), tap))
            b = self.proj(wv[:, m, :], wk, HLd, NT)
            pc, pk = PC[mt % 2], ("pc", mt % 2)
            for gi2 in range(2):
                self.cp(pc[:, gi2, :, 1:1 + seg], self.PB[b][:, gi2 * 256:(gi2 + 1) * 256].rearrange("p (s t) -> p s t", s=nsg),
                        r=[("pb", b)], w=[pk])
            self.bfree(b)
            yield
            for gi2 in range(2):
                b2, valid = self.hconv(pc, pk, gi2, 3, lambda tap, mt=mt: self.DNC[:, mt, tap:tap + 1], "dnc", cid=("dn", mt))
                if g3 == 2:
                    outv = QKV[:, mt, gi2 * 256:(gi2 + 1) * 256].rearrange("p (s t) -> p s t", s=nsg)
                    self.act(outv, valid, AF.Silu, r=[("pb", b2)], w=[("qkv", mt)])
                else:
                    outv = S32[mt][:, gi2 * 256:(gi2 + 1) * 256].rearrange("p (s t) -> p s t", s=nsg)
                    self.act(outv, valid, AF.Silu, r=[("pb", b2)], w=[("s32", mt)])
                self.bfree(b2)

        self.swp([item(mt) for mt in range(12)])
        def norm_item(mt):
            s32, sk = S32[mt], ("s32", mt)
            sq, qk_ = SQ[mt % 4], ("sq", mt % 4)
            rn, rk = RN[mt % 4], ("rn", mt % 4)
            self.tt(sq[:], s32[:], s32[:], ALU.mult, r=[sk], w=[qk_])
            b3 = self.balloc()
            self.mm(self.PB[b3][:], self.ONEB[:], sq[:], start=True, stop=True, r=["consts", qk_], w=[("pb", b3)])
            self.act(rn[:], self.PB[b3][:], AF.Ln, r=[("pb", b3)], w=[rk], scale=1.0, bias=EPS)
            self.bfree(b3)
            yield
            self.act(rn[:], rn[:], AF.Exp, r=[rk], w=[rk], scale=-0.5)
            yield
            scale = (128.0 ** -0.5) if mt < 4 else 1.0
            self.stt(QKV[:, mt, :], s32[:], scale, rn[:], ALU.mult, ALU.mult, r=[sk, rk], w=[("qkv", mt)])

        self.swp([norm_item(mt) for mt in range(8)])

    def dn_bufs(self, d):
        av = self.aview
        self.Bsh = {}
        for nm in ("R1", "R2", "E"):
            self.Bsh[nm] = av("dnsh_" + nm, [128, 4, 128], F32, key=("dnsh", nm))
        for nm in ("DT", "MD"):
            self.Bsh[nm] = av("dnsh_" + nm, [128, 4, 128], BF16, key=("dnsh", nm))
        for nm in ("TG1", "TG2", "TS1"):
            self.Bsh[nm] = av("dnsh_" + nm, [128, 4], F32, key=("dnsh", nm))
        self.Bs = []
        for sl in range(self.NSLOT):
            B = {}
            B["USB"] = av("dn%d_USB" % sl, [128, 4, 128], F32, key=("dn", sl, "USB"))
            for nm in ("MN", "AN", "T", "U", "YS", "YS2", "QKT", "KBt", "KD", "BV", "WT", "QD", "VN"):
                B[nm] = av("dn%d_%s" % (sl, nm), [128, 4, 128], BF16, key=("dn", sl, nm))
            for nm in ("GG", "BB", "LB", "GAMJ", "NGAMJ", "GTOT", "LGT", "EKB", "EKD"):
                B[nm] = av("dn%d_%s" % (sl, nm), [128, 4], F32, key=("dn", sl, nm))
            self.Bs.append(B)

    def dn_run(self, i, d, QKV, OFT, OB, HLd):
        order = list(range(4)) if d == 0 else list(reversed(range(4)))
        active = []
        nxt = 0
        rounds = 0
        while active or nxt < len(order):
            if nxt < len(order) and len(active) < self.NSLOT and (not active or rounds % self.STAGGER == 0):
                c = order[nxt]
                active.append(self.dn_chunk(i, c, d, QKV, OFT, OB, HLd, nxt % self.NSLOT))
                nxt += 1
            for gen in list(active):
                try:
                    next(gen)
                except StopIteration:
                    active.remove(gen)
            rounds += 1
            if getattr(self, "stop", False):
                return

    def dn_chunk(self, i, c, d, QKV, OFT, OB, HLd, sl):
        kb, B, SH, PB, l, ge, g = self.kb, self.Bs[sl], self.Bsh, self.PB, self.l, self.ge, self.g
        cs = slice(c * 128, (c + 1) * 128)
        hs = slice(HLd + c * 128, HLd + (c + 1) * 128)
        gc = i * 4 + c
        R = lambda *names: [("dn", sl, n) for n in names]
        RS = lambda *names: [("dnsh", n) for n in names]
        flat = lambda ap: ap.rearrange("p h i -> p (h i)")
        bc_t = lambda ap: ap.unsqueeze(1).to_broadcast([128, 4, 128])
        bc_h = lambda ap: ap.unsqueeze(2).to_broadcast([128, 4, 128])
        bg = self.balloc()
        for kt in range(8):
            self.mm(PB[bg][:, 0:16], self.H[:, kt, hs], self.WG[:, kt, :], start=(kt == 0), stop=(kt == 7), r=["H", "wg"], w=[("pb", bg)])
        a_ps = PB[bg][:, 4 * d:4 * d + 4]
        b_ps = PB[bg][:, 8 + 4 * d:12 + 4 * d]
        self.tt(SH["TG1"][:], a_ps, self.DTB[:, 4 * d:4 * d + 4], ALU.add, r=[("pb", bg), "dtb"], w=RS("TG1"))
        self.act(SH["TG2"][:], SH["TG1"][:], AF.Exp, r=RS("TG1"), w=RS("TG2"))
        self.act(SH["TG2"][:], SH["TG2"][:], AF.Ln, r=RS("TG2"), w=RS("TG2"), bias=1.0)
        self.tt(B["GG"][:], SH["TG2"][:], self.NEXPA[:, 4 * d:4 * d + 4], ALU.mult, r=RS("TG2") + ["nexpa"], w=R("GG"))
        self.act(SH["TS1"][:], b_ps, AF.Exp, r=[("pb", bg)], w=RS("TS1"), scale=-1.0)
        self.bfree(bg)
        self.act(SH["TS1"][:], SH["TS1"][:], AF.Ln, r=RS("TS1"), w=RS("TS1"), bias=1.0)
        self.ts(B["LB"][:], SH["TS1"][:], -1.0, ALU.mult, r=RS("TS1"), w=R("LB"))
        self.act(B["BB"][:], SH["TS1"][:], AF.Exp, r=RS("TS1"), w=R("BB"), scale=-1.0)
        yield
        tri = self.TRI[:, d, :]
        bgam = self.balloc()
        self.mm(PB[bgam][:, 0:4], tri, B["GG"][:], start=True, stop=True, r=["consts"] + R("GG"), w=[("pb", bgam)])
        self.mm(PB[bgam][:, 4:8], self.ONEF[:], B["GG"][:], start=True, stop=True, r=["consts"] + R("GG"), w=[("pb", bgam)])
        self.cp(B["GAMJ"][:], PB[bgam][:, 0:4], r=[("pb", bgam)], w=R("GAMJ"))
        self.cp(B["LGT"][:], PB[bgam][:, 4:8], r=[("pb", bgam)], w=R("LGT"))
        self.act(B["GTOT"][:], PB[bgam][:, 4:8], AF.Exp, r=[("pb", bgam)], w=R("GTOT"))
        self.ts(B["NGAMJ"][:], PB[bgam][:, 0:4], -1.0, ALU.mult, r=[("pb", bgam)], w=R("NGAMJ"))
        self.bfree(bgam)
        self.tt(SH["TG1"][:], B["GAMJ"][:], B["LB"][:], ALU.add, r=R("GAMJ", "LB"), w=RS("TG1"))
        self.act(B["EKB"][:], SH["TG1"][:], AF.Exp, r=RS("TG1"), w=R("EKB"))
        self.tt(SH["TG2"][:], B["LGT"][:], B["GAMJ"][:], ALU.subtract, r=R("LGT", "GAMJ"), w=RS("TG2"))
        self.act(B["EKD"][:], SH["TG2"][:], AF.Exp, r=RS("TG2"), w=R("EKD"))
        yield
        self.tt(SH["R1"][:], bc_t(tri), bc_h(B["GG"][:]), ALU.mult, r=["consts"] + R("GG"), w=RS("R1"), eng="pool")
        self.tt(SH["R2"][:], bc_t(self.IDF[:]), bc_h(B["LB"][:]), ALU.mult, r=["consts"] + R("LB"), w=RS("R2"), eng="pool")
        self.tt(SH["R2"][:], SH["R2"][:], SH["R1"][:], ALU.add, r=RS("R1", "R2"), w=RS("R2"), eng="pool")
        p0, p1, p2 = self.balloc(), self.balloc(), self.balloc()
        self.mm(PB[p0][:], self.ONEF[:], flat(SH["R1"][:]), start=True, stop=True, r=["consts"] + RS("R1"), w=[("pb", p0)])
        self.mm(PB[p1][:], self.ONEF[:], flat(SH["R1"][:]), start=True, stop=False, r=["consts"] + RS("R1"), w=[("pb", p1)])
        self.mm(PB[p1][:], self.IDB[:], flat(self.NM4[:, 2 * d, :, :]), start=False, stop=True, r=["consts"], w=[("pb", p1)])
        self.mm(PB[p2][:], self.ONEF[:], flat(SH["R2"][:]), start=True, stop=False, r=["consts"] + RS("R2"), w=[("pb", p2)])
        self.mm(PB[p2][:], self.IDB[:], flat(self.NM4[:, 2 * d + 1, :, :]), start=False, stop=True, r=["consts"], w=[("pb", p2)])
        self.act(flat(SH["E"][:]), PB[p0][:], AF.Exp, r=[("pb", p0)], w=RS("E"))
        self.bfree(p0)
        for h in range(4):
            self.act(SH["DT"][:, h, :], PB[p1][:, h * 128:(h + 1) * 128], AF.Exp, r=[("pb", p1)] + R("NGAMJ"), w=RS("DT"), bias=B["NGAMJ"][:, h:h + 1])
            self.act(SH["MD"][:, h, :], PB[p2][:, h * 128:(h + 1) * 128], AF.Exp, r=[("pb", p2)] + R("NGAMJ"), w=RS("MD"), bias=B["NGAMJ"][:, h:h + 1])
        self.bfree(p1)
        self.bfree(p2)
        self.tt(B["QD"][:], QKV[:, 0:4, cs], SH["E"][:], ALU.mult, r=[("qkv", h) for h in range(4)] + RS("E"), w=R("QD"), eng="pool")
        pkk, pqk = self.balloc(), self.balloc()
        for h in range(4):
            kh = QKV[:, 4 + h, cs]
            qh = QKV[:, h, cs]
            self.mm(PB[pkk][:, h * 128:(h + 1) * 128], kh, kh, start=True, stop=True, r=[("qkv", 4 + h)], w=[("pb", pkk)])
            self.mm(PB[pqk][:, h * 128:(h + 1) * 128], kh, qh, start=True, stop=True, r=[("qkv", 4 + h), ("qkv", h)], w=[("pb", pqk)])
        self.stt(flat(B["MN"][:]), PB[pkk][:], -1.0, flat(SH["MD"][:]), ALU.mult, ALU.mult, r=[("pb", pkk)] + RS("MD"), w=R("MN"))
        self.tt(flat(B["QKT"][:]), PB[pqk][:], flat(SH["DT"][:]), ALU.mult, r=[("pb", pqk)] + RS("DT"), w=R("QKT"))
        self.bfree(pkk)
        self.bfree(pqk)
        yield
        ptr = self.balloc()
        ptv = PB[ptr][:].bitcast(BF16)
        for h in range(4):
            self.tr(ptv[:, h * 128:(h + 1) * 128], B["MN"][:, h, :], self.IDB[:], r=["consts"] + R("MN"), w=[("pb", ptr)])
        self.act(flat(B["AN"][:]), ptv[:, 0:512], AF.Copy, r=[("pb", ptr)], w=R("AN"))
        self.bfree(ptr)
        mT, mU = (0, 1) if d == 0 else (1, 0)
        self.cp(flat(B["T"][:]), flat(self.ID4[:]), r=["consts"], w=R("T"), eng="pool")
        self.cp(flat(B["U"][:]), flat(self.ID4[:]), r=["consts"], w=R("U"), eng="pool")
        self.cpred(flat(B["U"][:]), flat(self.LM4[:, 0, mU, :, :]), flat(B["MN"][:]), r=["consts"] + R("MN", "U"), w=R("U"))
        yield
        self.cpred(flat(B["T"][:]), flat(self.LM4[:, 0, mT, :, :]), flat(B["AN"][:]), r=["consts"] + R("AN", "T"), w=R("T"))
        for lev in range(1, 7):
            lastlev = (lev == 6)
            py = self.balloc()
            py2 = None if lastlev else self.balloc()
            for h in range(4):
                hsl = slice(h * 128, (h + 1) * 128)
                self.mm(PB[py][:, hsl], B["AN"][:, h, :], B["U"][:, h, :], start=True, stop=True, r=R("AN", "U"), w=[("pb", py)])
            if not lastlev:
                for h in range(4):
                    hsl = slice(h * 128, (h + 1) * 128)
                    self.mm(PB[py2][:, hsl], B["MN"][:, h, :], B["T"][:, h, :], start=True, stop=True, r=R("MN", "T"), w=[("pb", py2)])
            self.act(flat(B["YS"][:]), PB[py][:], AF.Copy, r=[("pb", py)], w=R("YS"))
            self.bfree(py)
            if not lastlev:
                self.act(flat(B["YS2"][:]), PB[py2][:], AF.Copy, r=[("pb", py2)], w=R("YS2"))
                self.bfree(py2)
            yield
            pz = self.balloc()
            pz2 = None if lastlev else self.balloc()
            for h in range(4):
                hsl = slice(h * 128, (h + 1) * 128)
                self.mm(PB[pz][:, hsl], B["T"][:, h, :], B["YS"][:, h, :], start=True, stop=True, r=R("T", "YS"), w=[("pb", pz)])
            if not lastlev:
                for h in range(4):
                    hsl = slice(h * 128, (h + 1) * 128)
                    self.mm(PB[pz2][:, hsl], B["U"][:, h, :], B["YS2"][:, h, :], start=True, stop=True, r=R("U", "YS2"), w=[("pb", pz2)])
            self.cpred(flat(B["U"][:]), flat(self.LM4[:, lev, mU, :, :]), PB[pz][:], r=["consts", ("pb", pz)] + R("U"), w=R("U"))
            self.bfree(pz)
            if not lastlev:
                self.cpred(flat(B["T"][:]), flat(self.LM4[:, lev, mT, :, :]), PB[pz2][:], r=["consts", ("pb", pz2)] + R("T"), w=R("T"))
                self.bfree(pz2)
            yield
        pk_, pv_ = self.balloc(), self.balloc()
        pkv, pvv = PB[pk_][:].bitcast(BF16), PB[pv_][:].bitcast(BF16)
        for h in range(4):
            self.tr(pkv[:, h * 128:(h + 1) * 128], QKV[:, 4 + h, cs], self.IDB[:], r=["consts", ("qkv", 4 + h)], w=[("pb", pk_)])
            self.tr(pvv[:, h * 128:(h + 1) * 128], QKV[:, 8 + h, cs], self.IDB[:], r=["consts", ("qkv", 8 + h)], w=[("pb", pv_)])
        pk3 = pkv[:, 0:512].rearrange("p (h x) -> p h x", h=4)
        pv3 = pvv[:, 0:512].rearrange("p (h x) -> p h x", h=4)
        self.tt(B["KBt"][:], pk3, bc_h(B["EKB"][:]), ALU.mult, r=[("pb", pk_)] + R("EKB"), w=R("KBt"))
        self.tt(B["KD"][:], pk3, bc_h(B["EKD"][:]), ALU.mult, r=[("pb", pk_)] + R("EKD"), w=R("KD"))
        self.tt(B["BV"][:], pv3, bc_h(B["BB"][:]), ALU.mult, r=[("pb", pv_)] + R("BB"), w=R("BV"))
        self.bfree(pk_)
        self.bfree(pv_)
        yield
        pu, pw = self.balloc(), self.balloc()
        for h in range(4):
            hsl = slice(h * 128, (h + 1) * 128)
            self.mm(PB[pu][:, hsl], B["U"][:, h, :], B["BV"][:, h, :], start=True, stop=True, r=R("U", "BV"), w=[("pb", pu)])
        for h in range(4):
            hsl = slice(h * 128, (h + 1) * 128)
            self.mm(PB[pw][:, hsl], B["KBt"][:, h, :], B["U"][:, h, :], start=True, stop=True, r=R("KBt", "U"), w=[("pb", pw)])
        self.act(flat(B["USB"][:]), PB[pu][:], AF.Copy, r=[("pb", pu)], w=R("USB"))
        self.cp(flat(B["WT"][:]), PB[pw][:], r=[("pb", pw)], w=R("WT"))
        self.bfree(pu)
        self.bfree(pw)
        yield
        if g == "s":
            first = (gc == 0) if d == 0 else (gc == ge["T"] // 128 - 1)
            if first:
                kb.dma("pool", self.S[:], self.din["st0"][l, d], w=["S"])
                self.act(flat(self.SBF[:]), flat(self.S[:]), AF.Copy, r=["S"], w=["SBF"])
        else:
            first = (gc % 2 == 0) if d == 0 else (gc % 2 == 1)
            if first:
                self.ms(self.S[:], 0.0, w=["S"])
                self.ms(self.SBF[:], 0.0, w=["SBF"])
        pa = self.balloc()
        for h in range(4):
            hsl = slice(h * 128, (h + 1) * 128)
            self.mm(PB[pa][:, hsl], B["WT"][:, h, :], self.SBF[:, h, :], start=True, stop=True, r=R("WT") + ["SBF"], w=[("pb", pa)])
        self.tt(flat(B["VN"][:]), flat(B["USB"][:]), PB[pa][:], ALU.subtract, r=[("pb", pa)] + R("USB"), w=R("VN"))
        self.bfree(pa)
        po, ps_ = self.balloc(), self.balloc()
        for h in range(4):
            hsl = slice(h * 128, (h + 1) * 128)
            self.mm(PB[ps_][:, hsl], B["KD"][:, h, :], B["VN"][:, h, :], start=True, stop=True, r=R("KD", "VN"), w=[("pb", ps_)])
        for h in range(4):
            hsl = slice(h * 128, (h + 1) * 128)
            self.mm(PB[po][:, hsl], self.SBF[:, h, :], B["QD"][:, h, :], start=True, stop=False, r=R("QD") + ["SBF"], w=[("pb", po)])
            self.mm(PB[po][:, hsl], B["VN"][:, h, :], B["QKT"][:, h, :], start=False, stop=True, r=R("VN", "QKT"), w=[("pb", po)])
        for h in range(4):
            hsl = slice(h * 128, (h + 1) * 128)
            self.stt(self.S[:, h, :], self.S[:, h, :], B["GTOT"][:, h:h + 1], PB[ps_][:, hsl], ALU.mult, ALU.add,
                     r=["S", ("pb", ps_)] + R("GTOT"), w=["S"])
        self.bfree(ps_)
        self.act(flat(self.SBF[:]), flat(self.S[:]), AF.Copy, r=["S"], w=["SBF"])
        POv = PB[po][:].rearrange("p (h i) -> p h i", h=4)
        if d == 0:
            self.act(OFT[:, :, cs], POv, AF.Copy, r=[("pb", po)], w=["OFT"])
        else:
            self.tt(OB[:, :, cs], POv, OFT[:, :, cs], ALU.add, r=[("pb", po), "OFT"], w=["OB"])
        self.bfree(po)
        if g == "p":
            lastc = (gc % 2 == 1) if d == 0 else (gc % 2 == 0)
            if lastc:
                kb.dma("pool", self.dout["so"][gc // 2, l, d], self.S[:], r=["S"], w=[("dram", "so", gc // 2, l, d)])

    def onorm(self, OB):
        l, din, PB = self.l, self.din, self.PB
        HL = self.ge["HL"]
        ZS = self.ZS
        self.mark("onorm")
        SQ = [self.aview("on_sq%d" % k, [128, 512], BF16, key=("on_sq", k)) for k in range(2)]
        RN = [self.aview("on_rn%d" % k, [128, 512], F32, key=("on_rn", k)) for k in range(2)]
        TMP = [self.aview("on_tmp%d" % k, [128, 512], F32, key=("on_tmp", k)) for k in range(2)]
        src = din["win"][l, 12:16].rearrange("m p k c -> p m (k c)")
        wv, wk = self.wload_3d(src, 4, 1024, wid=("win", l, 12))
        for m in range(4):
            b = self.proj(wv[:, m, :], wk, HL, NT)
            self.act(ZS[:, m, :], PB[b][:], AF.Silu, r=[("pb", b)], w=[("zs", m)])
            self.bfree(b)
        for h in range(4):
            k2 = h % 2
            self.tt(SQ[k2][:], OB[:, h, :], OB[:, h, :], ALU.mult, r=["OB"], w=[("on_sq", k2)])
            b = self.balloc()
            self.mm(PB[b][:], self.ONEB[:], SQ[k2][:], start=True, stop=True, r=["consts", ("on_sq", k2)], w=[("pb", b)])
            self.rsqrt_(RN[k2][:], PB[b][:], 1.0 / 128, r=[("pb", b)], w=[("on_rn", k2)])
            self.bfree(b)
            self.tt(TMP[k2][:], OB[:, h, :], RN[k2][:], ALU.mult, r=["OB", ("on_rn", k2)], w=[("on_tmp", k2)])
            self.stt(self.OD[:, h, :], TMP[k2][:], self.DNNG[:, 0:1], ZS[:, h, :], ALU.mult, ALU.mult,
                     r=[("on_tmp", k2), "dnng", ("zs", h)], w=[("od", h)])

    def mixer_rest(self, i, xmid, XW):
        ge, l, kb, din, PB = self.ge, self.l, self.kb, self.din, self.PB
        HL, seg, nsg, gi = ge["HL"], ge["seg"], ge["nsg"], ge["gi"]
        self.phase()
        av = self.aview
        SP31 = seg + 30
        self.mark("mix_cf")
        PC31 = [av("pc31_%d" % k, [128, 2, nsg, SP31], BF16, key=("pc31", k)) for k in range(2)]
        SG = [av("sg%d" % k, [128, 640], F32, key=("sg", k)) for k in range(3)]
        HC = av("HC", [128, 4, 512], F32, key=[("hc", m) for m in range(4)])
        HCB = [av("hcb%d" % k, [128, 512], BF16, key=("hcb", k)) for k in range(2)]
        SQB = [av("sqb%d" % k, [128, 512], BF16, key=("sqb", k)) for k in range(2)]
        MEAN = av("MEAN", [128, 512], F32, key="mean")
        RSTD = av("RSTD", [128, 512], F32, key="rstd")
        HCN = av("HCN", [128, 4, 512], BF16, key=[("hcn", m) for m in range(4)])
        SCV = av("SCV", [128, 4, 512], BF16, key=[("scv", m) for m in range(4)])
        M = av("M", [128, 8, 512], BF16, key=[("m", j) for j in range(8)])
        TM = [av("tm%d" % k, [128, 512], F32, key=("tm", k)) for k in range(3)]
        for k in range(2):
            self.ms(PC31[k][:], 0.0, w=[("pc31", k)])
        wa, wak = self.wload_3d(din["win"][l, 16:20].rearrange("m p k c -> p m (k c)"), 4, 1024, wid=("win", l, 16))
        wb, wbk = self.wload_3d(din["win"][l, 20:24].rearrange("m p k c -> p m (k c)"), 4, 1024, wid=("win", l, 20))
        bs1, bs2 = self.balloc(), self.balloc()
        def cf_item(m):
            for tap in range(min(31, self.NDG - 32)):
                self.diag(self.CFC[:, m, tap:tap + 1], "cfc", ck=(self.phase_id, ("cf", m), tap))
            ba = self.proj(wa[:, m, :], wak, HL, NT)
            bb = self.proj(wb[:, m, :], wbk, HL, NT)
            sg, sgk = SG[m % 2], ("sg", m % 2)
            self.act(sg[:, 0:512], PB[bb][:], AF.Sigmoid, r=[("pb", bb)], w=[sgk])
            self.bfree(bb)
            pc, pk = PC31[m % 2], ("pc31", m % 2)
            for g2 in range(2):
                cs2 = slice(g2 * 256, (g2 + 1) * 256)
                self.tt(pc[:, g2, :, 15:15 + seg], PB[ba][:, cs2].rearrange("p (s t) -> p s t", s=nsg),
                        sg[:, cs2].rearrange("p (s t) -> p s t", s=nsg), ALU.mult, r=[("pb", ba), sgk], w=[pk])
            self.bfree(ba)
            yield
            for g2 in range(2):
                cs2 = slice(g2 * 256, (g2 + 1) * 256)
                b2, valid = self.hconv(pc, pk, g2, 31, lambda tap, m=m: self.CFC[:, m, tap:tap + 1], "cfc", cid=("cf", m))
                self.act(HC[:, m, cs2].rearrange("p (s t) -> p s t", s=nsg), valid, AF.Copy, r=[("pb", b2)], w=[("hc", m)])
                self.bfree(b2)
            hb, hbk = HCB[m % 2], ("hcb", m % 2)
            sq, sqk = SQB[m % 2], ("sqb", m % 2)
            self.cp(hb[:], HC[:, m, :], r=[("hc", m)], w=[hbk], eng="pool")
            self.act(sq[:], HC[:, m, :], AF.Square, r=[("hc", m)], w=[sqk])
            yield
            self.mm(PB[bs1][:], self.ONEB[:], hb[:], start=(m == 0), stop=(m == 3), r=["consts", hbk], w=[("pb", bs1)])
            self.mm(PB[bs2][:], self.ONEB[:], sq[:], start=(m == 0), stop=(m == 3), r=["consts", sqk], w=[("pb", bs2)])

        self.swp([cf_item(m) for m in range(4)])
        self.act(MEAN[:], PB[bs1][:], AF.Copy, r=[("pb", bs1)], w=["mean"], scale=1.0 / 512)
        self.bfree(bs1)
        self.tt(RSTD[:], MEAN[:], MEAN[:], ALU.mult, r=["mean"], w=["rstd"])
        self.stt(RSTD[:], PB[bs2][:], 1.0 / 512, RSTD[:], ALU.mult, ALU.subtract, r=[("pb", bs2), "rstd"], w=["rstd"])
        self.bfree(bs2)
        self.ts(RSTD[:], RSTD[:], 0.0, ALU.max, r=["rstd"], w=["rstd"])
        self.act(RSTD[:], RSTD[:], AF.Ln, r=["rstd"], w=["rstd"], bias=EPS)
        self.act(RSTD[:], RSTD[:], AF.Exp, r=["rstd"], w=["rstd"], scale=-0.5)
        for m in range(4):
            tm, tk = TM[m % 2], ("tm", m % 2)
            self.tt(tm[:], HC[:, m, :], MEAN[:], ALU.subtract, r=[("hc", m), "mean"], w=[tk])
            self.tt(tm[:], tm[:], RSTD[:], ALU.mult, r=[tk, "rstd"], w=[tk])
            self.act(HCN[:, m, :], tm[:], AF.Silu, r=[tk, "cfln"], w=[("hcn", m)], scale=self.CFLN[:, m, 0:1], bias=self.CFLN[:, m, 1:2])
        wbg, wbgk = self.wload_3d(din["win"][l, 24:28].rearrange("m p k c -> p m (k c)"), 4, 1024, wid=("win", l, 24))
        self.mark("mix_sc")
        wcg, wcgk = self.wload_3d(din["win"][l, 28:32].rearrange("m p k c -> p m (k c)"), 4, 1024, wid=("win", l, 28))
        wxh, wxhk = self.wload_3d(din["win"][l, 32:36].rearrange("m p k c -> p m (k c)"), 4, 1024, wid=("win", l, 32))
        if HL > 0:
            SCP = [av("scp%d" % k, [128, 640], BF16, key=("scp", k)) for k in range(2)]
        else:
            SCP = [av("scp%d" % k, [128, 2, 1, seg + 2], BF16, key=("scp", k)) for k in range(2)]
            for k in range(2):
                self.ms(SCP[k][:], 0.0, w=[("scp", k)])
        ranges = [(0, 512)] + ([(512, XW)] if XW > 512 else [])
        def sc_item(m):
            scp, sk = SCP[m % 2], ("scp", m % 2)
            cgs, cgk = SG[2], ("sg", 2)
            for (c0, c1) in ranges:
                bc = self.proj(wcg[:, m, :], wcgk, c0, c1 - c0)
                self.act(cgs[:, c0:c1], PB[bc][:, 0:c1 - c0], AF.Copy, r=[("pb", bc)], w=[cgk])
                self.bfree(bc)
                bx = self.proj(wxh[:, m, :], wxhk, c0, c1 - c0)
                if HL > 0:
                    self.tt(scp[:, c0:c1], PB[bx][:, 0:c1 - c0], cgs[:, c0:c1], ALU.mult, r=[("pb", bx), cgk], w=[sk])
                else:
                    for g2 in range(2):
                        cs2 = slice(g2 * 256, (g2 + 1) * 256)
                        self.tt(scp[:, g2, 0, 1:1 + seg], PB[bx][:, cs2], cgs[:, cs2], ALU.mult, r=[("pb", bx), cgk], w=[sk])
                self.bfree(bx)
            bbg = self.proj(wbg[:, m, :], wbgk, HL, NT)
            sg, sgk = SG[m % 2], ("sg", m % 2)
            self.act(sg[:, 0:512], PB[bbg][:], AF.Copy, r=[("pb", bbg)], w=[sgk])
            self.bfree(bbg)
            yield
            bcv = self.balloc()
            if HL > 0:
                for tap in range(3):
                    dg, dk = self.diag(self.SCC[:, m, tap:tap + 1], "scc")
                    self.mm(PB[bcv][:], dg, scp[:, tap * 64:tap * 64 + 512], start=(tap == 0), stop=(tap == 2), r=[dk, sk], w=[("pb", bcv)])
            else:
                for g2 in range(2):
                    for tap in range(3):
                        dg, dk = self.diag(self.SCC[:, m, tap:tap + 1], "scc", ck=(self.phase_id, "sc", m, tap))
                        self.mm(PB[bcv][:, g2 * 256:(g2 + 1) * 256], dg, scp[:, g2, 0, tap:tap + 256], start=(tap == 0), stop=(tap == 2),
                                r=[dk, sk], w=[("pb", bcv)])
            self.tt(SCV[:, m, :], PB[bcv][:], sg[:, 0:512], ALU.mult, r=[("pb", bcv), sgk], w=[("scv", m)])
            self.bfree(bcv)

        self.swp([sc_item(m) for m in range(4)])
        self.mark("mix_merge")
        for j in range(8):
            wg_, wgk = self.wload_3d(din["win"][l, 36 + 3 * j:39 + 3 * j].rearrange("m p k c -> p m (k c)"), 3, 1024, wid=("win", l, 36 + 3 * j))
            wy, wyk = self.wload_3d(din["wbr"][l, j].rearrange("p b k c -> p b (k c)"), 3, 512, wid=("wbr", l, j))
            srcs = [(self.OD, [("od", h) for h in range(4)]), (HCN, [("hcn", h) for h in range(4)]), (SCV, [("scv", h) for h in range(4)])]
            for br in range(3):
                bgt = self.proj(wg_[:, br, :], wgk, HL, NT)
                sg, sgk = SG[br], ("sg", br)
                self.act(sg[:, 0:512], PB[bgt][:], AF.Sigmoid, r=[("pb", bgt)], w=[sgk])
                self.bfree(bgt)
                buf, keys = srcs[br]
                by = self.balloc()
                for kt in range(4):
                    self.mm(PB[by][:], wy[:, br, kt * 128:(kt + 1) * 128], buf[:, kt, :], start=(kt == 0), stop=(kt == 3),
                            r=[wyk, keys[kt]], w=[("pb", by)])
                tm, tk = TM[br], ("tm", br)
                self.tt(tm[:], PB[by][:], sg[:, 0:512], ALU.mult, r=[("pb", by), sgk], w=[tk])
                self.bfree(by)
            self.tt(TM[0][:], TM[0][:], TM[1][:], ALU.add, r=[("tm", 0), ("tm", 1)], w=[("tm", 0)])
            self.tt(M[:, j, :], TM[0][:], TM[2][:], ALU.add, r=[("tm", 0), ("tm", 2)], w=[("m", j)])
        self.mark("mix_wo")
        self.out_proj_res(lambda j: self.wload_3d(din["wo"][l, j:j + 1].rearrange("m p k c -> p m (k c)"), 1, 1024, wid=("wo", l, j)),
                          8, lambda kt: M[:, kt, :], lambda kt: ("m", kt), 2, SQB, HL)
        t0 = i * NT
        dst = xmid.rearrange("(f p) t -> p f t", p=128)[:, :, t0:t0 + NT]
        kb.dma("pool", dst, self.X[:, :, HL:HL + NT], r=["X"], w=[("dram", id(xmid), i)])

    def out_proj_res(self, wfn, nk, rhs, rkeyfn, ic, SQB, HL):
        PB = self.PB
        gi = self.ge["gi"]
        self.MO = self.aview("MO", [128, 8, 512], F32, key=[("mo", j) for j in range(8)])
        RINV = self.aview("opr_rinv", [128, 512], F32, key="opr_rinv")
        TMP = [self.aview("opr_tmp%d" % k, [128, 512], F32, key=("opr_tmp", k)) for k in range(2)]
        bss = self.balloc()

        def item(j):
            wv, wk = wfn(j)
            b = self.balloc()
            for kt in range(nk):
                self.mm(PB[b][:], wv[:, 0, kt * 128:(kt + 1) * 128], rhs(kt), start=(kt == 0), stop=(kt == nk - 1),
                        r=[wk, rkeyfn(kt)], w=[("pb", b)])
            self.act(self.MO[:, j, :], PB[b][:], AF.Copy, r=[("pb", b)], w=[("mo", j)])
            sq, sqk = SQB[j % 2], ("sqb", j % 2)
            self.act(sq[:], PB[b][:], AF.Square, r=[("pb", b)], w=[sqk])
            self.bfree(b)
            yield
            self.mm(PB[bss][:], self.ONEB[:], sq[:], start=(j == 0), stop=(j == 7), r=["consts", sqk], w=[("pb", bss)])

        self.swp([item(j) for j in range(8)])
        self.rsqrt_(RINV[:], PB[bss][:], 1.0 / D, r=[("pb", bss)], w=["opr_rinv"])
        self.bfree(bss)
        for j in range(8):
            tmp, tk = TMP[j % 2], ("opr_tmp", j % 2)
            self.tt(tmp[:], self.MO[:, j, :], RINV[:], ALU.mult, r=[("mo", j), "opr_rinv"], w=[tk])
            xc = self.X[:, j, HL:HL + NT]
            self.stt(xc, tmp[:], self.MODC[:, gi, ic, j:j + 1], xc, ALU.mult, ALU.add, r=[tk, "modc", "X"], w=["X"])

    def loop_ffn(self, i, xmid, xout):
        ge, l, kb, din, PB = self.ge, self.l, self.kb, self.din, self.PB
        HL, seg, nsg, gi = ge["HL"], ge["seg"], ge["nsg"], ge["gi"]
        self.phase()
        XW, vlo, vhi = self.load_x(i, xmid, halo=True)
        self.mark("ffn_normmod")
        self.normmod(XW, vlo, vhi, 3, 4)
        av = self.aview
        HID = av("HID", [128, 22, 512], BF16, key=[("hid", c) for c in range(22)])
        self.mark("ffn_up")
        SA = [av("sa%d" % k, [128, 512], F32, key=("sa", k)) for k in range(2)]
        SQB = [av("fsqb%d" % k, [128, 512], BF16, key=("sqb", k)) for k in range(2)]
        if HL > 0:
            UB = [av("ub%d" % k, [128, 640], BF16, key=("ub", k)) for k in range(4)]
        else:
            UB = [av("ub%d" % k, [128, 2, 1, seg + 2], BF16, key=("ub", k)) for k in range(4)]
            for k in range(4):
                self.ms(UB[k][:], 0.0, w=[("ub", k)])
        ranges = [(0, 512)] + ([(512, XW)] if XW > 512 else [])
        wst = {}

        def up_item(c):
            if c % 2 == 0:
                wst["w"] = self.wload_3d(din["wup"][l, c:c + 2].rearrange("m p a k c -> p m (a k c)"), 2, 2048, wid=("wup", l, c))
            wv, wk = wst["w"]
            ubs = []
            for ab in range(2):
                ub, uk = UB[(c % 2) * 2 + ab], ("ub", (c % 2) * 2 + ab)
                ubs.append((ub, uk))
                wsl = wv[:, c % 2, ab * 1024:(ab + 1) * 1024]
                for (c0, c1) in ranges:
                    b = self.proj(wsl, wk, c0, c1 - c0)
                    if HL > 0:
                        self.act(ub[:, c0:c1], PB[b][:, 0:c1 - c0], AF.Copy, r=[("pb", b)], w=[uk])
                    else:
                        for g2 in range(2):
                            self.act(ub[:, g2, 0, 1:1 + seg], PB[b][:, g2 * 256:(g2 + 1) * 256], AF.Copy, r=[("pb", b)], w=[uk])
                    self.bfree(b)
            yield
            cb = []
            for ab in range(2):
                ub, uk = ubs[ab]
                ct = ab * 22 + c
                bcv = self.balloc()
                if HL > 0:
                    for tap in range(3):
                        dg, dk = self.diag(self.FFC[:, ct, tap:tap + 1], "ffc")
                        self.mm(PB[bcv][:], dg, ub[:, tap * 64:tap * 64 + 512], start=(tap == 0), stop=(tap == 2), r=[dk, uk], w=[("pb", bcv)])
                else:
                    for g2 in range(2):
                        for tap in range(3):
                            dg, dk = self.diag(self.FFC[:, ct, tap:tap + 1], "ffc", ck=(self.phase_id, "ffn", ct, tap))
                            self.mm(PB[bcv][:, g2 * 256:(g2 + 1) * 256], dg, ub[:, g2, 0, tap:tap + 256], start=(tap == 0), stop=(tap == 2),
                                    r=[dk, uk], w=[("pb", bcv)])
                cb.append(bcv)
            sa, sak = SA[c % 2], ("sa", c % 2)
            self.act(sa[:], PB[cb[0]][:], AF.Silu, r=[("pb", cb[0])], w=[sak])
            self.bfree(cb[0])
            self.tt(HID[:, c, :], PB[cb[1]][:], sa[:], ALU.mult, r=[("pb", cb[1]), sak], w=[("hid", c)])
            self.bfree(cb[1])

        self.swp([up_item(c) for c in range(22)])
        self.mark("ffn_down")
        self.out_proj_res(lambda j: self.wload_3d(din["wdown"][l, j:j + 1].rearrange("m p k c -> p m (k c)"), 1, 2816, wid=("wdown", l, j)),
                          22, lambda kt: HID[:, kt, :], lambda kt: ("hid", kt), 5, SQB, HL)
        t0 = i * NT
        dst = xout.rearrange("(f p) t -> p f t", p=128)[:, :, t0:t0 + NT]
        kb.dma("pool", dst, self.X[:, :, HL:HL + NT], r=["X"], w=[("dram", id(xout), i)])


_CACHE = {}


def get_program(cfg=None):
    key = repr(sorted((cfg or {}).items()))
    if key not in _CACHE:
        _CACHE[key] = Prog(cfg).build()
    return _CACHE[key]


def make_in_maps(inputs):
    sh = _shared(inputs)
    xs = np.asarray(inputs["x_sample"], np.float32)
    xp = np.asarray(inputs["x_prompt"], np.float32)
    st = np.asarray(inputs["state_dn"], np.float32)
    c = np.asarray(inputs["c"], np.float32)
    cc = np.asarray(inputs["c_ctx"], np.float32)
    maps = []
    for core in range(8):
        b = core % 4
        m = dict(sh)
        m["xs"] = np.ascontiguousarray(xs[b].T)
        m["xp"] = np.ascontiguousarray(xp[4 * core:4 * core + 4].reshape(TP, D).T)
        m["st0"] = np.ascontiguousarray(st[b].transpose(0, 1, 3, 2, 4))
        m["cvec"] = np.ascontiguousarray(np.stack([_fm(c[b], 8), _fm(cc, 8)], axis=2))
        maps.append(m)
    return maps


def kernel(**inputs):
    nc = get_program()
    maps = make_in_maps(inputs)
    res = run_bass_kernel_spmd(nc, maps, core_ids=list(range(8)))
    R = res.results
    y_sample = np.stack([np.ascontiguousarray(R[b]["ys"].T) for b in range(4)]).astype(np.float32)
    y_prompt = np.concatenate([np.ascontiguousarray(R[c]["yp"].T).reshape(4, 256, D) for c in range(8)]).astype(np.float32)
    so = np.concatenate([R[c]["so"] for c in range(8)])
    new_state = np.ascontiguousarray(so.transpose(0, 1, 2, 4, 3, 5)).astype(np.float32)
    return (y_prompt, y_sample, new_state)
```

```python
import contextlib
import numpy as np
import concourse.bass as bass
import concourse.mybir as mybir
from concourse.bass_utils import run_bass_kernel_spmd

F32 = mybir.dt.float32
BF16 = mybir.dt.bfloat16
U8 = mybir.dt.uint8
AF = mybir.ActivationFunctionType
ALU = mybir.AluOpType

ENGS = ("pe", "act", "dve", "pool", "sp")
NS_DMA = 12
EPS = 1e-6
NT = 512
D = 1024
NEG = -30000.0


class KB:
    def __init__(self, nc, same_engine_sync=True):
        self.nc = nc
        self.prog = {e: [] for e in ENGS}
        self.cnt = {e: 0 for e in ENGS}
        self.seen = {e: {e2: -1 for e2 in ENGS} for e in ENGS}
        self.seen_dma = {e: set() for e in ENGS}
        self.lastw = {}
        self.readers = {}
        self.dmas = []
        self.dma_cnt = {e: 0 for e in ENGS}
        self.last_slot = {}
        self.dma_mark = {e: 0 for e in ENGS}
        self.pending = {}
        self.needed = {e: set() for e in ENGS}
        self.same_engine_sync = same_engine_sync
        self.n_inst = 0

    def snapshot(self, key):
        out = []
        d = self.lastw.get(key)
        if d is not None:
            out.append(d)
        out.extend(self.readers.get(key, {}).values())
        out.extend(self.pending.get(key, []))
        return out

    def _collect(self, r, w):
        deps = []
        if self.pending:
            for k in list(r) + list(w):
                p = self.pending.pop(k, None)
                if p:
                    deps.extend((d, True) for d in p)
        for k in r:
            d = self.lastw.get(k)
            if d is not None:
                deps.append((d, True))
        for k in w:
            d = self.lastw.get(k)
            if d is not None:
                deps.append((d, True))
            for d in self.readers.get(k, {}).values():
                deps.append((d, False))
        return deps

    def _emit_waits(self, eng, deps):
        for d, is_raw in deps:
            if d[0] == "c":
                _, e2, idx = d
                if e2 == eng:
                    if eng == "pe" or not self.same_engine_sync:
                        continue
                if idx <= self.seen[eng][e2]:
                    continue
                self.seen[eng][e2] = idx
                self.needed[e2].add(idx)
                self.prog[eng].append(("wc", e2, idx))
            else:
                did = d[1]
                if did < self.dma_mark[eng] or did in self.seen_dma[eng]:
                    continue
                self.seen_dma[eng].add(did)
                self.prog[eng].append(("wd", did))

    def _update(self, dep, rk, r, w):
        for k in r:
            self.readers.setdefault(k, {})[rk] = dep
        for k in w:
            self.lastw[k] = dep
            self.readers[k] = {}

    def op(self, eng, fn, r=(), w=()):
        deps = self._collect(r, w)
        self._emit_waits(eng, deps)
        idx = self.cnt[eng]
        self.cnt[eng] += 1
        self.prog[eng].append(("i", fn, idx))
        self._update(("c", eng, idx), eng, r, w)
        self.n_inst += 1
        return idx

    def dma(self, q, out_ap, in_ap, r=(), w=()):
        deps = self._collect(r, w)
        n = self.dma_cnt[q]
        self.dma_cnt[q] += 1
        slot = n % NS_DMA
        value = 16 * (n // NS_DMA + 1)
        did = len(self.dmas)
        prev = self.last_slot.get((q, slot))
        if prev is not None:
            deps.append((("d", prev), True))
        self.last_slot[(q, slot)] = did
        self._emit_waits(q, deps)
        self.dmas.append((q, slot, value))
        self.prog[q].append(("dma", out_ap, in_ap, did))
        self._update(("d", did), ("d", did), r, w)
        self.n_inst += 1
        return did

    def barrier(self):
        deps = []
        for e2 in ENGS:
            if e2 != "sp" and self.cnt[e2] > 0:
                deps.append((("c", e2, self.cnt[e2] - 1), True))
        for did in range(self.dma_mark["sp"], len(self.dmas)):
            deps.append((("d", did), True))
        self._emit_waits("sp", deps)
        idx = self.op("sp", lambda e: e.nop())
        nd = len(self.dmas)
        for e in ENGS:
            if e != "sp":
                self._emit_waits(e, [(("c", "sp", idx), True)])
            for e2 in ENGS:
                self.seen[e][e2] = max(self.seen[e][e2], self.cnt[e2] - 1)
            self.dma_mark[e] = nd
            self.seen_dma[e] = set()

    def wait_all_dmas(self, eng):
        self._emit_waits(eng, [(("d", did), True) for did in range(self.dma_mark[eng], len(self.dmas))])

    def emit(self, st):
        nc = self.nc
        sems = {e: st.enter_context(nc.semaphore("s_" + e)) for e in ENGS}
        dsems = {}
        for q in ENGS:
            if self.dma_cnt[q] > 0:
                dsems[q] = [st.enter_context(nc.semaphore("d_%s_%d" % (q, i)))
                            for i in range(min(NS_DMA, self.dma_cnt[q]))]
        rank = {}
        for e in ENGS:
            for i, idx in enumerate(sorted(self.needed[e])):
                rank[(e, idx)] = i + 1
        block = st.enter_context(nc.Block())
        handles = {"pe": "tensor", "act": "scalar", "dve": "vector", "pool": "gpsimd", "sp": "sync"}
        dmas = self.dmas

        def mk(ename):
            entries = self.prog[ename]

            def body(h):
                for ent in entries:
                    t = ent[0]
                    if t == "wc":
                        h.wait_ge(sems[ent[1]], rank[(ent[1], ent[2])])
                    elif t == "wd":
                        q, slot, value = dmas[ent[1]]
                        h.wait_ge(dsems[q][slot], value)
                    elif t == "i":
                        ins = ent[1](h)
                        if (ename, ent[2]) in rank:
                            ins.then_inc(sems[ename], 1)
                    else:
                        q, slot, value = dmas[ent[3]]
                        h.dma_start(out=ent[1], in_=ent[2]).then_inc(dsems[q][slot], 16)
            return body

        for ename in ENGS:
            if self.prog[ename]:
                getattr(block, handles[ename])(mk(ename))


def _panel(W):
    K, M = W.shape
    return np.ascontiguousarray(W.reshape(K // 128, 128, M // 128, 128).transpose(2, 1, 0, 3))


def _fm(v, nt):
    return np.ascontiguousarray(v.reshape(nt, 128).T)


def _consts():
    c = {}
    c["c_ident"] = np.eye(128, dtype=np.float32)
    t = np.arange(128)
    c["c_tri"] = np.stack([(t[:, None] <= t[None, :]), (t[:, None] >= t[None, :])]).astype(np.float32)
    j = t[:, None]
    i = t[None, :]
    nm = np.stack([i >= j, i > j, i <= j, i < j])
    c["c_nmask"] = np.where(nm, 0.0, NEG).astype(np.float32)
    lm = np.zeros((7, 2, 128, 128), np.uint8)
    for l in range(7):
        s = 1 << l
        same = (t[:, None] // (2 * s)) == (t[None, :] // (2 * s))
        L = same & ((t[:, None] // s) % 2 == 1) & ((t[None, :] // s) % 2 == 0)
        lm[l, 0] = L
        lm[l, 1] = L.T
    c["c_lmask"] = lm
    return c


def _shared(inp):
    sh = {}
    f = lambda a: np.asarray(a, dtype=np.float32)
    w_mod = f(inp["w_mod"])
    sh["wmod"] = np.stack([_panel(w_mod[l]) for l in range(2)])
    sh["bmod"] = np.stack([_fm(f(inp["b_mod"])[l], 48) for l in range(2)])
    g = [f(inp[k]) for k in ("g_pre_mix", "g_post_mix", "g_pre_ffn", "g_post_ffn")]
    sh["gains"] = np.stack([np.stack([_fm(gg[l], 8) for gg in g], axis=1) for l in range(2)])
    w_in = f(inp["w_in"])
    cols = np.concatenate([np.arange(0, 2048), np.arange(2064, 4624)])
    gate0 = 4624
    gcols = np.concatenate([np.arange(gate0 + b * 1024 + j * 128, gate0 + b * 1024 + (j + 1) * 128)
                            for j in range(8) for b in range(3)])
    cols = np.concatenate([cols, gcols])
    sh["win"] = np.stack([_panel(w_in[l][:, cols]) for l in range(2)])
    sh["wgate"] = np.stack([np.ascontiguousarray(w_in[l][:, 2048:2064].reshape(8, 128, 16).transpose(1, 0, 2))
                            for l in range(2)])
    dn_conv = f(inp["dn_conv"])
    sh["dnconv"] = np.stack([np.ascontiguousarray(dn_conv[l].T.reshape(12, 128, 3).transpose(1, 0, 2)) for l in range(2)])
    sh["alog"] = np.ascontiguousarray(np.broadcast_to(f(inp["dn_a_log"]).reshape(2, 1, 8), (2, 128, 8)))
    sh["dtb"] = np.ascontiguousarray(np.broadcast_to(f(inp["dn_dt_bias"]).reshape(2, 1, 8), (2, 128, 8)))
    sh["dnng"] = np.ascontiguousarray(f(inp["dn_norm_g"]).reshape(2, 128, 1))
    wdn, wcf, wsc = f(inp["w_dn_out"]), f(inp["w_cf_out"]), f(inp["w_sc_out"])
    sh["wbr"] = np.stack([np.stack([_panel(wdn[l]), _panel(wcf[l]), _panel(wsc[l])], axis=2) for l in range(2)])
    cf_conv = f(inp["cf_conv"])
    sh["cfconv"] = np.stack([np.ascontiguousarray(cf_conv[l].T.reshape(4, 128, 31).transpose(1, 0, 2)) for l in range(2)])
    sh["cfln"] = np.stack([np.stack([_fm(f(inp["cf_ln_g"])[l], 4), _fm(f(inp["cf_ln_b"])[l], 4)], axis=2) for l in range(2)])
    sc_conv = f(inp["sc_conv"])
    sh["scconv"] = np.stack([np.ascontiguousarray(sc_conv[l].T.reshape(4, 128, 3).transpose(1, 0, 2)) for l in range(2)])
    sh["wo"] = np.stack([_panel(f(inp["w_o"])[l]) for l in range(2)])
    wup = f(inp["w_ffn_up"])
    pu = [_panel(wup[l]) for l in range(2)]
    sh["wup"] = np.stack([np.ascontiguousarray(np.stack([p[0:22], p[22:44]], axis=2)) for p in pu])
    ffn_conv = f(inp["ffn_conv"])
    sh["ffnconv"] = np.stack([np.ascontiguousarray(ffn_conv[l].T.reshape(44, 128, 3).transpose(1, 0, 2)) for l in range(2)])
    sh["wdown"] = np.stack([_panel(f(inp["w_ffn_down"])[l]) for l in range(2)])
    sh.update(_consts())
    return sh


SHARED_SHAPES = {
    "wmod": ([2, 48, 128, 8, 128], F32), "bmod": ([2, 128, 48], F32), "gains": ([2, 128, 4, 8], F32),
    "win": ([2, 60, 128, 8, 128], F32), "wgate": ([2, 128, 8, 16], F32), "dnconv": ([2, 128, 12, 3], F32),
    "alog": ([2, 128, 8], F32), "dtb": ([2, 128, 8], F32), "dnng": ([2, 128, 1], F32),
    "wbr": ([2, 8, 128, 3, 4, 128], F32), "cfconv": ([2, 128, 4, 31], F32), "cfln": ([2, 128, 4, 2], F32),
    "scconv": ([2, 128, 4, 3], F32), "wo": ([2, 8, 128, 8, 128], F32), "wup": ([2, 22, 128, 2, 8, 128], F32),
    "ffnconv": ([2, 128, 44, 3], F32), "wdown": ([2, 8, 128, 22, 128], F32),
    "c_ident": ([128, 128], F32), "c_tri": ([2, 128, 128], F32), "c_nmask": ([4, 128, 128], F32),
    "c_lmask": ([7, 2, 128, 128], U8),
}
TS = 4096
TP = 1024
CORE_SHAPES = {"xs": ([D, TS], F32), "xp": ([D, TP], F32), "st0": ([2, 2, 128, 4, 128], F32), "cvec": ([128, 8, 2], F32)}


class Prog:
    def __init__(self, cfg=None):
        self.cfg = cfg or {}
        self.nc = bass.Bass("TRN2", target_bir_lowering=False)
        self.st = contextlib.ExitStack()
        self.kb = KB(self.nc)
        self.free_banks = list(range(8))
        self.dg_i = 0
        self.regions = []
        self.wcache = {}
        self.marks = []
        self.NSLOT = self.cfg.get("nslot", 3)
        self.STAGGER = self.cfg.get("stagger", 5)
        self.dg_cache = {}
        self.dg_gen = {}
        self.NDG = 64
        self.wr_i = 0

    def sb(self, name, shape, dt):
        return self.st.enter_context(self.nc.sbuf_tensor(name, list(shape), dt))

    def balloc(self):
        assert self.free_banks, "out of PSUM banks"
        return self.free_banks.pop(0)

    def bfree(self, b):
        assert b not in self.free_banks
        self.free_banks.append(b)

    def claim(self, key, lo, hi):
        seeds = []
        keep = []
        for (k2, lo2, hi2) in self.regions:
            if lo2 < hi and lo < hi2:
                if k2 != key:
                    seeds.extend(self.kb.snapshot(k2))
                if not (lo <= lo2 and hi2 <= hi):
                    keep.append((k2, lo2, hi2))
            else:
                keep.append((k2, lo2, hi2))
        keep.append((key, lo, hi))
        self.regions = keep
        if seeds:
            self.kb.pending.setdefault(key, []).extend(seeds)

    def aview(self, name, shape, dt, key=None):
        n = 1
        for s in shape[1:]:
            n *= s
        nbytes = n * (4 if dt == F32 else (2 if dt == BF16 else 1))
        nbytes = (nbytes + 31) // 32 * 32
        off = self.aoff
        self.aoff += nbytes
        assert self.aoff <= self.ARENA, ("arena overflow", name, self.aoff)
        if key is not None:
            for k_ in (key if isinstance(key, list) else [key]):
                self.claim(k_, off, off + nbytes)
        a = self.arena[:, off // 4:(off + nbytes) // 4]
        if dt != F32:
            a = a.bitcast(dt)
        a = a[:, 0:n]
        if len(shape) == 3:
            a = a.rearrange("p (a b) -> p a b", a=shape[1])
        elif len(shape) == 4:
            a = a.rearrange("p (a b c) -> p a b c", a=shape[1], b=shape[2])
        elif len(shape) == 5:
            a = a.rearrange("p (a b c d) -> p a b c d", a=shape[1], b=shape[2], c=shape[3])
        return a

    def swp(self, gens):
        active = []
        gens = list(gens)
        k = 0
        while k < len(gens) or active:
            if k < len(gens):
                active.append(gens[k])
                k += 1
            for a in list(reversed(active)):
                try:
                    next(a)
                except StopIteration:
                    active.remove(a)

    def mark(self, name):
        self.marks.append((name, self.kb.cnt["pe"]))

    def phase(self):
        if self.cfg.get("barriers", False):
            self.kb.barrier()
        self.aoff = 0
        self.phase_id = getattr(self, "phase_id", 0) + 1

    def mm(self, out, lhsT, rhs, start, stop, r, w):
        self.kb.op("pe", lambda e: e.matmul(out, lhsT, rhs, start=start, stop=stop), r=r, w=w)

    def tr(self, out, in_, ident, r, w):
        self.kb.op("pe", lambda e: e.transpose(out=out, in_=in_, identity=ident), r=r, w=w)

    def act(self, out, in_, func, r, w, scale=None, bias=None):
        kw = {}
        if scale is not None:
            kw["scale"] = scale
        if bias is not None:
            kw["bias"] = bias
        self.kb.op("act", lambda e: e.activation(out=out, in_=in_, func=func, **kw), r=r, w=w)

    def tt(self, out, in0, in1, op, r, w, eng="dve"):
        self.kb.op(eng, lambda e: e.tensor_tensor(out=out, in0=in0, in1=in1, op=op), r=r, w=w)

    def stt(self, out, in0, scalar, in1, op0, op1, r, w):
        self.kb.op("dve", lambda e: e.scalar_tensor_tensor(out=out, in0=in0, scalar=scalar, in1=in1, op0=op0, op1=op1), r=r, w=w)

    def ts(self, out, in0, s1, op0, r, w, s2=None, op1=None, eng="dve"):
        if op1 is None:
            self.kb.op(eng, lambda e: e.tensor_scalar(out=out, in0=in0, scalar1=s1, scalar2=None, op0=op0), r=r, w=w)
        else:
            self.kb.op(eng, lambda e: e.tensor_scalar(out=out, in0=in0, scalar1=s1, scalar2=s2, op0=op0, op1=op1), r=r, w=w)

    def cp(self, out, in_, r, w, eng="dve"):
        self.kb.op(eng, lambda e: e.tensor_copy(out=out, in_=in_), r=r, w=w)

    def ms(self, ap, val, w, eng="pool"):
        self.kb.op(eng, lambda e: e.memset(ap, val), w=w)

    def cpred(self, out, mask, data, r, w):
        self.kb.op("dve", lambda e: e.copy_predicated(out=out, mask=mask, data=data), r=r, w=w)

    def diag(self, colv, rkey, ck=None):
        if ck is not None and ck in self.dg_cache:
            s, gen = self.dg_cache[ck]
            if self.dg_gen.get(s) == gen:
                return self.DG[:, s, :], ("dg", s)
        s = self.dg_i % self.NDG
        self.dg_i += 1
        self.dg_gen[s] = self.dg_i
        if ck is not None:
            self.dg_cache[ck] = (s, self.dg_i)
        out = self.DG[:, s, :]
        key = ("dg", s)
        self.ts(out, self.IDF[:], colv, ALU.mult, r=["consts", rkey], w=[key], eng="dve")
        return out, key

    def wload(self, src, n):
        s = self.wr_i % self.NWR
        self.wr_i += 1
        key = ("wr", s)
        dst = self.WR[:, s, 0:n]
        self.kb.dma("pool", dst, src, w=[key])
        return dst, key

    def rsqrt_(self, out, in_, scale, r, w):
        self.act(out, in_, AF.Ln, r=r, w=w, scale=scale, bias=EPS)
        self.act(out, out, AF.Exp, r=w, w=w, scale=-0.5)

    def build(self):
        nc, kb = self.nc, self.kb
        cfg = self.cfg
        dbg = cfg.get("dbg", False)
        self.din = {}
        for k, (shp, dt) in list(SHARED_SHAPES.items()) + list(CORE_SHAPES.items()):
            self.din[k] = nc.dram_tensor(k, shp, dt, kind="ExternalInput").ap()
        skind = "ExternalOutput" if dbg else "Internal"
        self.dout = {
            "ys": nc.dram_tensor("ys", [D, TS], F32, kind="ExternalOutput").ap(),
            "yp": nc.dram_tensor("yp", [D, TP], F32, kind="ExternalOutput").ap(),
            "so": nc.dram_tensor("so", [4, 2, 2, 128, 4, 128], F32, kind="ExternalOutput").ap(),
        }
        self.scr = {}
        for g, T in (("s", TS), ("p", TP)):
            self.scr["x1" + g] = nc.dram_tensor("x1" + g, [D, T], F32, kind=skind).ap()
            self.scr["x2" + g] = nc.dram_tensor("x2" + g, [D, T], F32, kind=skind).ap()
            self.scr["of" + g] = nc.dram_tensor("of" + g, [512, T], F32, kind=skind).ap()
            self.scr["qk" + g] = nc.dram_tensor("qk" + g, [1536, T], BF16, kind="Internal").ap()
        self.wsc = nc.dram_tensor("wsc", [104, 128, 4096], BF16, kind="Internal").ap()
        st = self.st
        with st:
            self.alloc()
            self.prologue()
            nl = cfg.get("layers", 2)
            groups = cfg.get("groups", ("s", "p"))
            for l in range(nl):
                self.layer_params(l)
                for g in groups:
                    self.run_group(l, g, nl)
            kb.barrier()
            kb.wait_all_dmas("sp")
            kb.emit(st)
        return nc

    def alloc(self):
        self.PB = [self.st.enter_context(self.nc.psum_tensor("pb%d" % i, [128, 512], F32)) for i in range(8)]
        sb = self.sb
        self.IDF = sb("idf", [128, 128], F32)
        self.IDB = sb("idb", [128, 128], BF16)
        self.ID4 = sb("id4", [128, 4, 128], BF16)
        self.ONEF = sb("onef", [128, 128], F32)
        self.ONEB = sb("oneb", [128, 128], BF16)
        self.TRI = sb("tri", [128, 2, 128], F32)
        self.NM4 = sb("nm4", [128, 4, 4, 128], BF16)
        self.LM4 = sb("lm4", [128, 7, 2, 4, 128], U8)
        self.LMr = sb("lmr", [128, 7, 2, 128], U8)
        self.NMr = sb("nmr", [128, 4, 128], F32)
        self.CV = sb("cv", [128, 8, 2], F32)
        self.SC = sb("sc", [128, 8, 2], BF16)
        self.MOD = sb("mod", [128, 2, 48, 2], F32)
        self.BMOD = sb("bmodsb", [128, 2, 48], F32)
        self.MODC = sb("modc", [128, 2, 6, 8], F32)
        self.GAINS = sb("gains_sb", [128, 4, 8], F32)
        self.WG = sb("wg", [128, 8, 16], BF16)
        self.DNC = sb("dnc", [128, 12, 3], F32)
        self.NEXPA = sb("nexpa", [128, 8], F32)
        self.DTB = sb("dtb_sb", [128, 8], F32)
        self.DNNG = sb("dnng_sb", [128, 1], F32)
        self.CFC = sb("cfc", [128, 4, 31], F32)
        self.CFLN = sb("cfln_sb", [128, 4, 2], F32)
        self.SCC = sb("scc", [128, 4, 3], F32)
        self.FFC = sb("ffc", [128, 44, 3], F32)
        self.X = sb("X", [128, 8, 640], F32)
        self.H = sb("H", [128, 8, 640], BF16)
        self.OD = sb("OD", [128, 4, 512], BF16)
        self.NWR = 4
        self.WR = sb("WR", [128, self.NWR, 4096], BF16)
        self.DG = sb("DG", [128, self.NDG, 128], BF16)
        self.S = sb("S", [128, 4, 128], F32)
        self.SBF = sb("SBF", [128, 4, 128], BF16)
        self.ARENA = self.cfg.get("arena_kb", 97) * 1024
        self.arena = sb("arena", [128, self.ARENA // 4], F32)
        self.aoff = 0

    def prologue(self):
        kb, din = self.kb, self.din
        kb.dma("sp", self.IDF[:], din["c_ident"], w=["consts"])
        kb.dma("pool", self.IDB[:], din["c_ident"], w=["consts"])
        kb.dma("sp", self.TRI[:], din["c_tri"].rearrange("d t i -> t d i"), w=["consts"])
        kb.dma("sp", self.NMr[:], din["c_nmask"].rearrange("k j i -> j k i"), w=["nmr"])
        kb.dma("sp", self.LMr[:], din["c_lmask"].rearrange("l k i j -> i l k j"), w=["lmr"])
        kb.dma("sp", self.CV[:], din["cvec"], w=["cv"])
        kb.dma("sp", self.BMOD[:], din["bmod"].rearrange("l p m -> p l m"), w=["bmod"])
        self.ms(self.ONEF[:], 1.0, w=["consts"])
        self.ms(self.ONEB[:], 1.0, w=["consts"])
        for h in range(4):
            self.cp(self.ID4[:, h, :], self.IDF[:], r=["consts"], w=["consts"], eng="pool")
            self.cp(self.NM4[:, :, h, :], self.NMr[:], r=["nmr"], w=["consts"], eng="pool")
            self.cp(self.LM4[:, :, :, h, :], self.LMr[:], r=["lmr"], w=["consts"], eng="pool")
        self.act(self.SC[:], self.CV[:], AF.Silu, r=["cv"], w=["sc"])
        for l in range(2):
            for g in range(12):
                src = din["wmod"][l, 4 * g:4 * g + 4].rearrange("m p k c -> p m (k c)")
                wv, wk = self.wload_3d(src, 4, 1024)
                for m in range(4):
                    mt = 4 * g + m
                    b = self.balloc()
                    for kt in range(8):
                        self.mm(self.PB[b][:, 0:2], wv[:, m, kt * 128:(kt + 1) * 128], self.SC[:, kt, :],
                                start=(kt == 0), stop=(kt == 7), r=[wk, "sc"], w=[("pb", b)])
                    self.ts(self.MOD[:, l, mt, :], self.PB[b][:, 0:2], self.BMOD[:, l, mt:mt + 1], ALU.add,
                            r=[("pb", b), "bmod"], w=["mod"])
                    self.bfree(b)

    def wload_3d(self, src, m, n, wid=None):
        s = self.wr_i % self.NWR
        self.wr_i += 1
        key = ("wr", s)
        dst = self.WR[:, s, 0:m * n].rearrange("p (m n) -> p m n", m=m)
        if wid is None or not self.cfg.get("wcache", True):
            self.kb.dma("pool", dst, src, w=[key])
            return dst, key
        if wid not in self.wcache:
            idx = len(self.wcache)
            self.wcache[wid] = idx
            self.kb.dma("pool", dst, src, w=[key])
            sc = self.wsc[idx][:, 0:m * n].rearrange("p (m n) -> p m n", m=m)
            self.kb.dma("pool", sc, dst, r=[key], w=[("dram", "w", idx)])
        else:
            idx = self.wcache[wid]
            sc = self.wsc[idx][:, 0:m * n].rearrange("p (m n) -> p m n", m=m)
            self.kb.dma("sp", dst, sc, r=[("dram", "w", idx)], w=[key])
        return dst, key

    def layer_params(self, l):
        kb, din = self.kb, self.din
        kb.barrier()
        kb.dma("sp", self.GAINS[:], din["gains"][l], w=["gains"])
        kb.dma("pool", self.WG[:], din["wgate"][l], w=["wg"])
        kb.dma("sp", self.DNC[:], din["dnconv"][l], w=["dnc"])
        kb.dma("sp", self.NEXPA[:], din["alog"][l], w=["nexpa"])
        kb.dma("sp", self.DTB[:], din["dtb"][l], w=["dtb"])
        kb.dma("sp", self.DNNG[:], din["dnng"][l], w=["dnng"])
        kb.dma("sp", self.CFC[:], din["cfconv"][l], w=["cfc"])
        kb.dma("sp", self.CFLN[:], din["cfln"][l], w=["cfln"])
        kb.dma("sp", self.SCC[:], din["scconv"][l], w=["scc"])
        kb.dma("sp", self.FFC[:], din["ffnconv"][l], w=["ffc"])
        self.act(self.NEXPA[:], self.NEXPA[:], AF.Exp, r=["nexpa"], w=["nexpa"])
        self.ts(self.NEXPA[:], self.NEXPA[:], -1.0, ALU.mult, r=["nexpa"], w=["nexpa"])
        for g in range(2):
            M = self.MOD[:, l, :, g]
            self.stt(self.MODC[:, g, 0, :], M[:, 8:16], 1.0, self.GAINS[:, 0, :], ALU.add, ALU.mult, r=["mod", "gains"], w=["modc"])
            self.cp(self.MODC[:, g, 1, :], M[:, 0:8], r=["mod"], w=["modc"])
            self.tt(self.MODC[:, g, 2, :], M[:, 16:24], self.GAINS[:, 1, :], ALU.mult, r=["mod", "gains"], w=["modc"])
            self.stt(self.MODC[:, g, 3, :], M[:, 32:40], 1.0, self.GAINS[:, 2, :], ALU.add, ALU.mult, r=["mod", "gains"], w=["modc"])
            self.cp(self.MODC[:, g, 4, :], M[:, 24:32], r=["mod"], w=["modc"])
            self.tt(self.MODC[:, g, 5, :], M[:, 40:48], self.GAINS[:, 3, :], ALU.mult, r=["mod", "gains"], w=["modc"])

    def cut(self, n):
        if self.cfg.get("cut") == n:
            self.stop = True
        return getattr(self, "stop", False)

    def geom(self, g):
        if g == "s":
            return dict(T=TS, ntile=TS // NT, seg=64, HL=64, gi=0, nsg=4)
        return dict(T=TP, ntile=TP // NT, seg=256, HL=0, gi=1, nsg=1)

    def run_group(self, l, g, nl):
        ge = self.geom(g)
        self.ge = ge
        self.g = g
        self.l = l
        xin = self.din["x" + g] if l == 0 else self.scr["x2" + g]
        xmid = self.scr["x1" + g]
        xout = self.dout["y" + g] if l == nl - 1 else self.scr["x2" + g]
        stages = self.cfg.get("stages", (1, 2, 3))
        if 1 in stages:
            for i in range(ge["ntile"]):
                if not getattr(self, "stop", False):
                    self.loop_dn(i, 0, xin, None)
        if 2 in stages:
            for i in reversed(range(ge["ntile"])):
                self.loop_dn(i, 1, xin, xmid)
        if 3 in stages:
            for i in range(ge["ntile"]):
                self.loop_ffn(i, xmid, xout)

    def load_x(self, i, xsrc, halo):
        ge = self.ge
        HL = ge["HL"] if halo else 0
        t0 = i * NT
        lo = max(0, t0 - HL)
        hi = min(ge["T"], t0 + NT + HL)
        c0 = lo - (t0 - HL)
        n = hi - lo
        XW = NT + 2 * HL
        src = xsrc.rearrange("(f p) t -> p f t", p=128)[:, :, lo:hi]
        rk = [("dram", id(xsrc), j) for j in range(lo // NT, (hi - 1) // NT + 1)]
        self.kb.dma("pool", self.X[:, :, c0:c0 + n], src, r=rk, w=["X"])
        if c0 > 0:
            self.ms(self.X[:, :, 0:c0], 0.0, w=["X"])
        if c0 + n < XW:
            self.ms(self.X[:, :, c0 + n:XW], 0.0, w=["X"])
        return XW, c0, c0 + n

    def normmod(self, XW, vlo, vhi, ia, ib):
        gi = self.ge["gi"]
        SQ = [self.aview("nm_sq%d" % k, [128, 640], BF16, key=("nm_sq", k)) for k in range(2)]
        RINV = self.aview("nm_rinv", [128, 640], F32, key="nm_rinv")
        TMP = [self.aview("nm_tmp%d" % k, [128, 640], F32, key=("nm_tmp", k)) for k in range(2)]
        ba = self.balloc()
        bb = self.balloc() if XW > 512 else None
        n1 = min(512, XW)
        for ft in range(8):
            sq = SQ[ft % 2]
            k = ("nm_sq", ft % 2)
            if ft % 2 == 0:
                self.act(sq[:, 0:XW], self.X[:, ft, 0:XW], AF.Square, r=["X"], w=[k])
            else:
                self.tt(sq[:, 0:XW], self.X[:, ft, 0:XW], self.X[:, ft, 0:XW], ALU.mult, r=["X"], w=[k])
            self.mm(self.PB[ba][:, 0:n1], self.ONEB[:], sq[:, 0:n1], start=(ft == 0), stop=(ft == 7), r=["consts", k], w=[("pb", ba)])
            if bb is not None:
                self.mm(self.PB[bb][:, 0:XW - 512], self.ONEB[:], sq[:, 512:XW], start=(ft == 0), stop=(ft == 7), r=["consts", k], w=[("pb", bb)])
        if self.cut(0.1):
            self.bfree(ba)
            return
        self.act(RINV[:, 0:n1], self.PB[ba][:, 0:n1], AF.Ln, r=[("pb", ba)], w=["nm_rinv"], scale=1.0 / D, bias=EPS)
        self.bfree(ba)
        if bb is not None:
            self.act(RINV[:, 512:XW], self.PB[bb][:, 0:XW - 512], AF.Ln, r=[("pb", bb)], w=["nm_rinv"], scale=1.0 / D, bias=EPS)
            self.bfree(bb)
        if self.cut(0.2):
            return
        self.act(RINV[:, 0:XW], RINV[:, 0:XW], AF.Exp, r=["nm_rinv"], w=["nm_rinv"], scale=-0.5)
        if self.cut(0.3):
            return
        for ft in range(8):
            tmp = TMP[ft % 2]
            k = ("nm_tmp", ft % 2)
            self.stt(tmp[:, 0:XW], self.X[:, ft, 0:XW], self.MODC[:, gi, ia, ft:ft + 1], RINV[:, 0:XW], ALU.mult, ALU.mult,
                     r=["X", "modc", "nm_rinv"], w=[k])
            if self.cfg.get("nm_mode") == 1:
                continue
            if self.cfg.get("nm_mode") == 2 or (self.cfg.get("nm_mode") == 3 and ft > 0) or (self.cfg.get("nm_mode") == 4 and ft > 1) or (self.cfg.get("nm_mode") == 5 and ft != 1):
                self.act(self.H[:, ft, 0:XW], tmp[:, 0:XW], AF.Copy, r=[k, "modc"], w=["H"])
                continue
            self.act(self.H[:, ft, 0:XW], tmp[:, 0:XW], AF.Identity, r=[k, "modc"], w=["H"], bias=self.MODC[:, gi, ib, ft:ft + 1])
        if vlo > 0:
            self.ms(self.H[:, :, 0:vlo], 0.0, w=["H"])
        if vhi < XW:
            self.ms(self.H[:, :, vhi:XW], 0.0, w=["H"])

    def proj(self, wv, wk, c0, n, nk=8, rhs=None, rkey="H"):
        b = self.balloc()
        for kt in range(nk):
            r_ = self.H[:, kt, c0:c0 + n] if rhs is None else rhs(kt)
            self.mm(self.PB[b][:, 0:n], wv[:, kt * 128:(kt + 1) * 128], r_, start=(kt == 0), stop=(kt == nk - 1),
                    r=[wk, rkey], w=[("pb", b)])
        return b

    def hconv(self, PC, pkey, gi2, ntap, colv_fn, ckey, cid=None):
        ge = self.ge
        nsg, seg = ge["nsg"], ge["seg"]
        pad = ntap // 2
        SP = seg + 2 * pad
        L = nsg * SP - 2 * pad
        flat = PC[:, gi2, :, :].rearrange("p s t -> p (s t)")
        b = self.balloc()
        for tap in range(ntap):
            dg, dk = self.diag(colv_fn(tap), ckey, ck=(self.phase_id, cid, tap))
            self.mm(self.PB[b][:, 0:L], dg, flat[:, tap:tap + L], start=(tap == 0), stop=(tap == ntap - 1), r=[dk, pkey], w=[("pb", b)])
        valid = self.PB[b][:, 0:nsg * SP].rearrange("p (s t) -> p s t", s=nsg)[:, :, 0:seg]
        return b, valid

    def loop_dn(self, i, d, xin, xmid):
        ge, l, kb, din = self.ge, self.l, self.kb, self.din
        HL, seg, nsg = ge["HL"], ge["seg"], ge["nsg"]
        self.phase()
        QKV = self.aview("QKV", [128, 12, 512], BF16, key=[("qkv", m) for m in range(12)])
        OFT = self.aview("OFT", [128, 4, 512], F32, key="OFT")
        OB = self.aview("OB", [128, 4, 512], F32, key="OB") if d == 1 else None
        self.ZS = self.aview("ZS", [128, 4, 512], BF16, key=[("zs", m) for m in range(4)]) if d == 1 else None
        mark = self.aoff
        t0 = i * NT
        qsc = self.scr["qk" + self.g].rearrange("(m p) t -> p m t", p=128)[:, :, t0:t0 + NT]
        ofs = self.scr["of" + self.g].rearrange("(h p) t -> p h t", p=128)[:, :, t0:t0 + NT]
        XW, vlo, vhi = self.load_x(i, xin, halo=(d == 1))
        self.mark("dn_normmod")
        HLd = HL if d == 1 else 0
        if d == 1:
            kb.dma("pool", QKV[:], qsc, r=[("dram", "qk", self.g, i)], w=[("qkv", m) for m in range(12)])
            kb.dma("pool", OFT[:], ofs, r=[("dram", "of", self.g, i)], w=["OFT"])
        if getattr(self, "stop", False) or self.cut(0):
            return
        self.normmod(XW, vlo, vhi, 0, 1)
        if self.cut(1):
            return
        if d == 0:
            self.qkv_stage(QKV, HLd)
            kb.dma("pool", qsc, QKV[:], r=[("qkv", m) for m in range(12)], w=[("dram", "qk", self.g, i)])
        if self.cut(2):
            return
        if self.cfg.get("barriers", False):
            kb.barrier()
        self.aoff = mark
        self.dn_bufs(d)
        self.mark("chunks")
        self.dn_run(i, d, QKV, OFT, OB, HLd)
        if getattr(self, "stop", False):
            return
        if d == 0:
            kb.dma("pool", ofs, OFT[:], r=["OFT"], w=[("dram", "of", self.g, i)])
            return
        self.onorm(OB)
        self.mixer_rest(i, xmid, XW)

    def qkv_stage(self, QKV, HLd):
        ge, l, din = self.ge, self.l, self.din
        seg, nsg = ge["seg"], ge["nsg"]
        SP3 = seg + 2
        PC = [self.aview("pc%d" % k, [128, 2, nsg, SP3], BF16, key=("pc", k)) for k in range(2)]
        self.mark("qkv")
        S32 = [self.aview("s32_%d" % k, [128, 512], F32, key=("s32", k)) for k in range(8)]
        SQ = [self.aview("sq_%d" % k, [128, 512], BF16, key=("sq", k)) for k in range(4)]
        RN = [self.aview("rn_%d" % k, [128, 512], F32, key=("rn", k)) for k in range(4)]
        for k in range(2):
            self.ms(PC[k][:], 0.0, w=[("pc", k)])
        wst = {}

        def item(mt):
            g3, m = mt // 4, mt % 4
            if m == 0:
                src = din["win"][l, 4 * g3:4 * g3 + 4].rearrange("m p k c -> p m (k c)")
                wst["w"] = self.wload_3d(src, 4, 1024, wid=("win", l, 4 * g3))
            wv, wk = wst["w"]
            for tap in range(3):
                self.diag(self.DNC[:, mt, tap:tap + 1], "dnc", ck=(self.phase_id, ("dn", mt), tap))
            b = self.proj(wv[:, m, :], wk, HLd, NT)
            pc, pk = PC[mt % 2], ("pc", mt % 2)
            for gi2 in range(2):
                self.cp(pc[:, gi2, :, 1:1 + seg], self.PB[b][:, gi2 * 256:(gi2 + 1) * 256].rearrange("p (s t) -> p s t", s=nsg),
                        r=[("pb", b)], w=[pk])
            self.bfree(b)
            yield
            for gi2 in range(2):
                b2, valid = self.hconv(pc, pk, gi2, 3, lambda tap, mt=mt: self.DNC[:, mt, tap:tap + 1], "dnc", cid=("dn", mt))
                if g3 == 2:
                    outv = QKV[:, mt, gi2 * 256:(gi2 + 1) * 256].rearrange("p (s t) -> p s t", s=nsg)
                    self.act(outv, valid, AF.Silu, r=[("pb", b2)], w=[("qkv", mt)])
                else:
                    outv = S32[mt][:, gi2 * 256:(gi2 + 1) * 256].rearrange("p (s t) -> p s t", s=nsg)
                    self.act(outv, valid, AF.Silu, r=[("pb", b2)], w=[("s32", mt)])
                self.bfree(b2)

        self.swp([item(mt) for mt in range(12)])
        def norm_item(mt):
            s32, sk = S32[mt], ("s32", mt)
            sq, qk_ = SQ[mt % 4], ("sq", mt % 4)
            rn, rk = RN[mt % 4], ("rn", mt % 4)
            self.tt(sq[:], s32[:], s32[:], ALU.mult, r=[sk], w=[qk_])
            b3 = self.balloc()
            self.mm(self.PB[b3][:], self.ONEB[:], sq[:], start=True, stop=True, r=["consts", qk_], w=[("pb", b3)])
            self.act(rn[:], self.PB[b3][:], AF.Ln, r=[("pb", b3)], w=[rk], scale=1.0, bias=EPS)
            self.bfree(b3)
            yield
            self.act(rn[:], rn[:], AF.Exp, r=[rk], w=[rk], scale=-0.5)
            yield
            scale = (128.0 ** -0.5) if mt < 4 else 1.0
            self.stt(QKV[:, mt, :], s32[:], scale, rn[:], ALU.mult, ALU.mult, r=[sk, rk], w=[("qkv", mt)])

        self.swp([norm_item(mt) for mt in range(8)])

    def dn_bufs(self, d):
        av = self.aview
        self.Bsh = {}
        for nm in ("R1", "R2", "E"):
            self.Bsh[nm] = av("dnsh_" + nm, [128, 4, 128], F32, key=("dnsh", nm))
        for nm in ("DT", "MD"):
            self.Bsh[nm] = av("dnsh_" + nm, [128, 4, 128], BF16, key=("dnsh", nm))
        for nm in ("TG1", "TG2", "TS1"):
            self.Bsh[nm] = av("dnsh_" + nm, [128, 4], F32, key=("dnsh", nm))
        self.Bs = []
        for sl in range(self.NSLOT):
            B = {}
            B["USB"] = av("dn%d_USB" % sl, [128, 4, 128], F32, key=("dn", sl, "USB"))
            for nm in ("MN", "AN", "T", "U", "YS", "YS2", "QKT", "KBt", "KD", "BV", "WT", "QD", "VN"):
                B[nm] = av("dn%d_%s" % (sl, nm), [128, 4, 128], BF16, key=("dn", sl, nm))
            for nm in ("GG", "BB", "LB", "GAMJ", "NGAMJ", "GTOT", "LGT", "EKB", "EKD"):
                B[nm] = av("dn%d_%s" % (sl, nm), [128, 4], F32, key=("dn", sl, nm))
            self.Bs.append(B)

    def dn_run(self, i, d, QKV, OFT, OB, HLd):
        order = list(range(4)) if d == 0 else list(reversed(range(4)))
        active = []
        nxt = 0
        rounds = 0
        while active or nxt < len(order):
            if nxt < len(order) and len(active) < self.NSLOT and (not active or rounds % self.STAGGER == 0):
                c = order[nxt]
                active.append(self.dn_chunk(i, c, d, QKV, OFT, OB, HLd, nxt % self.NSLOT))
                nxt += 1
            for gen in list(active):
                try:
                    next(gen)
                except StopIteration:
                    active.remove(gen)
            rounds += 1
            if getattr(self, "stop", False):
                return

    def dn_chunk(self, i, c, d, QKV, OFT, OB, HLd, sl):
        kb, B, SH, PB, l, ge, g = self.kb, self.Bs[sl], self.Bsh, self.PB, self.l, self.ge, self.g
        cs = slice(c * 128, (c + 1) * 128)
        hs = slice(HLd + c * 128, HLd + (c + 1) * 128)
        gc = i * 4 + c
        R = lambda *names: [("dn", sl, n) for n in names]
        RS = lambda *names: [("dnsh", n) for n in names]
        flat = lambda ap: ap.rearrange("p h i -> p (h i)")
        bc_t = lambda ap: ap.unsqueeze(1).to_broadcast([128, 4, 128])
        bc_h = lambda ap: ap.unsqueeze(2).to_broadcast([128, 4, 128])
        bg = self.balloc()
        for kt in range(8):
            self.mm(PB[bg][:, 0:16], self.H[:, kt, hs], self.WG[:, kt, :], start=(kt == 0), stop=(kt == 7), r=["H", "wg"], w=[("pb", bg)])
        a_ps = PB[bg][:, 4 * d:4 * d + 4]
        b_ps = PB[bg][:, 8 + 4 * d:12 + 4 * d]
        self.tt(SH["TG1"][:], a_ps, self.DTB[:, 4 * d:4 * d + 4], ALU.add, r=[("pb", bg), "dtb"], w=RS("TG1"))
        self.act(SH["TG2"][:], SH["TG1"][:], AF.Exp, r=RS("TG1"), w=RS("TG2"))
        self.act(SH["TG2"][:], SH["TG2"][:], AF.Ln, r=RS("TG2"), w=RS("TG2"), bias=1.0)
        self.tt(B["GG"][:], SH["TG2"][:], self.NEXPA[:, 4 * d:4 * d + 4], ALU.mult, r=RS("TG2") + ["nexpa"], w=R("GG"))
        self.act(SH["TS1"][:], b_ps, AF.Exp, r=[("pb", bg)], w=RS("TS1"), scale=-1.0)
        self.bfree(bg)
        self.act(SH["TS1"][:], SH["TS1"][:], AF.Ln, r=RS("TS1"), w=RS("TS1"), bias=1.0)
        self.ts(B["LB"][:], SH["TS1"][:], -1.0, ALU.mult, r=RS("TS1"), w=R("LB"))
        self.act(B["BB"][:], SH["TS1"][:], AF.Exp, r=RS("TS1"), w=R("BB"), scale=-1.0)
        yield
        tri = self.TRI[:, d, :]
        bgam = self.balloc()
        self.mm(PB[bgam][:, 0:4], tri, B["GG"][:], start=True, stop=True, r=["consts"] + R("GG"), w=[("pb", bgam)])
        self.mm(PB[bgam][:, 4:8], self.ONEF[:], B["GG"][:], start=True, stop=True, r=["consts"] + R("GG"), w=[("pb", bgam)])
        self.cp(B["GAMJ"][:], PB[bgam][:, 0:4], r=[("pb", bgam)], w=R("GAMJ"))
        self.cp(B["LGT"][:], PB[bgam][:, 4:8], r=[("pb", bgam)], w=R("LGT"))
        self.act(B["GTOT"][:], PB[bgam][:, 4:8], AF.Exp, r=[("pb", bgam)], w=R("GTOT"))
        self.ts(B["NGAMJ"][:], PB[bgam][:, 0:4], -1.0, ALU.mult, r=[("pb", bgam)], w=R("NGAMJ"))
        self.bfree(bgam)
        self.tt(SH["TG1"][:], B["GAMJ"][:], B["LB"][:], ALU.add, r=R("GAMJ", "LB"), w=RS("TG1"))
        self.act(B["EKB"][:], SH["TG1"][:], AF.Exp, r=RS("TG1"), w=R("EKB"))
        self.tt(SH["TG2"][:], B["LGT"][:], B["GAMJ"][:], ALU.subtract, r=R("LGT", "GAMJ"), w=RS("TG2"))
        self.act(B["EKD"][:], SH["TG2"][:], AF.Exp, r=RS("TG2"), w=R("EKD"))
        yield
        self.tt(SH["R1"][:], bc_t(tri), bc_h(B["GG"][:]), ALU.mult, r=["consts"] + R("GG"), w=RS("R1"), eng="pool")
        self.tt(SH["R2"][:], bc_t(self.IDF[:]), bc_h(B["LB"][:]), ALU.mult, r=["consts"] + R("LB"), w=RS("R2"), eng="pool")
        self.tt(SH["R2"][:], SH["R2"][:], SH["R1"][:], ALU.add, r=RS("R1", "R2"), w=RS("R2"), eng="pool")
        p0, p1, p2 = self.balloc(), self.balloc(), self.balloc()
        self.mm(PB[p0][:], self.ONEF[:], flat(SH["R1"][:]), start=True, stop=True, r=["consts"] + RS("R1"), w=[("pb", p0)])
        self.mm(PB[p1][:], self.ONEF[:], flat(SH["R1"][:]), start=True, stop=False, r=["consts"] + RS("R1"), w=[("pb", p1)])
        self.mm(PB[p1][:], self.IDB[:], flat(self.NM4[:, 2 * d, :, :]), start=False, stop=True, r=["consts"], w=[("pb", p1)])
        self.mm(PB[p2][:], self.ONEF[:], flat(SH["R2"][:]), start=True, stop=False, r=["consts"] + RS("R2"), w=[("pb", p2)])
        self.mm(PB[p2][:], self.IDB[:], flat(self.NM4[:, 2 * d + 1, :, :]), start=False, stop=True, r=["consts"], w=[("pb", p2)])
        self.act(flat(SH["E"][:]), PB[p0][:], AF.Exp, r=[("pb", p0)], w=RS("E"))
        self.bfree(p0)
        for h in range(4):
            self.act(SH["DT"][:, h, :], PB[p1][:, h * 128:(h + 1) * 128], AF.Exp, r=[("pb", p1)] + R("NGAMJ"), w=RS("DT"), bias=B["NGAMJ"][:, h:h + 1])
            self.act(SH["MD"][:, h, :], PB[p2][:, h * 128:(h + 1) * 128], AF.Exp, r=[("pb", p2)] + R("NGAMJ"), w=RS("MD"), bias=B["NGAMJ"][:, h:h + 1])
        self.bfree(p1)
        self.bfree(p2)
        self.tt(B["QD"][:], QKV[:, 0:4, cs], SH["E"][:], ALU.mult, r=[("qkv", h) for h in range(4)] + RS("E"), w=R("QD"), eng="pool")
        pkk, pqk = self.balloc(), self.balloc()
        for h in range(4):
            kh = QKV[:, 4 + h, cs]
            qh = QKV[:, h, cs]
            self.mm(PB[pkk][:, h * 128:(h + 1) * 128], kh, kh, start=True, stop=True, r=[("qkv", 4 + h)], w=[("pb", pkk)])
            self.mm(PB[pqk][:, h * 128:(h + 1) * 128], kh, qh, start=True, stop=True, r=[("qkv", 4 + h), ("qkv", h)], w=[("pb", pqk)])
        self.stt(flat(B["MN"][:]), PB[pkk][:], -1.0, flat(SH["MD"][:]), ALU.mult, ALU.mult, r=[("pb", pkk)] + RS("MD"), w=R("MN"))
        self.tt(flat(B["QKT"][:]), PB[pqk][:], flat(SH["DT"][:]), ALU.mult, r=[("pb", pqk)] + RS("DT"), w=R("QKT"))
        self.bfree(pkk)
        self.bfree(pqk)
        yield
        ptr = self.balloc()
        ptv = PB[ptr][:].bitcast(BF16)
        for h in range(4):
            self.tr(ptv[:, h * 128:(h + 1) * 128], B["MN"][:, h, :], self.IDB[:], r=["consts"] + R("MN"), w=[("pb", ptr)])
        self.act(flat(B["AN"][:]), ptv[:, 0:512], AF.Copy, r=[("pb", ptr)], w=R("AN"))
        self.bfree(ptr)
        mT, mU = (0, 1) if d == 0 else (1, 0)
        self.cp(flat(B["T"][:]), flat(self.ID4[:]), r=["consts"], w=R("T"), eng="pool")
        self.cp(flat(B["U"][:]), flat(self.ID4[:]), r=["consts"], w=R("U"), eng="pool")
        self.cpred(flat(B["U"][:]), flat(self.LM4[:, 0, mU, :, :]), flat(B["MN"][:]), r=["consts"] + R("MN", "U"), w=R("U"))
        yield
        self.cpred(flat(B["T"][:]), flat(self.LM4[:, 0, mT, :, :]), flat(B["AN"][:]), r=["consts"] + R("AN", "T"), w=R("T"))
        for lev in range(1, 7):
            lastlev = (lev == 6)
            py = self.balloc()
            py2 = None if lastlev else self.balloc()
            for h in range(4):
                hsl = slice(h * 128, (h + 1) * 128)
                self.mm(PB[py][:, hsl], B["AN"][:, h, :], B["U"][:, h, :], start=True, stop=True, r=R("AN", "U"), w=[("pb", py)])
            if not lastlev:
                for h in range(4):
                    hsl = slice(h * 128, (h + 1) * 128)
                    self.mm(PB[py2][:, hsl], B["MN"][:, h, :], B["T"][:, h, :], start=True, stop=True, r=R("MN", "T"), w=[("pb", py2)])
            self.act(flat(B["YS"][:]), PB[py][:], AF.Copy, r=[("pb", py)], w=R("YS"))
            self.bfree(py)
            if not lastlev:
                self.act(flat(B["YS2"][:]), PB[py2][:], AF.Copy, r=[("pb", py2)], w=R("YS2"))
                self.bfree(py2)
            yield
            pz = self.balloc()
            pz2 = None if lastlev else self.balloc()
            for h in range(4):
                hsl = slice(h * 128, (h + 1) * 128)
                self.mm(PB[pz][:, hsl], B["T"][:, h, :], B["YS"][:, h, :], start=True, stop=True, r=R("T", "YS"), w=[("pb", pz)])
            if not lastlev:
                for h in range(4):
                    hsl = slice(h * 128, (h + 1) * 128)
                    self.mm(PB[pz2][:, hsl], B["U"][:, h, :], B["YS2"][:, h, :], start=True, stop=True, r=R("U", "YS2"), w=[("pb", pz2)])
            self.cpred(flat(B["U"][:]), flat(self.LM4[:, lev, mU, :, :]), PB[pz][:], r=["consts", ("pb", pz)] + R("U"), w=R("U"))
            self.bfree(pz)
            if not lastlev:
                self.cpred(flat(B["T"][:]), flat(self.LM4[:, lev, mT, :, :]), PB[pz2][:], r=["consts", ("pb", pz2)] + R("T"), w=R("T"))
                self.bfree(pz2)
            yield
        pk_, pv_ = self.balloc(), self.balloc()
        pkv, pvv = PB[pk_][:].bitcast(BF16), PB[pv_][:].bitcast(BF16)
        for h in range(4):
            self.tr(pkv[:, h * 128:(h + 1) * 128], QKV[:, 4 + h, cs], self.IDB[:], r=["consts", ("qkv", 4 + h)], w=[("pb", pk_)])
            self.tr(pvv[:, h * 128:(h + 1) * 128], QKV[:, 8 + h, cs], self.IDB[:], r=["consts", ("qkv", 8 + h)], w=[("pb", pv_)])
        pk3 = pkv[:, 0:512].rearrange("p (h x) -> p h x", h=4)
        pv3 = pvv[:, 0:512].rearrange("p (h x) -> p h x", h=4)
        self.tt(B["KBt"][:], pk3, bc_h(B["EKB"][:]), ALU.mult, r=[("pb", pk_)] + R("EKB"), w=R("KBt"))
        self.tt(B["KD"][:], pk3, bc_h(B["EKD"][:]), ALU.mult, r=[("pb", pk_)] + R("EKD"), w=R("KD"))
        self.tt(B["BV"][:], pv3, bc_h(B["BB"][:]), ALU.mult, r=[("pb", pv_)] + R("BB"), w=R("BV"))
        self.bfree(pk_)
        self.bfree(pv_)
        yield
        pu, pw = self.balloc(), self.balloc()
        for h in range(4):
            hsl = slice(h * 128, (h + 1) * 128)
            self.mm(PB[pu][:, hsl], B["U"][:, h, :], B["BV"][:, h, :], start=True, stop=True, r=R("U", "BV"), w=[("pb", pu)])
        for h in range(4):
            hsl = slice(h * 128, (h + 1) * 128)
            self.mm(PB[pw][:, hsl], B["KBt"][:, h, :], B["U"][:, h, :], start=True, stop=True, r=R("KBt", "U"), w=[("pb", pw)])
        self.act(flat(B["USB"][:]), PB[pu][:], AF.Copy, r=[("pb", pu)], w=R("USB"))
        self.cp(flat(B["WT"][:]), PB[pw][:], r=[("pb", pw)], w=R("WT"))
        self.bfree(pu)
        self.bfree(pw)
        yield
        if g == "s":
            first = (gc == 0) if d == 0 else (gc == ge["T"] // 128 - 1)
            if first:
                kb.dma("pool", self.S[:], self.din["st0"][l, d], w=["S"])
                self.act(flat(self.SBF[:]), flat(self.S[:]), AF.Copy, r=["S"], w=["SBF"])
        else:
            first = (gc % 2 == 0) if d == 0 else (gc % 2 == 1)
            if first:
                self.ms(self.S[:], 0.0, w=["S"])
                self.ms(self.SBF[:], 0.0, w=["SBF"])
        pa = self.balloc()
        for h in range(4):
            hsl = slice(h * 128, (h + 1) * 128)
            self.mm(PB[pa][:, hsl], B["WT"][:, h, :], self.SBF[:, h, :], start=True, stop=True, r=R("WT") + ["SBF"], w=[("pb", pa)])
        self.tt(flat(B["VN"][:]), flat(B["USB"][:]), PB[pa][:], ALU.subtract, r=[("pb", pa)] + R("USB"), w=R("VN"))
        self.bfree(pa)
        po, ps_ = self.balloc(), self.balloc()
        for h in range(4):
            hsl = slice(h * 128, (h + 1) * 128)
            self.mm(PB[ps_][:, hsl], B["KD"][:, h, :], B["VN"][:, h, :], start=True, stop=True, r=R("KD", "VN"), w=[("pb", ps_)])
        for h in range(4):
            hsl = slice(h * 128, (h + 1) * 128)
            self.mm(PB[po][:, hsl], self.SBF[:, h, :], B["QD"][:, h, :], start=True, stop=False, r=R("QD") + ["SBF"], w=[("pb", po)])
            self.mm(PB[po][:, hsl], B["VN"][:, h, :], B["QKT"][:, h, :], start=False, stop=True, r=R("VN", "QKT"), w=[("pb", po)])
        for h in range(4):
            hsl = slice(h * 128, (h + 1) * 128)
            self.stt(self.S[:, h, :], self.S[:, h, :], B["GTOT"][:, h:h + 1], PB[ps_][:, hsl], ALU.mult, ALU.add,
                     r=["S", ("pb", ps_)] + R("GTOT"), w=["S"])
        self.bfree(ps_)
        self.act(flat(self.SBF[:]), flat(self.S[:]), AF.Copy, r=["S"], w=["SBF"])
        POv = PB[po][:].rearrange("p (h i) -> p h i", h=4)
        if d == 0:
            self.act(OFT[:, :, cs], POv, AF.Copy, r=[("pb", po)], w=["OFT"])
        else:
            self.tt(OB[:, :, cs], POv, OFT[:, :, cs], ALU.add, r=[("pb", po), "OFT"], w=["OB"])
        self.bfree(po)
        if g == "p":
            lastc = (gc % 2 == 1) if d == 0 else (gc % 2 == 0)
            if lastc:
                kb.dma("pool", self.dout["so"][gc // 2, l, d], self.S[:], r=["S"], w=[("dram", "so", gc // 2, l, d)])

    def onorm(self, OB):
        l, din, PB = self.l, self.din, self.PB
        HL = self.ge["HL"]
        ZS = self.ZS
        self.mark("onorm")
        SQ = [self.aview("on_sq%d" % k, [128, 512], BF16, key=("on_sq", k)) for k in range(2)]
        RN = [self.aview("on_rn%d" % k, [128, 512], F32, key=("on_rn", k)) for k in range(2)]
        TMP = [self.aview("on_tmp%d" % k, [128, 512], F32, key=("on_tmp", k)) for k in range(2)]
        src = din["win"][l, 12:16].rearrange("m p k c -> p m (k c)")
        wv, wk = self.wload_3d(src, 4, 1024, wid=("win", l, 12))
        for m in range(4):
            b = self.proj(wv[:, m, :], wk, HL, NT)
            self.act(ZS[:, m, :], PB[b][:], AF.Silu, r=[("pb", b)], w=[("zs", m)])
            self.bfree(b)
        for h in range(4):
            k2 = h % 2
            self.tt(SQ[k2][:], OB[:, h, :], OB[:, h, :], ALU.mult, r=["OB"], w=[("on_sq", k2)])
            b = self.balloc()
            self.mm(PB[b][:], self.ONEB[:], SQ[k2][:], start=True, stop=True, r=["consts", ("on_sq", k2)], w=[("pb", b)])
            self.rsqrt_(RN[k2][:], PB[b][:], 1.0 / 128, r=[("pb", b)], w=[("on_rn", k2)])
            self.bfree(b)
            self.tt(TMP[k2][:], OB[:, h, :], RN[k2][:], ALU.mult, r=["OB", ("on_rn", k2)], w=[("on_tmp", k2)])
            self.stt(self.OD[:, h, :], TMP[k2][:], self.DNNG[:, 0:1], ZS[:, h, :], ALU.mult, ALU.mult,
                     r=[("on_tmp", k2), "dnng", ("zs", h)], w=[("od", h)])

    def mixer_rest(self, i, xmid, XW):
        ge, l, kb, din, PB = self.ge, self.l, self.kb, self.din, self.PB
        HL, seg, nsg, gi = ge["HL"], ge["seg"], ge["nsg"], ge["gi"]
        self.phase()
        av = self.aview
        SP31 = seg + 30
        self.mark("mix_cf")
        PC31 = [av("pc31_%d" % k, [128, 2, nsg, SP31], BF16, key=("pc31", k)) for k in range(2)]
        SG = [av("sg%d" % k, [128, 640], F32, key=("sg", k)) for k in range(3)]
        HC = av("HC", [128, 4, 512], F32, key=[("hc", m) for m in range(4)])
        HCB = [av("hcb%d" % k, [128, 512], BF16, key=("hcb", k)) for k in range(2)]
        SQB = [av("sqb%d" % k, [128, 512], BF16, key=("sqb", k)) for k in range(2)]
        MEAN = av("MEAN", [128, 512], F32, key="mean")
        RSTD = av("RSTD", [128, 512], F32, key="rstd")
        HCN = av("HCN", [128, 4, 512], BF16, key=[("hcn", m) for m in range(4)])
        SCV = av("SCV", [128, 4, 512], BF16, key=[("scv", m) for m in range(4)])
        M = av("M", [128, 8, 512], BF16, key=[("m", j) for j in range(8)])
        TM = [av("tm%d" % k, [128, 512], F32, key=("tm", k)) for k in range(3)]
        for k in range(2):
            self.ms(PC31[k][:], 0.0, w=[("pc31", k)])
        wa, wak = self.wload_3d(din["win"][l, 16:20].rearrange("m p k c -> p m (k c)"), 4, 1024, wid=("win", l, 16))
        wb, wbk = self.wload_3d(din["win"][l, 20:24].rearrange("m p k c -> p m (k c)"), 4, 1024, wid=("win", l, 20))
        bs1, bs2 = self.balloc(), self.balloc()
        def cf_item(m):
            for tap in range(min(31, self.NDG - 32)):
                self.diag(self.CFC[:, m, tap:tap + 1], "cfc", ck=(self.phase_id, ("cf", m), tap))
            ba = self.proj(wa[:, m, :], wak, HL, NT)
            bb = self.proj(wb[:, m, :], wbk, HL, NT)
            sg, sgk = SG[m % 2], ("sg", m % 2)
            self.act(sg[:, 0:512], PB[bb][:], AF.Sigmoid, r=[("pb", bb)], w=[sgk])
            self.bfree(bb)
            pc, pk = PC31[m % 2], ("pc31", m % 2)
            for g2 in range(2):
                cs2 = slice(g2 * 256, (g2 + 1) * 256)
                self.tt(pc[:, g2, :, 15:15 + seg], PB[ba][:, cs2].rearrange("p (s t) -> p s t", s=nsg),
                        sg[:, cs2].rearrange("p (s t) -> p s t", s=nsg), ALU.mult, r=[("pb", ba), sgk], w=[pk])
            self.bfree(ba)
            yield
            for g2 in range(2):
                cs2 = slice(g2 * 256, (g2 + 1) * 256)
                b2, valid = self.hconv(pc, pk, g2, 31, lambda tap, m=m: self.CFC[:, m, tap:tap + 1], "cfc", cid=("cf", m))
                self.act(HC[:, m, cs2].rearrange("p (s t) -> p s t", s=nsg), valid, AF.Copy, r=[("pb", b2)], w=[("hc", m)])
                self.bfree(b2)
            hb, hbk = HCB[m % 2], ("hcb", m % 2)
            sq, sqk = SQB[m % 2], ("sqb", m % 2)
            self.cp(hb[:], HC[:, m, :], r=[("hc", m)], w=[hbk], eng="pool")
            self.act(sq[:], HC[:, m, :], AF.Square, r=[("hc", m)], w=[sqk])
            yield
            self.mm(PB[bs1][:], self.ONEB[:], hb[:], start=(m == 0), stop=(m == 3), r=["consts", hbk], w=[("pb", bs1)])
            self.mm(PB[bs2][:], self.ONEB[:], sq[:], start=(m == 0), stop=(m == 3), r=["consts", sqk], w=[("pb", bs2)])

        self.swp([cf_item(m) for m in range(4)])
        self.act(MEAN[:], PB[bs1][:], AF.Copy, r=[("pb", bs1)], w=["mean"], scale=1.0 / 512)
        self.bfree(bs1)
        self.tt(RSTD[:], MEAN[:], MEAN[:], ALU.mult, r=["mean"], w=["rstd"])
        self.stt(RSTD[:], PB[bs2][:], 1.0 / 512, RSTD[:], ALU.mult, ALU.subtract, r=[("pb", bs2), "rstd"], w=["rstd"])
        self.bfree(bs2)
        self.ts(RSTD[:], RSTD[:], 0.0, ALU.max, r=["rstd"], w=["rstd"])
        self.act(RSTD[:], RSTD[:], AF.Ln, r=["rstd"], w=["rstd"], bias=EPS)
        self.act(RSTD[:], RSTD[:], AF.Exp, r=["rstd"], w=["rstd"], scale=-0.5)
        for m in range(4):
            tm, tk = TM[m % 2], ("tm", m % 2)
            self.tt(tm[:], HC[:, m, :], MEAN[:], ALU.subtract, r=[("hc", m), "mean"], w=[tk])
            self.tt(tm[:], tm[:], RSTD[:], ALU.mult, r=[tk, "rstd"], w=[tk])
            self.act(HCN[:, m, :], tm[:], AF.Silu, r=[tk, "cfln"], w=[("hcn", m)], scale=self.CFLN[:, m, 0:1], bias=self.CFLN[:, m, 1:2])
        wbg, wbgk = self.wload_3d(din["win"][l, 24:28].rearrange("m p k c -> p m (k c)"), 4, 1024, wid=("win", l, 24))
        self.mark("mix_sc")
        wcg, wcgk = self.wload_3d(din["win"][l, 28:32].rearrange("m p k c -> p m (k c)"), 4, 1024, wid=("win", l, 28))
        wxh, wxhk = self.wload_3d(din["win"][l, 32:36].rearrange("m p k c -> p m (k c)"), 4, 1024, wid=("win", l, 32))
        if HL > 0:
            SCP = [av("scp%d" % k, [128, 640], BF16, key=("scp", k)) for k in range(2)]
        else:
            SCP = [av("scp%d" % k, [128, 2, 1, seg + 2], BF16, key=("scp", k)) for k in range(2)]
            for k in range(2):
                self.ms(SCP[k][:], 0.0, w=[("scp", k)])
        ranges = [(0, 512)] + ([(512, XW)] if XW > 512 else [])
        def sc_item(m):
            scp, sk = SCP[m % 2], ("scp", m % 2)
            cgs, cgk = SG[2], ("sg", 2)
            for (c0, c1) in ranges:
                bc = self.proj(wcg[:, m, :], wcgk, c0, c1 - c0)
                self.act(cgs[:, c0:c1], PB[bc][:, 0:c1 - c0], AF.Copy, r=[("pb", bc)], w=[cgk])
                self.bfree(bc)
                bx = self.proj(wxh[:, m, :], wxhk, c0, c1 - c0)
                if HL > 0:
                    self.tt(scp[:, c0:c1], PB[bx][:, 0:c1 - c0], cgs[:, c0:c1], ALU.mult, r=[("pb", bx), cgk], w=[sk])
                else:
                    for g2 in range(2):
                        cs2 = slice(g2 * 256, (g2 + 1) * 256)
                        self.tt(scp[:, g2, 0, 1:1 + seg], PB[bx][:, cs2], cgs[:, cs2], ALU.mult, r=[("pb", bx), cgk], w=[sk])
                self.bfree(bx)
            bbg = self.proj(wbg[:, m, :], wbgk, HL, NT)
            sg, sgk = SG[m % 2], ("sg", m % 2)
            self.act(sg[:, 0:512], PB[bbg][:], AF.Copy, r=[("pb", bbg)], w=[sgk])
            self.bfree(bbg)
            yield
            bcv = self.balloc()
            if HL > 0:
                for tap in range(3):
                    dg, dk = self.diag(self.SCC[:, m, tap:tap + 1], "scc")
                    self.mm(PB[bcv][:], dg, scp[:, tap * 64:tap * 64 + 512], start=(tap == 0), stop=(tap == 2), r=[dk, sk], w=[("pb", bcv)])
            else:
                for g2 in range(2):
                    for tap in range(3):
                        dg, dk = self.diag(self.SCC[:, m, tap:tap + 1], "scc", ck=(self.phase_id, "sc", m, tap))
                        self.mm(PB[bcv][:, g2 * 256:(g2 + 1) * 256], dg, scp[:, g2, 0, tap:tap + 256], start=(tap == 0), stop=(tap == 2),
                                r=[dk, sk], w=[("pb", bcv)])
            self.tt(SCV[:, m, :], PB[bcv][:], sg[:, 0:512], ALU.mult, r=[("pb", bcv), sgk], w=[("scv", m)])
            self.bfree(bcv)

        self.swp([sc_item(m) for m in range(4)])
        self.mark("mix_merge")
        for j in range(8):
            wg_, wgk = self.wload_3d(din["win"][l, 36 + 3 * j:39 + 3 * j].rearrange("m p k c -> p m (k c)"), 3, 1024, wid=("win", l, 36 + 3 * j))
            wy, wyk = self.wload_3d(din["wbr"][l, j].rearrange("p b k c -> p b (k c)"), 3, 512, wid=("wbr", l, j))
            srcs = [(self.OD, [("od", h) for h in range(4)]), (HCN, [("hcn", h) for h in range(4)]), (SCV, [("scv", h) for h in range(4)])]
            for br in range(3):
                bgt = self.proj(wg_[:, br, :], wgk, HL, NT)
                sg, sgk = SG[br], ("sg", br)
                self.act(sg[:, 0:512], PB[bgt][:], AF.Sigmoid, r=[("pb", bgt)], w=[sgk])
                self.bfree(bgt)
                buf, keys = srcs[br]
                by = self.balloc()
                for kt in range(4):
                    self.mm(PB[by][:], wy[:, br, kt * 128:(kt + 1) * 128], buf[:, kt, :], start=(kt == 0), stop=(kt == 3),
                            r=[wyk, keys[kt]], w=[("pb", by)])
                tm, tk = TM[br], ("tm", br)
                self.tt(tm[:], PB[by][:], sg[:, 0:512], ALU.mult, r=[("pb", by), sgk], w=[tk])
                self.bfree(by)
            self.tt(TM[0][:], TM[0][:], TM[1][:], ALU.add, r=[("tm", 0), ("tm", 1)], w=[("tm", 0)])
            self.tt(M[:, j, :], TM[0][:], TM[2][:], ALU.add, r=[("tm", 0), ("tm", 2)], w=[("m", j)])
        self.mark("mix_wo")
        self.out_proj_res(lambda j: self.wload_3d(din["wo"][l, j:j + 1].rearrange("m p k c -> p m (k c)"), 1, 1024, wid=("wo", l, j)),
                          8, lambda kt: M[:, kt, :], lambda kt: ("m", kt), 2, SQB, HL)
        t0 = i * NT
        dst = xmid.rearrange("(f p) t -> p f t", p=128)[:, :, t0:t0 + NT]
        kb.dma("pool", dst, self.X[:, :, HL:HL + NT], r=["X"], w=[("dram", id(xmid), i)])

    def out_proj_res(self, wfn, nk, rhs, rkeyfn, ic, SQB, HL):
        PB = self.PB
        gi = self.ge["gi"]
        self.MO = self.aview("MO", [128, 8, 512], F32, key=[("mo", j) for j in range(8)])
        RINV = self.aview("opr_rinv", [128, 512], F32, key="opr_rinv")
        TMP = [self.aview("opr_tmp%d" % k, [128, 512], F32, key=("opr_tmp", k)) for k in range(2)]
        bss = self.balloc()

        def item(j):
            wv, wk = wfn(j)
            b = self.balloc()
            for kt in range(nk):
                self.mm(PB[b][:], wv[:, 0, kt * 128:(kt + 1) * 128], rhs(kt), start=(kt == 0), stop=(kt == nk - 1),
                        r=[wk, rkeyfn(kt)], w=[("pb", b)])
            self.act(self.MO[:, j, :], PB[b][:], AF.Copy, r=[("pb", b)], w=[("mo", j)])
            sq, sqk = SQB[j % 2], ("sqb", j % 2)
            self.act(sq[:], PB[b][:], AF.Square, r=[("pb", b)], w=[sqk])
            self.bfree(b)
            yield
            self.mm(PB[bss][:], self.ONEB[:], sq[:], start=(j == 0), stop=(j == 7), r=["consts", sqk], w=[("pb", bss)])

        self.swp([item(j) for j in range(8)])
        self.rsqrt_(RINV[:], PB[bss][:], 1.0 / D, r=[("pb", bss)], w=["opr_rinv"])
        self.bfree(bss)
        for j in range(8):
            tmp, tk = TMP[j % 2], ("opr_tmp", j % 2)
            self.tt(tmp[:], self.MO[:, j, :], RINV[:], ALU.mult, r=[("mo", j), "opr_rinv"], w=[tk])
            xc = self.X[:, j, HL:HL + NT]
            self.stt(xc, tmp[:], self.MODC[:, gi, ic, j:j + 1], xc, ALU.mult, ALU.add, r=[tk, "modc", "X"], w=["X"])

    def loop_ffn(self, i, xmid, xout):
        ge, l, kb, din, PB = self.ge, self.l, self.kb, self.din, self.PB
        HL, seg, nsg, gi = ge["HL"], ge["seg"], ge["nsg"], ge["gi"]
        self.phase()
        XW, vlo, vhi = self.load_x(i, xmid, halo=True)
        self.mark("ffn_normmod")
        self.normmod(XW, vlo, vhi, 3, 4)
        av = self.aview
        HID = av("HID", [128, 22, 512], BF16, key=[("hid", c) for c in range(22)])
        self.mark("ffn_up")
        SA = [av("sa%d" % k, [128, 512], F32, key=("sa", k)) for k in range(2)]
        SQB = [av("fsqb%d" % k, [128, 512], BF16, key=("sqb", k)) for k in range(2)]
        if HL > 0:
            UB = [av("ub%d" % k, [128, 640], BF16, key=("ub", k)) for k in range(4)]
        else:
            UB = [av("ub%d" % k, [128, 2, 1, seg + 2], BF16, key=("ub", k)) for k in range(4)]
            for k in range(4):
                self.ms(UB[k][:], 0.0, w=[("ub", k)])
        ranges = [(0, 512)] + ([(512, XW)] if XW > 512 else [])
        wst = {}

        def up_item(c):
            if c % 2 == 0:
                wst["w"] = self.wload_3d(din["wup"][l, c:c + 2].rearrange("m p a k c -> p m (a k c)"), 2, 2048, wid=("wup", l, c))
            wv, wk = wst["w"]
            ubs = []
            for ab in range(2):
                ub, uk = UB[(c % 2) * 2 + ab], ("ub", (c % 2) * 2 + ab)
                ubs.append((ub, uk))
                wsl = wv[:, c % 2, ab * 1024:(ab + 1) * 1024]
                for (c0, c1) in ranges:
                    b = self.proj(wsl, wk, c0, c1 - c0)
                    if HL > 0:
                        self.act(ub[:, c0:c1], PB[b][:, 0:c1 - c0], AF.Copy, r=[("pb", b)], w=[uk])
                    else:
                        for g2 in range(2):
                            self.act(ub[:, g2, 0, 1:1 + seg], PB[b][:, g2 * 256:(g2 + 1) * 256], AF.Copy, r=[("pb", b)], w=[uk])
                    self.bfree(b)
            yield
            cb = []
            for ab in range(2):
                ub, uk = ubs[ab]
                ct = ab * 22 + c
                bcv = self.balloc()
                if HL > 0:
                    for tap in range(3):
                        dg, dk = self.diag(self.FFC[:, ct, tap:tap + 1], "ffc")
                        self.mm(PB[bcv][:], dg, ub[:, tap * 64:tap * 64 + 512], start=(tap == 0), stop=(tap == 2), r=[dk, uk], w=[("pb", bcv)])
                else:
                    for g2 in range(2):
                        for tap in range(3):
                            dg, dk = self.diag(self.FFC[:, ct, tap:tap + 1], "ffc", ck=(self.phase_id, "ffn", ct, tap))
                            self.mm(PB[bcv][:, g2 * 256:(g2 + 1) * 256], dg, ub[:, g2, 0, tap:tap + 256], start=(tap == 0), stop=(tap == 2),
                                    r=[dk, uk], w=[("pb", bcv)])
                cb.append(bcv)
            sa, sak = SA[c % 2], ("sa", c % 2)
            self.act(sa[:], PB[cb[0]][:], AF.Silu, r=[("pb", cb[0])], w=[sak])
            self.bfree(cb[0])
            self.tt(HID[:, c, :], PB[cb[1]][:], sa[:], ALU.mult, r=[("pb", cb[1]), sak], w=[("hid", c)])
            self.bfree(cb[1])

        self.swp([up_item(c) for c in range(22)])
        self.mark("ffn_down")
        self.out_proj_res(lambda j: self.wload_3d(din["wdown"][l, j:j + 1].rearrange("m p k c -> p m (k c)"), 1, 2816, wid=("wdown", l, j)),
                          22, lambda kt: HID[:, kt, :], lambda kt: ("hid", kt), 5, SQB, HL)
        t0 = i * NT
        dst = xout.rearrange("(f p) t -> p f t", p=128)[:, :, t0:t0 + NT]
        kb.dma("pool", dst, self.X[:, :, HL:HL + NT], r=["X"], w=[("dram", id(xout), i)])


_CACHE = {}


def get_program(cfg=None):
    key = repr(sorted((cfg or {}).items()))
    if key not in _CACHE:
        _CACHE[key] = Prog(cfg).build()
    return _CACHE[key]


def make_in_maps(inputs):
    sh = _shared(inputs)
    xs = np.asarray(inputs["x_sample"], np.float32)
    xp = np.asarray(inputs["x_prompt"], np.float32)
    st = np.asarray(inputs["state_dn"], np.float32)
    c = np.asarray(inputs["c"], np.float32)
    cc = np.asarray(inputs["c_ctx"], np.float32)
    maps = []
    for core in range(8):
        b = core % 4
        m = dict(sh)
        m["xs"] = np.ascontiguousarray(xs[b].T)
        m["xp"] = np.ascontiguousarray(xp[4 * core:4 * core + 4].reshape(TP, D).T)
        m["st0"] = np.ascontiguousarray(st[b].transpose(0, 1, 3, 2, 4))
        m["cvec"] = np.ascontiguousarray(np.stack([_fm(c[b], 8), _fm(cc, 8)], axis=2))
        maps.append(m)
    return maps


def kernel(**inputs):
    nc = get_program()
    maps = make_in_maps(inputs)
    res = run_bass_kernel_spmd(nc, maps, core_ids=list(range(8)))
    R = res.results
    y_sample = np.stack([np.ascontiguousarray(R[b]["ys"].T) for b in range(4)]).astype(np.float32)
    y_prompt = np.concatenate([np.ascontiguousarray(R[c]["yp"].T).reshape(4, 256, D) for c in range(8)]).astype(np.float32)
    so = np.concatenate([R[c]["so"] for c in range(8)])
    new_state = np.ascontiguousarray(so.transpose(0, 1, 2, 4, 3, 5)).astype(np.float32)
    return (y_prompt, y_sample, new_state)
```
